# Optimizing a Trainium2 kernel written in Bass

```python
import math
import jax
import jax.numpy as jnp
from jax import lax
import numpy as np

D_MODEL = 1024
BATCH = 2
SEQ = 16384
DEPTH = 4

GRID_W = 64
CTX_LEN = 256
D_FF = 2816
N_MOD = 9
NORM_EPS = 1e-6
NA_HEADS = 8
NA_HEAD_DIM = 32
NA_ROWS = 8
NA_COLS = 16
RET_HEADS = 4
RET_QK_DIM = 64
RET_V_DIM = 128
RET_CHUNK = 128
ROPE_BASE = 10000.0
S5_GROUPS = 16
S5_GROUP_CH = 16
S5_STATE = 64
NA_WIDTH = NA_HEADS * NA_HEAD_DIM
RET_QK_WIDTH = RET_HEADS * RET_QK_DIM
RET_V_WIDTH = RET_HEADS * RET_V_DIM
S5_WIDTH = S5_GROUPS * S5_GROUP_CH
MIX_WIDTH = NA_WIDTH + RET_V_WIDTH + S5_WIDTH
IN_SPLITS = (NA_WIDTH, NA_WIDTH, NA_WIDTH, RET_QK_WIDTH, RET_QK_WIDTH, RET_V_WIDTH, RET_V_WIDTH, S5_WIDTH)
IN_WIDTH = sum(IN_SPLITS)

kernel_name = "hybrid_natten_retnet_s5_macaron_block"


def rms_norm(x):
    xf = x.astype(jnp.float32)
    return (xf * lax.rsqrt(jnp.mean(xf * xf, axis=-1, keepdims=True) + NORM_EPS)).astype(x.dtype)


def modulate(h, shift, scale):
    return h * (1 + scale) + shift


def swiglu(h, w_in, w_out):
    a, b = jnp.split(h @ w_in, 2, axis=-1)
    return (jax.nn.silu(a) * b) @ w_out


def to_heads(z, n_heads):
    b, t, _ = z.shape
    return z.reshape(b, t, n_heads, -1).transpose(0, 2, 1, 3)


def from_heads(z):
    b, h, t, d = z.shape
    return z.transpose(0, 2, 1, 3).reshape(b, t, h * d)


def axial_rope(z):
    n, d = z.shape[-2], z.shape[-1]
    nf = d // 4
    inv = ROPE_BASE ** (-jnp.arange(nf, dtype=jnp.float32) / nf)
    t = jnp.arange(n)
    row = (t // GRID_W).astype(jnp.float32)
    col = (t % GRID_W).astype(jnp.float32)
    ang = jnp.concatenate([row[:, None] * inv, col[:, None] * inv], axis=-1)
    cos = jnp.cos(ang).astype(z.dtype)
    sin = jnp.sin(ang).astype(z.dtype)
    z1, z2 = jnp.split(z, 2, axis=-1)
    return jnp.concatenate([z1 * cos - z2 * sin, z1 * sin + z2 * cos], axis=-1)


def neighborhood_attention(q, k, v, kc, vc, rpb, rows):
    b, h, n, dh = q.shape
    kr = min(NA_ROWS, rows)
    qg = q.reshape(b, h, rows, GRID_W, dh)
    kg = k.reshape(b, h, rows, GRID_W, dh)
    vg = v.reshape(b, h, rows, GRID_W, dh)
    col_start = np.clip(np.arange(GRID_W) - NA_COLS // 2, 0, GRID_W - NA_COLS)
    col_idx = col_start[:, None] + np.arange(NA_COLS)[None, :]
    col_off = col_idx - np.arange(GRID_W)[:, None] + NA_COLS - 1
    scale = dh ** -0.5
    n_loc = kr * NA_COLS

    def one_row(r):
        rs = jnp.clip(r - kr // 2, 0, rows - kr)
        k_win = lax.dynamic_slice_in_dim(kg, rs, kr, axis=2)[:, :, :, col_idx]
        v_win = lax.dynamic_slice_in_dim(vg, rs, kr, axis=2)[:, :, :, col_idx]
        q_row = lax.dynamic_index_in_dim(qg, r, axis=2, keepdims=False)
        row_off = rs + jnp.arange(kr) - r + NA_ROWS - 1
        bias = jnp.transpose(rpb[:, row_off][:, :, col_off], (0, 2, 1, 3))
        s_loc = jnp.einsum('bhwd,bhiwjd->bhwij', q_row, k_win).astype(jnp.float32) * scale
        s_loc = s_loc + bias[None].astype(jnp.float32)
        s_ctx = jnp.einsum('bhwd,bhld->bhwl', q_row, kc).astype(jnp.float32) * scale
        s = jnp.concatenate([s_loc.reshape(b, h, GRID_W, n_loc), s_ctx], axis=-1)
        p = jax.nn.softmax(s, axis=-1).astype(v.dtype)
        p_loc = p[..., :n_loc].reshape(b, h, GRID_W, kr, NA_COLS)
        p_ctx = p[..., n_loc:]
        return (jnp.einsum('bhwij,bhiwjd->bhwd', p_loc, v_win)
                + jnp.einsum('bhwl,bhld->bhwd', p_ctx, vc))

    out = lax.map(one_row, jnp.arange(rows))
    return jnp.transpose(out, (1, 0, 3, 2, 4)).reshape(b, n, h * dh)


def context_attention(qc, kc, vc):
    scale = qc.shape[-1] ** -0.5
    s = jnp.einsum('bhqd,bhkd->bhqk', qc, kc).astype(jnp.float32) * scale
    p = jax.nn.softmax(s, axis=-1).astype(vc.dtype)
    return from_heads(jnp.einsum('bhqk,bhkd->bhqd', p, vc))


def retention_chunkwise(q, k, v, log_g, s0, inclusive):
    f32 = jnp.float32
    b, h, t, dk = q.shape
    dv = v.shape[-1]
    nc = t // RET_CHUNK
    qc = q.astype(f32).reshape(b, h, nc, RET_CHUNK, dk)
    kc = k.astype(f32).reshape(b, h, nc, RET_CHUNK, dk)
    vc = v.astype(f32).reshape(b, h, nc, RET_CHUNK, dv)
    pos = jnp.arange(RET_CHUNK, dtype=f32)
    diff = pos[:, None] - pos[None, :]
    mask = (diff >= 0) if inclusive else (diff > 0)
    lg = log_g[:, None, None]
    dmat = jnp.where(mask, jnp.exp(lg * jnp.where(mask, diff, 0.0)), 0.0)
    scores = jnp.einsum('bhnid,bhnjd->bhnij', qc, kc) * dmat[None, :, None]
    o_intra = jnp.einsum('bhnij,bhnjv->bhniv', scores, vc)
    w_end = jnp.exp(log_g[:, None] * (RET_CHUNK - 1 - pos))
    kv = jnp.einsum('bhnjd,hj,bhnjv->bhndv', kc, w_end, vc)
    g_chunk = jnp.exp(log_g * RET_CHUNK)[:, None, None]
    if s0 is None:
        s0 = jnp.zeros((b, h, dk, dv), f32)

    def step(s, kv_n):
        return g_chunk * s + kv_n, s

    s_last, s_prev = lax.scan(step, s0.astype(f32), jnp.moveaxis(kv, 2, 0))
    w_in = jnp.exp(log_g[:, None] * (pos + 1.0))
    o_cross = jnp.einsum('bhnid,nbhdv->bhniv', qc, s_prev) * w_in[None, :, None, :, None]
    return (o_intra + o_cross).reshape(b, h, t, dv), s_last


def retention_final_state(k, v, log_g):
    t = k.shape[2]
    w = jnp.exp(log_g[:, None] * (t - 1 - jnp.arange(t, dtype=jnp.float32)))
    return jnp.einsum('bhtd,ht,bhtv->bhdv', k.astype(jnp.float32), w, v.astype(jnp.float32))


def retention_output(o, gate, gn_gain):
    mu = jnp.mean(o, axis=-1, keepdims=True)
    var = jnp.mean(jnp.square(o - mu), axis=-1, keepdims=True)
    o = from_heads((o - mu) * lax.rsqrt(var + NORM_EPS)) * gn_gain.astype(jnp.float32)
    return (jax.nn.silu(gate.astype(jnp.float32)) * o).astype(gate.dtype)


def retention_mixer(q, k, v, gate, qc, kc, vc, gate_c, decay, gn_gain, need_ctx):
    log_g = jax.nn.log_sigmoid(decay.astype(jnp.float32))
    flip = lambda z: jnp.flip(z, axis=2)
    y_ctx = None
    if need_ctx:
        oc_f, s_f = retention_chunkwise(qc, kc, vc, log_g[0], None, True)
        oc_b, s_b = retention_chunkwise(flip(qc), flip(kc), flip(vc), log_g[1], None, False)
        y_ctx = retention_output(oc_f + flip(oc_b), gate_c, gn_gain)
    else:
        s_f = retention_final_state(kc, vc, log_g[0])
        s_b = retention_final_state(flip(kc), flip(vc), log_g[1])
    o_f, _ = retention_chunkwise(q, k, v, log_g[0], s_f, True)
    o_b, _ = retention_chunkwise(flip(q), flip(k), flip(v), log_g[1], s_b, False)
    y = retention_output(o_f + flip(o_b), gate, gn_gain)
    return y, y_ctx


def s5_discretize(a_re, a_im, log_dt, b_re, b_im):
    f32 = jnp.float32
    a_re = jnp.minimum(a_re.astype(f32), -1e-4)
    a_im = a_im.astype(f32)
    dt = jnp.exp(log_dt.astype(f32))[..., None]
    mag = jnp.exp(dt * a_re)
    ab_re = mag * jnp.cos(dt * a_im)
    ab_im = mag * jnp.sin(dt * a_im)
    den = a_re * a_re + a_im * a_im
    nr = ab_re - 1.0
    f_re = ((nr * a_re + ab_im * a_im) / den)[..., None]
    f_im = ((ab_im * a_re - nr * a_im) / den)[..., None]
    br = b_re.astype(f32)[None]
    bi = b_im.astype(f32)[None]
    bb_re = f_re * br - f_im * bi
    bb_im = f_re * bi + f_im * br
    return ab_re, ab_im, bb_re, bb_im


def _ssm_combine(e1, e2):
    ar1, ai1, br1, bi1 = e1
    ar2, ai2, br2, bi2 = e2
    return (ar1 * ar2 - ai1 * ai2,
            ar1 * ai2 + ai1 * ar2,
            ar2 * br1 - ai2 * bi1 + br2,
            ar2 * bi1 + ai2 * br1 + bi2)


def s5_scan(u, ab_re, ab_im, bb_re, bb_im, s0_re, s0_im):
    bu_re = jnp.einsum('tbgc,gpc->tbgp', u, bb_re)
    bu_im = jnp.einsum('tbgc,gpc->tbgp', u, bb_im)
    if s0_re is not None:
        bu_re = bu_re.at[0].add(ab_re * s0_re - ab_im * s0_im)
        bu_im = bu_im.at[0].add(ab_re * s0_im + ab_im * s0_re)
    shape = (u.shape[0], 1) + ab_re.shape
    a_re = jnp.broadcast_to(ab_re, shape)
    a_im = jnp.broadcast_to(ab_im, shape)
    _, _, x_re, x_im = lax.associative_scan(_ssm_combine, (a_re, a_im, bu_re, bu_im), axis=0)
    return x_re, x_im


def s5_readout(x_re, x_im, c_re, c_im):
    return (jnp.einsum('tbgp,gcp->tbgc', x_re, c_re.astype(jnp.float32))
            - jnp.einsum('tbgp,gcp->tbgc', x_im, c_im.astype(jnp.float32)))


def s5_glu(y, w_glu):
    a, g = jnp.split(jax.nn.gelu(y).astype(w_glu.dtype) @ w_glu, 2, axis=-1)
    return a * jax.nn.sigmoid(g)


def s5_mixer(u, uc, a_re, a_im, log_dt, b_re, b_im, c_re, c_im, d, w_glu, need_ctx):
    b, n, _ = u.shape
    l = uc.shape[1]
    ut = u.astype(jnp.float32).reshape(b, n, S5_GROUPS, S5_GROUP_CH).transpose(1, 0, 2, 3)
    uct = uc.astype(jnp.float32).reshape(b, l, S5_GROUPS, S5_GROUP_CH).transpose(1, 0, 2, 3)
    ab_re, ab_im, bb_re, bb_im = s5_discretize(a_re, a_im, log_dt, b_re, b_im)
    d_gc = d.astype(jnp.float32).reshape(S5_GROUPS, S5_GROUP_CH)
    y = ut * d_gc
    y_c = uct * d_gc if need_ctx else None
    for dn in range(2):
        fl = (lambda z: jnp.flip(z, axis=0)) if dn == 1 else (lambda z: z)
        xc_re, xc_im = s5_scan(fl(uct), ab_re[dn], ab_im[dn], bb_re[dn], bb_im[dn], None, None)
        x_re, x_im = s5_scan(fl(ut), ab_re[dn], ab_im[dn], bb_re[dn], bb_im[dn], xc_re[-1], xc_im[-1])
        y = y + fl(s5_readout(x_re, x_im, c_re[dn], c_im[dn]))
        if need_ctx:
            y_c = y_c + fl(s5_readout(xc_re, xc_im, c_re[dn], c_im[dn]))
    out = s5_glu(y.transpose(1, 0, 2, 3).reshape(b, n, S5_WIDTH), w_glu).astype(u.dtype)
    out_c = None
    if need_ctx:
        out_c = s5_glu(y_c.transpose(1, 0, 2, 3).reshape(b, l, S5_WIDTH), w_glu).astype(u.dtype)
    return out, out_c


def token_mixing(h, hc, w_in, na_q_gain, na_k_gain, na_rpb, ret_decay, ret_gn,
                 s5_a_re, s5_a_im, s5_log_dt, s5_b_re, s5_b_im, s5_c_re, s5_c_im, s5_d, s5_w_glu,
                 need_ctx):
    rows = h.shape[1] // GRID_W
    cuts = [int(v) for v in np.cumsum(IN_SPLITS)[:-1]]
    qa, ka, va, qb, kb, vb, gb, ub = jnp.split(h @ w_in, cuts, axis=-1)
    qa_c, ka_c, va_c, qb_c, kb_c, vb_c, gb_c, ub_c = jnp.split(hc @ w_in, cuts, axis=-1)

    qk_norm = lambda z, g: rms_norm(to_heads(z, NA_HEADS)) * g
    k_ac = qk_norm(ka_c, na_k_gain)
    v_ac = to_heads(va_c, NA_HEADS)
    y_a = neighborhood_attention(qk_norm(qa, na_q_gain), qk_norm(ka, na_k_gain),
                                 to_heads(va, NA_HEADS), k_ac, v_ac, na_rpb, rows)

    k_scale = RET_QK_DIM ** -0.5
    q_b = axial_rope(to_heads(qb, RET_HEADS))
    k_b = axial_rope(to_heads(kb, RET_HEADS)) * k_scale
    y_b, y_b_c = retention_mixer(q_b, k_b, to_heads(vb, RET_HEADS), gb,
                                 to_heads(qb_c, RET_HEADS), to_heads(kb_c, RET_HEADS) * k_scale,
                                 to_heads(vb_c, RET_HEADS), gb_c, ret_decay, ret_gn, need_ctx)

    y_c, y_c_c = s5_mixer(ub, ub_c, s5_a_re, s5_a_im, s5_log_dt, s5_b_re, s5_b_im,
                          s5_c_re, s5_c_im, s5_d, s5_w_glu, need_ctx)

    y = jnp.concatenate([y_a.astype(h.dtype), y_b.astype(h.dtype), y_c.astype(h.dtype)], axis=-1)
    y_ctx = None
    if need_ctx:
        y_a_c = context_attention(qk_norm(qa_c, na_q_gain), k_ac, v_ac)
        y_ctx = jnp.concatenate([y_a_c.astype(h.dtype), y_b_c.astype(h.dtype), y_c_c.astype(h.dtype)], axis=-1)
    return y, y_ctx


def setup_inputs(seed: int = 0) -> dict:
    key = jax.random.key(seed)
    ks = iter(jax.random.split(key, 32))
    f32 = jnp.float32
    D = D_MODEL

    def nrm(shape, s):
        return jax.random.normal(next(ks), shape, f32) * s

    x = nrm((BATCH, SEQ, D), 1.0)
    c = nrm((BATCH, D), 1.0)
    ctx = nrm((BATCH, CTX_LEN, D), 1.0)
    c_ctx = nrm((D,), 1.0)
    w_mod = nrm((DEPTH, D, N_MOD * D), 0.5 * D ** -0.5)
    b_mod = nrm((DEPTH, N_MOD * D), 0.02)
    ffn1_w_in = nrm((DEPTH, D, 2 * D_FF), D ** -0.5)
    ffn1_w_out = nrm((DEPTH, D_FF, D), D_FF ** -0.5)
    w_in = nrm((DEPTH, D, IN_WIDTH), D ** -0.5)
    w_out = nrm((DEPTH, MIX_WIDTH, D), MIX_WIDTH ** -0.5)
    na_q_gain = 1.0 + nrm((DEPTH, NA_HEAD_DIM), 0.05)
    na_k_gain = 1.0 + nrm((DEPTH, NA_HEAD_DIM), 0.05)
    na_rpb = nrm((DEPTH, NA_HEADS, 2 * NA_ROWS - 1, 2 * NA_COLS - 1), 0.2)
    ret_base = jnp.asarray(np.log(2.0 ** (5 + np.arange(RET_HEADS)) - 1.0), f32)
    ret_decay = ret_base + nrm((DEPTH, 2, RET_HEADS), 0.05)
    ret_gn = 1.0 + nrm((DEPTH, RET_V_WIDTH), 0.05)
    s5_a_re = -0.5 + nrm((DEPTH, 2, S5_GROUPS, S5_STATE), 0.01)
    s5_a_im = jnp.pi * jnp.arange(S5_STATE, dtype=f32) + nrm((DEPTH, 2, S5_GROUPS, S5_STATE), 0.01)
    s5_log_dt = jax.random.uniform(next(ks), (DEPTH, 2, S5_GROUPS), f32,
                                   minval=math.log(1e-3), maxval=math.log(1e-1))
    s5_b_re = nrm((DEPTH, S5_GROUPS, S5_STATE, S5_GROUP_CH), S5_GROUP_CH ** -0.5)
    s5_b_im = nrm((DEPTH, S5_GROUPS, S5_STATE, S5_GROUP_CH), S5_GROUP_CH ** -0.5)
    s5_c_re = nrm((DEPTH, 2, S5_GROUPS, S5_GROUP_CH, S5_STATE), S5_STATE ** -0.5)
    s5_c_im = nrm((DEPTH, 2, S5_GROUPS, S5_GROUP_CH, S5_STATE), S5_STATE ** -0.5)
    s5_d = nrm((DEPTH, S5_WIDTH), 0.5)
    s5_w_glu = nrm((DEPTH, S5_WIDTH, 2 * S5_WIDTH), S5_WIDTH ** -0.5)
    ffn2_w_in = nrm((DEPTH, D, 2 * D_FF), D ** -0.5)
    ffn2_w_out = nrm((DEPTH, D_FF, D), D_FF ** -0.5)
    return {"x": x, "c": c, "ctx": ctx, "c_ctx": c_ctx, "w_mod": w_mod, "b_mod": b_mod,
            "ffn1_w_in": ffn1_w_in, "ffn1_w_out": ffn1_w_out, "w_in": w_in, "w_out": w_out,
            "na_q_gain": na_q_gain, "na_k_gain": na_k_gain, "na_rpb": na_rpb,
            "ret_decay": ret_decay, "ret_gn": ret_gn,
            "s5_a_re": s5_a_re, "s5_a_im": s5_a_im, "s5_log_dt": s5_log_dt,
            "s5_b_re": s5_b_re, "s5_b_im": s5_b_im, "s5_c_re": s5_c_re, "s5_c_im": s5_c_im,
            "s5_d": s5_d, "s5_w_glu": s5_w_glu, "ffn2_w_in": ffn2_w_in, "ffn2_w_out": ffn2_w_out}


def reference(x, c, ctx, c_ctx, w_mod, b_mod, ffn1_w_in, ffn1_w_out, w_in, w_out,
              na_q_gain, na_k_gain, na_rpb, ret_decay, ret_gn,
              s5_a_re, s5_a_im, s5_log_dt, s5_b_re, s5_b_im, s5_c_re, s5_c_im, s5_d, s5_w_glu,
              ffn2_w_in, ffn2_w_out):
    xc = ctx
    sc = jax.nn.silu(c)[:, None, :]
    scc = jax.nn.silu(c_ctx)[None, None, :]
    for l in range(DEPTH):
        need_ctx = l < DEPTH - 1
        m = jnp.split(sc @ w_mod[l] + b_mod[l], N_MOD, axis=-1)
        mc = jnp.split(scc @ w_mod[l] + b_mod[l], N_MOD, axis=-1)
        x = x + 0.5 * m[2] * swiglu(modulate(rms_norm(x), m[0], m[1]), ffn1_w_in[l], ffn1_w_out[l])
        xc = xc + 0.5 * mc[2] * swiglu(modulate(rms_norm(xc), mc[0], mc[1]), ffn1_w_in[l], ffn1_w_out[l])
        y, y_ctx = token_mixing(modulate(rms_norm(x), m[3], m[4]), modulate(rms_norm(xc), mc[3], mc[4]),
                                w_in[l], na_q_gain[l], na_k_gain[l], na_rpb[l], ret_decay[l], ret_gn[l],
                                s5_a_re[l], s5_a_im[l], s5_log_dt[l], s5_b_re[l], s5_b_im[l],
                                s5_c_re[l], s5_c_im[l], s5_d[l], s5_w_glu[l], need_ctx)
        x = x + m[5] * (y @ w_out[l])
        x = x + 0.5 * m[8] * swiglu(modulate(rms_norm(x), m[6], m[7]), ffn2_w_in[l], ffn2_w_out[l])
        if need_ctx:
            xc = xc + mc[5] * (y_ctx @ w_out[l])
            xc = xc + 0.5 * mc[8] * swiglu(modulate(rms_norm(xc), mc[6], mc[7]), ffn2_w_in[l], ffn2_w_out[l])
    return x
```

```python
import numpy as np
import concourse.bass as bass
import concourse.mybir as mybir

F32 = mybir.dt.float32
BF16 = mybir.dt.bfloat16
ALU = mybir.AluOpType
AF = mybir.ActivationFunctionType

ENGS = ("pe", "act", "dve", "pool", "sp")


class Buf:
    __slots__ = ("name", "writer", "readers")

    def __init__(self, name=""):
        self.name = name
        self.writer = None
        self.readers = []


class Sched:
    def __init__(self, nc, same_engine_sync=True):
        self.nc = nc
        self.streams = {e: [] for e in ENGS}
        self.count = {}
        self.waited = {e: {} for e in ENGS}
        self.semkeys = list(ENGS)
        self.same = same_engine_sync
        self.ndma = 0

    def new_dma_sem(self, name):
        key = "dma_" + name
        assert key not in self.semkeys
        self.semkeys.append(key)
        return key

    def _deps(self, eng, reads, writes):
        deps = []
        for b in reads:
            if b.writer is not None:
                deps.append(b.writer)
        for b in writes:
            if b.writer is not None:
                deps.append(b.writer)
            deps.extend(b.readers)
        need = {}
        for (k, v) in deps:
            if (not self.same) and k == eng:
                continue
            if k == "pe" and eng == "pe":
                continue
            if need.get(k, 0) < v:
                need[k] = v
        out = []
        for k, v in need.items():
            if self.waited[eng].get(k, 0) < v:
                self.waited[eng][k] = v
                out.append((k, v))
        return out

    def op(self, eng, fn, reads=(), writes=(), dma_sem=None):
        waits = self._deps(eng, reads, writes)
        if dma_sem is not None:
            key = dma_sem
            inc = 16
            prev = self.count.get(key, 0)
            if prev > 0 and self.waited[eng].get(key, 0) < prev:
                self.waited[eng][key] = prev
                waits = [w for w in waits if w[0] != key] + [(key, prev)]
        else:
            key = eng
            inc = 1
        self.count[key] = self.count.get(key, 0) + inc
        tok = (key, self.count[key])
        self.streams[eng].append((waits, fn, key, inc))
        for b in writes:
            b.writer = tok
            b.readers = []
        for b in reads:
            if b not in writes:
                b.readers.append(tok)
        return tok

    def barrier(self):
        snap = dict(self.count)
        for e in ENGS:
            waits = []
            for k, v in snap.items():
                if v > 0 and self.waited[e].get(k, 0) < v:
                    self.waited[e][k] = v
                    waits.append((k, v))
            if waits:
                self.streams[e].append((waits, None, None, 0))

    def emit(self, final_waits=True):
        nc = self.nc
        sems = {}
        ctxs = []
        for k in self.semkeys:
            if self.count.get(k, 0) > 0 or k in ENGS:
                c = nc.semaphore("s_" + k)
                sems[k] = c.__enter__()
                ctxs.append(c)
        blk_ctx = nc.Block()
        block = blk_ctx.__enter__()
        streams = self.streams
        counts = self.count

        def run(engname, h):
            for (waits, fn, key, inc) in streams[engname]:
                for (k, v) in waits:
                    h.wait_ge(sems[k], v)
                if fn is None:
                    continue
                inst = fn(h)
                inst.then_inc(sems[key], inc)
            if engname == "sp" and final_waits:
                for k, v in counts.items():
                    if v > 0:
                        h.wait_ge(sems[k], v)

        @block.tensor
        def _(h):
            run("pe", h)

        @block.scalar
        def _(h):
            run("act", h)

        @block.vector
        def _(h):
            run("dve", h)

        @block.gpsimd
        def _(h):
            run("pool", h)

        @block.sync
        def _(h):
            run("sp", h)

        blk_ctx.__exit__(None, None, None)
        for c in reversed(ctxs):
            c.__exit__(None, None, None)

from concourse.bass_utils import run_bass_kernel_spmd

D = 1024
DFF = 2816
EPS = 1e-6
NCORE = 8


class Prog:
    def __init__(self):
        self.nc = bass.Bass("TRN2", target_bir_lowering=False)
        self.s = Sched(self.nc)
        self._big_c = self.nc.sbuf_tensor("big", [128, 96000], BF16)
        self.big = self._big_c.__enter__()
        self.off = 0
        self._ps = []
        self.nps = 0

    def sb(self, n, dt, shape=None):
        sz = 4 if dt == F32 else 2
        self.off = (self.off + 31) // 32 * 32
        o = self.off
        self.off += n * sz
        assert self.off <= 96000 * 2, ("sbuf overflow", self.off)
        return self.big[:, o // 2: o // 2 + n * sz // 2].bitcast(dt)

    def ps(self, dt=F32):
        n = 512 if dt == F32 else 1024
        c = self.nc.psum_tensor("ps%d" % self.nps, [128, n], dt)
        self.nps += 1
        t = c.__enter__()
        self._ps.append(c)
        return t[:, :]

    def din(self, name, shape, dt=F32):
        return self.nc.dram_tensor(name, list(shape), dt, kind="ExternalInput").ap()

    def dout(self, name, shape, dt=F32):
        return self.nc.dram_tensor(name, list(shape), dt, kind="ExternalOutput").ap()

    def finish(self):
        self.s.emit()
        for c in reversed(self._ps):
            c.__exit__(None, None, None)
        self._big_c.__exit__(None, None, None)
        return self.nc


def bcast_free(ap1, n):
    return bass.AP(ap1.tensor, ap1.offset, [list(ap1.ap[0]), [0, n]])


TOK = 4224
TILES = [(i * 256, 2, 0) for i in range(16)] + [(4096, 1, 1)]
C_GELU = 0.044715
K_GELU = 1.5957691216057308


def build_token_prog(passes):
    P = Prog()
    s = P.s
    x_in = P.din("x", [TOK, D])
    x_out = P.dout("xo", [TOK, D])
    x_scr = P.nc.dram_tensor("xscr", [TOK, D], F32, kind="Internal").ap()
    upd = [i for i, p in enumerate(passes) if p["kind"] in ("ffn", "outproj")]
    last_upd = upd[-1] if upd else -1
    ident_d = P.din("ident", [128, 128])
    w1s = P.sb(8 * 5632, BF16)
    w2s = P.sb(22 * 1024, BF16)
    gT = P.sb(22 * 256, BF16)
    xts = [P.sb(1024, F32) for _ in range(4)]
    hTs = [P.sb(8 * 256, BF16) for _ in range(2)]
    xn = P.sb(1024, BF16)
    sgs = [P.sb(256, F32) for _ in range(2)]
    tmp = P.sb(1024, F32)
    gbc = P.sb(2048, F32)
    modt = P.sb(64, F32)
    small = P.sb(16, F32)
    ident = P.sb(128, BF16)
    identf = P.sb(128, F32)
    tps = [P.ps(BF16) for _ in range(2)]
    abs_ = [P.ps(F32) for _ in range(2)]
    obs = [(P.ps(F32), P.ps(F32)) for _ in range(2)]

    dsem = {n: s.new_dma_sem(n) for n in ["w0", "w1", "x0", "x1", "x2", "x3", "misc", "st", "st2", "y0", "y1"]}
    b_ident = Buf()
    s.op("sp", lambda h: h.dma_start(out=identf, in_=ident_d), writes=[b_ident], dma_sem=dsem["misc"])
    s.op("dve", lambda h: h.tensor_copy(out=ident, in_=identf), reads=[b_ident], writes=[b_ident])

    stage = [gT[:, 0:2816].bitcast(F32), gT[:, 2816:5632].bitcast(F32)]
    xbufs = [Buf() for _ in range(4)]
    xtile_bufs = {}

    def xdram_buf(t):
        if t not in xtile_bufs:
            xtile_bufs[t] = Buf()
        return xtile_bufs[t]

    cast_rr = [0]

    def load_w(dst3, dram, K, N, bstage, bdst):
        nch = (N + 1407) // 1408
        cw = N // nch
        assert cw * nch == N
        for kt in range(K):
            for c in range(nch):
                j = cast_rr[0] % 2
                cast_rr[0] += 1
                st = stage[j][:, 0:cw]
                src = dram[kt * 128:(kt + 1) * 128, c * cw:(c + 1) * cw]
                dst = dst3[:, kt, c * cw:(c + 1) * cw]
                s.op("sp", (lambda h, st=st, src=src: h.dma_start(out=st, in_=src)), writes=[bstage[j]], dma_sem=dsem["w%d" % j])
                eng = "pool" if (cast_rr[0] % 3) else "act"
                if eng == "pool":
                    s.op("pool", (lambda h, st=st, dst=dst: h.tensor_copy(out=dst, in_=st)), reads=[bstage[j]], writes=[bdst])
                else:
                    s.op("act", (lambda h, st=st, dst=dst: h.copy(out=dst, in_=st)), reads=[bstage[j]], writes=[bdst])

    def front_end(xt, bxt, hT, bhT, sub, cls, sc1, sh, bmod, tp, btp):
        ss = small[:, 0:1]
        rstd = small[:, 1:2]
        bss = b_ss[0]
        s.op("act", lambda h: h.activation(out=tmp[:, 0:512].bitcast(BF16), in_=xt, func=AF.Square, accum_out=ss), reads=[bxt], writes=[bss, b_tmp[0]])
        s.op("dve", lambda h: h.tensor_scalar(out=rstd, in0=ss, scalar1=1.0 / D, scalar2=EPS, op0=ALU.mult, op1=ALU.add), reads=[bss], writes=[bss])
        s.op("act", lambda h: h.activation(out=rstd, in_=rstd, func=AF.Sqrt), reads=[bss], writes=[bss])
        s.op("dve", lambda h: h.reciprocal(out=rstd, in_=rstd), reads=[bss], writes=[bss])
        s.op("dve", lambda h: h.tensor_scalar(out=xn, in0=xt, scalar1=rstd, scalar2=None, op0=ALU.mult), reads=[bss, bxt], writes=[b_xn[0]])
        for kt in range(8):
            s.op("pe", (lambda h, kt=kt: h.transpose(out=tp[:, kt * 128:(kt + 1) * 128], in_=xn[:, kt * 128:(kt + 1) * 128], identity=ident)),
                 reads=[b_xn[0], b_ident], writes=[btp])
        hv = hT.rearrange("p (k t) -> p k t", k=8)
        for kt in range(8):
            s.op("act", (lambda h, kt=kt: h.activation(out=hv[:, kt, sub * 128:(sub + 1) * 128], in_=tp[:, kt * 128:(kt + 1) * 128],
                                                      func=AF.Identity, scale=sc1[:, kt, cls:cls + 1], bias=sh[:, kt, cls:cls + 1])),
                 reads=[btp, bmod], writes=[bhT])

    b_tmp = [Buf()]
    b_xn = [Buf()]
    b_tp = [Buf(), Buf()]
    b_ab = [Buf(), Buf()]
    b_ob = [Buf(), Buf()]
    b_hT = [Buf(), Buf()]
    b_sg = [Buf(), Buf()]
    b_ss = [Buf()]
    b_ystg = [Buf(), Buf()]
    b_y = [Buf(), Buf(), Buf(), Buf()]

    def load_mod(mod_d, gbc_d):
        bmod = Buf()
        bg = Buf()
        mv = modt[:, 0:32].rearrange("p (k v) -> p k v", k=8)
        s.op("sp", lambda h: h.dma_start(out=mv, in_=mod_d), writes=[bmod], dma_sem=dsem["misc"])
        sc1 = modt[:, 32:48].rearrange("p (k c) -> p k c", k=8)
        sh = modt[:, 48:64].rearrange("p (k c) -> p k c", k=8)
        mv2 = modt[:, 0:32].rearrange("p (k c v) -> p k c v", k=8, c=2)
        s.op("dve", lambda h: h.tensor_scalar(out=sc1, in0=mv2[:, :, :, 0], scalar1=1.0, scalar2=None, op0=ALU.add), reads=[bmod], writes=[bmod])
        s.op("dve", lambda h: h.tensor_copy(out=sh, in_=mv2[:, :, :, 1]), reads=[bmod], writes=[bmod])
        if gbc_d is not None:
            gv = gbc.rearrange("p (c f) -> p c f", c=2)
            s.op("sp", lambda h: h.dma_start(out=gv, in_=gbc_d.rearrange("c p f -> p c f")), writes=[bg], dma_sem=dsem["misc"])
        return sc1, sh, bmod, bg

    first = [True]

    def xsrc():
        return x_in if first[0] else x_scr

    for pi, pk in enumerate(passes):
        kind = pk["kind"]
        s.barrier()
        bst = [Buf(), Buf()]
        bw1, bw2 = Buf(), Buf()
        w1v = w1s.rearrange("p (k n) -> p k n", k=8)
        w2v = w2s.rearrange("p (k n) -> p k n", k=22)
        if kind == "ffn":
            w1_d = P.din("w1_%d" % pi, [D, 2 * DFF])
            w2_d = P.din("w2_%d" % pi, [DFF, D])
            mod_d = P.din("mod_%d" % pi, [128, 8, 4])
            gbc_d = P.din("gbc_%d" % pi, [2, 128, D])
            sc1, sh, bmod, bg = load_mod(mod_d, gbc_d)
            gscale = 0.5
            s.op("pool", lambda h: h.tensor_scalar(out=gbc, in0=gbc, scalar1=0.5, scalar2=None, op0=ALU.mult), reads=[bg], writes=[bg])
            load_w(w1v, w1_d, 8, 2 * DFF, bst, bw1)
            load_w(w2v, w2_d, 22, D, bst, bw2)
            s.barrier()
            bgT = Buf()
            gTv = gT.rearrange("p (f t) -> p f t", f=22)
            src = xsrc()
            for ti, (t0, nsub, cls) in enumerate(TILES):
                TT = nsub * 128
                hT = hTs[ti % 2]
                bhT = b_hT[ti % 2]
                subx = []
                for sub in range(nsub):
                    xi = (ti * 2 + sub) % 4
                    xt, bxt = xts[xi], xbufs[xi]
                    r0 = t0 + sub * 128
                    s.op("sp", (lambda h, xt=xt, r0=r0, src=src: h.dma_start(out=xt, in_=src[r0:r0 + 128, :])),
                         reads=[xdram_buf(r0)], writes=[bxt], dma_sem=dsem["x%d" % xi])
                    tp = tps[(ti * 2 + sub) % 2]
                    front_end(xt, bxt, hT, bhT, sub, cls, sc1, sh, bmod, tp, b_tp[(ti * 2 + sub) % 2])
                    subx.append((xt, bxt, r0))
                hv = hT.rearrange("p (k t) -> p k t", k=8)
                for f in range(22):
                    ab = abs_[f % 2]
                    bab = b_ab[f % 2]
                    for half in range(2):
                        for kt in range(8):
                            s.op("pe", (lambda h, ab=ab, half=half, kt=kt, f=f, hv=hv, TT=TT: h.matmul(
                                ab[:, half * 256: half * 256 + TT], lhsT=w1v[:, kt, half * DFF + f * 128: half * DFF + (f + 1) * 128],
                                rhs=hv[:, kt, 0:TT], start=(kt == 0), stop=(kt == 7))), reads=[bw1, bhT], writes=[bab])
                    sg = sgs[f % 2]
                    bsg = b_sg[f % 2]
                    s.op("act", (lambda h, sg=sg, ab=ab, TT=TT: h.activation(out=sg[:, 0:TT], in_=ab[:, 0:TT], func=AF.Silu)), reads=[bab], writes=[bsg])
                    s.op("dve", (lambda h, sg=sg, ab=ab, TT=TT, f=f: h.tensor_tensor(out=gTv[:, f, 0:TT], in0=sg[:, 0:TT], in1=ab[:, 256:256 + TT], op=ALU.mult)),
                         reads=[bsg, bab], writes=[bgT])
                for sub in range(nsub):
                    xt, bxt, r0 = subx[sub]
                    ob = obs[sub % 2]
                    bo = b_ob[sub % 2]
                    for half in range(2):
                        for f in range(22):
                            s.op("pe", (lambda h, ob=ob, half=half, f=f, sub=sub: h.matmul(
                                ob[half][:, 0:512], lhsT=gTv[:, f, sub * 128:(sub + 1) * 128], rhs=w2v[:, f, half * 512:(half + 1) * 512],
                                start=(f == 0), stop=(f == 21))), reads=[bgT, bw2], writes=[bo])
                    for half in range(2):
                        s.op("dve", (lambda h, ob=ob, half=half, cls=cls: h.tensor_tensor(
                            out=tmp[:, half * 512:(half + 1) * 512], in0=ob[half][:, 0:512], in1=gbc[:, cls * 1024 + half * 512: cls * 1024 + (half + 1) * 512], op=ALU.mult)),
                            reads=[bo, bg], writes=[b_tmp[0]])
                    s.op("dve", (lambda h, xt=xt: h.tensor_tensor(out=xt, in0=xt, in1=tmp, op=ALU.add)), reads=[b_tmp[0], bxt], writes=[bxt])
                    s.op("sp", (lambda h, xt=xt, r0=r0: h.dma_start(out=x_scr[r0:r0 + 128, :], in_=xt)), reads=[bxt], writes=[xdram_buf(r0)], dma_sem=dsem["st"])
                    if pi == last_upd:
                        s.op("sp", (lambda h, xt=xt, r0=r0: h.dma_start(out=x_out[r0:r0 + 128, :], in_=xt)), reads=[bxt], dma_sem=dsem["st2"])
            first[0] = False
        elif kind == "inproj":
            wi_d = P.din("wi_%d" % pi, [D, 2560])
            mod_d = P.din("mod_%d" % pi, [128, 8, 4])
            proj_d = P.dout("proj", [TOK, 2560])
            sc1, sh, bmod, bg = load_mod(mod_d, None)
            load_w(w1v, wi_d, 8, 2560, bst, bw1)
            s.barrier()
            prs = [w2s[:, 0:5120].bitcast(F32), w2s[:, 5120:10240].bitcast(F32)]
            bprs = [Buf(), Buf()]
            src = xsrc()
            n128 = TOK // 128
            for ti in range(n128):
                cls = 1 if ti == n128 - 1 else 0
                xi = ti % 4
                xt, bxt = xts[xi], xbufs[xi]
                r0 = ti * 128
                s.op("sp", (lambda h, xt=xt, r0=r0, src=src: h.dma_start(out=xt, in_=src[r0:r0 + 128, :])),
                     reads=[xdram_buf(r0)], writes=[bxt], dma_sem=dsem["x%d" % xi])
                hT = hTs[ti % 2]
                bhT = b_hT[ti % 2]
                front_end(xt, bxt, hT, bhT, 0, cls, sc1, sh, bmod, tps[ti % 2], b_tp[ti % 2])
                hv = hT.rearrange("p (k t) -> p k t", k=8)
                pr, bpr = prs[ti % 2], bprs[ti % 2]
                for cb in range(5):
                    ob = obs[cb % 2][0]
                    bo = b_ob[cb % 2]
                    for kt in range(8):
                        s.op("pe", (lambda h, ob=ob, kt=kt, cb=cb, hv=hv: h.matmul(ob[:, 0:512], lhsT=hv[:, kt, 0:128], rhs=w1v[:, kt, cb * 512:(cb + 1) * 512],
                                                                                 start=(kt == 0), stop=(kt == 7))), reads=[bhT, bw1], writes=[bo])
                    if cb % 2 == 0:
                        s.op("act", (lambda h, ob=ob, pr=pr, cb=cb: h.copy(out=pr[:, cb * 512:(cb + 1) * 512], in_=ob[:, 0:512])), reads=[bo], writes=[bpr])
                    else:
                        s.op("dve", (lambda h, ob=ob, pr=pr, cb=cb: h.tensor_copy(out=pr[:, cb * 512:(cb + 1) * 512], in_=ob[:, 0:512])), reads=[bo], writes=[bpr])
                s.op("sp", (lambda h, pr=pr, r0=r0: h.dma_start(out=proj_d[r0:r0 + 128, :], in_=pr)), reads=[bpr], dma_sem=dsem["y%d" % (ti % 2)])
        elif kind == "outproj":
            yab_d = P.din("yab", [768, TOK])
            ysf_d = P.din("ysf", [256, TOK])
            ysb_d = P.din("ysb", [256, TOK])
            wg_d = P.din("wg", [256, 512])
            wo_d = P.din("wo", [D, D])
            gbc_d = P.din("gbc_%d" % pi, [2, 128, D])
            bg = Buf()
            gv = gbc.rearrange("p (c f) -> p c f", c=2)
            s.op("sp", (lambda h, gv=gv, gbc_d=gbc_d: h.dma_start(out=gv, in_=gbc_d.rearrange("c p f -> p c f"))), writes=[bg], dma_sem=dsem["misc"])
            wgv = w1v[:, 0:2, 1024:1536]
            load_w(w1v, wo_d, 8, D, bst, bw1)
            wgview = w1s.rearrange("p (k n) -> p k n", k=8)[:, :, 1024:1536]
            load_w(wgview, wg_d, 2, 512, bst, bw1)
            s.barrier()
            ystg = [w2s[:, 0:3072].bitcast(F32), w2s[:, 3072:6144].bitcast(F32)]
            ys1 = w2s[:, 6144:7168].bitcast(F32)
            ys2 = w2s[:, 7168:8192].bitcast(F32)
            ys3 = w2s[:, 8192:9216].bitcast(F32)
            geT = w2s[:, 9216:9728]
            src = xsrc()
            for ti, (t0, nsub, cls) in enumerate(TILES):
                TT = nsub * 128
                yT = hTs[ti % 2]
                yv = yT.rearrange("p (k t) -> p k t", k=8)
                byT = b_hT[ti % 2]
                st = ystg[ti % 2]
                stv = st.rearrange("p (k t) -> p k t", k=6)
                bstg = b_ystg[ti % 2]
                s.op("sp", (lambda h, stv=stv, t0=t0, TT=TT: h.dma_start(out=stv[:, :, 0:TT], in_=yab_d[:, t0:t0 + TT].rearrange("(k p) t -> p k t", p=128))),
                     writes=[bstg], dma_sem=dsem["y%d" % (ti % 2)])
                s.op("pool", (lambda h, stv=stv, yv=yv, TT=TT: h.tensor_copy(out=yv[:, 0:6, 0:TT], in_=stv[:, :, 0:TT])), reads=[bstg], writes=[byT])
                y1v = ys1.rearrange("p (k t) -> p k t", k=2)
                y2v = ys2.rearrange("p (k t) -> p k t", k=2)
                y3v = ys3.rearrange("p (k t) -> p k t", k=2)
                gev = geT.rearrange("p (k t) -> p k t", k=2)
                by1, by2, by3, bge = b_y
                s.op("sp", (lambda h, t0=t0, TT=TT: h.dma_start(out=y1v[:, :, 0:TT], in_=ysf_d[:, t0:t0 + TT].rearrange("(k p) t -> p k t", p=128))), writes=[by1], dma_sem=dsem["w0"])
                s.op("sp", (lambda h, t0=t0, TT=TT: h.dma_start(out=y2v[:, :, 0:TT], in_=ysb_d[:, t0:t0 + TT].rearrange("(k p) t -> p k t", p=128))), writes=[by2], dma_sem=dsem["w1"])
                s.op("dve", lambda h: h.tensor_tensor(out=ys1, in0=ys1, in1=ys2, op=ALU.add), reads=[by1, by2], writes=[by1])
                s.op("dve", lambda h: h.tensor_tensor(out=ys2, in0=ys1, in1=ys1, op=ALU.mult), reads=[by1], writes=[by2])
                s.op("dve", lambda h: h.tensor_scalar(out=ys2, in0=ys2, scalar1=C_GELU, scalar2=1.0, op0=ALU.mult, op1=ALU.add), reads=[by2], writes=[by2])
                s.op("dve", lambda h: h.tensor_tensor(out=ys2, in0=ys2, in1=ys1, op=ALU.mult), reads=[by1, by2], writes=[by2])
                s.op("act", lambda h: h.activation(out=ys3, in_=ys2, func=AF.Sigmoid, scale=K_GELU), reads=[by2], writes=[by3])
                s.op("dve", lambda h: h.tensor_tensor(out=geT, in0=ys1, in1=ys3, op=ALU.mult), reads=[by1, by3], writes=[bge])
                gl = abs_[ti % 2]
                gl2 = abs_[(ti + 1) % 2]
                bgl = b_ab[ti % 2]
                bgl2 = b_ob[ti % 2]
                glb = [gl, obs[ti % 2][1]]
                for fb in range(4):
                    dst = glb[fb // 2][:, (fb % 2) * 256:(fb % 2) * 256 + TT]
                    for kt in range(2):
                        s.op("pe", (lambda h, dst=dst, fb=fb, kt=kt, TT=TT: h.matmul(dst, lhsT=w1v[:, kt, 1024 + fb * 128:1024 + (fb + 1) * 128], rhs=gev[:, kt, 0:TT],
                                                                                    start=(kt == 0), stop=(kt == 1))), reads=[bge, bw1], writes=[bgl, bgl2])
                for i in range(2):
                    s.op("act", (lambda h, i=i, TT=TT, glb=glb: h.activation(out=ys3[:, i * 256:i * 256 + TT], in_=glb[1][:, i * 256:i * 256 + TT], func=AF.Sigmoid)), reads=[bgl, bgl2], writes=[by3])
                    s.op("dve", (lambda h, i=i, TT=TT, yv=yv, glb=glb: h.tensor_tensor(out=yv[:, 6 + i, 0:TT], in0=glb[0][:, i * 256:i * 256 + TT], in1=ys3[:, i * 256:i * 256 + TT], op=ALU.mult)),
                         reads=[bgl, bgl2, by3], writes=[byT])
                for sub in range(nsub):
                    xi = (ti * 2 + sub) % 4
                    xt, bxt = xts[xi], xbufs[xi]
                    r0 = t0 + sub * 128
                    s.op("sp", (lambda h, xt=xt, r0=r0, src=src: h.dma_start(out=xt, in_=src[r0:r0 + 128, :])),
                         reads=[xdram_buf(r0)], writes=[bxt], dma_sem=dsem["x%d" % xi])
                    ob = obs[sub % 2]
                    bo = b_ob[sub % 2]
                    for half in range(2):
                        for kt in range(8):
                            s.op("pe", (lambda h, ob=ob, half=half, kt=kt, sub=sub, yv=yv: h.matmul(
                                ob[half][:, 0:512], lhsT=yv[:, kt, sub * 128:(sub + 1) * 128], rhs=w1v[:, kt, half * 512:(half + 1) * 512],
                                start=(kt == 0), stop=(kt == 7))), reads=[byT, bw1], writes=[bo])
                    for half in range(2):
                        s.op("dve", (lambda h, ob=ob, half=half, cls=cls: h.tensor_tensor(
                            out=tmp[:, half * 512:(half + 1) * 512], in0=ob[half][:, 0:512], in1=gbc[:, cls * 1024 + half * 512: cls * 1024 + (half + 1) * 512], op=ALU.mult)),
                            reads=[bo, bg], writes=[b_tmp[0]])
                    s.op("dve", (lambda h, xt=xt: h.tensor_tensor(out=xt, in0=xt, in1=tmp, op=ALU.add)), reads=[b_tmp[0], bxt], writes=[bxt])
                    s.op("sp", (lambda h, xt=xt, r0=r0: h.dma_start(out=x_scr[r0:r0 + 128, :], in_=xt)), reads=[bxt], writes=[xdram_buf(r0)], dma_sem=dsem["st"])
                    if pi == last_upd:
                        s.op("sp", (lambda h, xt=xt, r0=r0: h.dma_start(out=x_out[r0:r0 + 128, :], in_=xt)), reads=[bxt], dma_sem=dsem["st2"])
            first[0] = False
    return P.finish()


def build_mod_prog():
    P = Prog()
    s = P.s
    NCOL = 4608
    cT_d = P.din("cT", [128, 8, 3])
    w_d = P.din("w", [D, NCOL])
    b_d = P.din("b", [3, NCOL])
    m_d = P.dout("m", [3, NCOL])
    cT = P.sb(24, F32)
    ws = [P.sb(8 * 512, F32) for _ in range(2)]
    bt = P.sb(NCOL, F32)
    mt = P.sb(NCOL, F32)
    pss = [P.ps(F32) for _ in range(2)]
    d0, d1, dm = s.new_dma_sem("a0"), s.new_dma_sem("a1"), s.new_dma_sem("am")
    bc, bb, bm = Buf(), Buf(), Buf()
    bws = [Buf(), Buf()]
    bps = [Buf(), Buf()]
    cv = cT.rearrange("p (k r) -> p k r", k=8)
    s.op("sp", lambda h: h.dma_start(out=cv, in_=cT_d), writes=[bc], dma_sem=dm)
    s.op("sp", lambda h: h.dma_start(out=bt[0:3, :], in_=b_d), writes=[bb], dma_sem=dm)
    s.op("act", lambda h: h.activation(out=cT, in_=cT, func=AF.Silu), reads=[bc], writes=[bc])
    for cb in range(NCOL // 512):
        j = cb % 2
        wv = ws[j].rearrange("p (k n) -> p k n", k=8)
        s.op("sp", (lambda h, wv=wv, cb=cb: h.dma_start(out=wv, in_=w_d[:, cb * 512:(cb + 1) * 512].rearrange("(k p) n -> p k n", p=128))),
             writes=[bws[j]], dma_sem=[d0, d1][j])
        for kt in range(8):
            s.op("pe", (lambda h, wv=wv, kt=kt, j=j: h.matmul(pss[j][0:3, 0:512], lhsT=cv[:, kt, :], rhs=wv[:, kt, :], start=(kt == 0), stop=(kt == 7))),
                 reads=[bc, bws[j]], writes=[bps[j]])
        s.op("dve", (lambda h, j=j, cb=cb: h.tensor_tensor(out=mt[0:3, cb * 512:(cb + 1) * 512], in0=pss[j][0:3, 0:512], in1=bt[0:3, cb * 512:(cb + 1) * 512], op=ALU.add)),
             reads=[bps[j], bb], writes=[bm])
    s.op("sp", lambda h: h.dma_start(out=m_d, in_=mt[0:3, :]), reads=[bm], dma_sem=dm)
    return P.finish()


_PROG_CACHE = {}


def get_prog(key, fn):
    if key not in _PROG_CACHE:
        _PROG_CACHE[key] = fn()
    return _PROG_CACHE[key]


def run(nc, in_maps):
    res = run_bass_kernel_spmd(nc, in_maps, core_ids=list(range(NCORE)))
    return res.results


NS = 16640
NCH = 130
NROW = 260
MAGIC = 12582912.0
TWO_PI = 6.283185307179586


def ap3(ap, mid_n):
    return bass.AP(ap.tensor, ap.offset, [list(ap.ap[0]), [0, mid_n], list(ap.ap[1])])


def build_mixer_prog(do_na=True, do_ret=True, do_s5=True):
    P = Prog()
    s = P.s
    banks = [P.ps(F32) for _ in range(8)]
    dsem = {n: s.new_dma_sem(n) for n in ["a0", "a1", "b0", "b1", "c0", "c1", "m", "o0", "o1"]}

    if do_na:
        P.off = 0
        naq_d = P.din("naq", [64, NS]); nak_d = P.din("nak", [64, NS])
        navA_d = P.din("navA", [128, NCH * 66]); navB_d = P.din("navB", [128, 127 * 66])
        nag_d = P.din("nag", [64, 2]); naG_d = P.din("naG", [128, 4096]); naM_d = P.din("naM", [128, 64])
        obd_d = P.din("onesbd", [64, 64])
        yA_d = P.dout("yA", [64, NROW, 64])
        qT = P.sb(NS, BF16); kT = P.sb(NS, BF16)
        v1A = P.sb(NCH * 66, BF16); v1B = P.sb(127 * 66, BF16)
        EB = P.sb(4096, F32); Mk = P.sb(64, F32); gq = P.sb(2, F32); obd = P.sb(64, F32)
        stg = [P.sb(2048, F32) for _ in range(2)]
        sqs = [P.sb(512, F32) for _ in range(2)]
        rr = [P.sb(512, F32) for _ in range(2)]
        pTs = [P.sb(384, BF16) for _ in range(4)]
        rec = [P.sb(16, F32) for _ in range(2)]
        yst = [P.sb(320, F32) for _ in range(2)]
        bq, bk, bvA, bvB, bEB, bM, bg, bobd = [Buf() for _ in range(8)]
        bstg = [Buf(), Buf()]; bsq = [Buf(), Buf()]; brr = [Buf(), Buf()]; bpT = [Buf(), Buf(), Buf(), Buf()]; brec = [Buf(), Buf()]; byst = [Buf(), Buf()]
        bbank = [Buf() for _ in range(8)]
        s.op("sp", lambda h: h.dma_start(out=gq[0:64, :], in_=nag_d), writes=[bg], dma_sem=dsem["m"])
        s.op("sp", lambda h: h.dma_start(out=obd[0:64, :], in_=obd_d), writes=[bobd], dma_sem=dsem["m"])
        s.op("sp", lambda h: h.dma_start(out=Mk, in_=naM_d), writes=[bM], dma_sem=dsem["m"])
        s.op("sp", lambda h: h.dma_start(out=EB, in_=naG_d), writes=[bEB], dma_sem=dsem["m"])
        s.op("act", lambda h: h.activation(out=EB, in_=EB, func=AF.Exp), reads=[bEB], writes=[bEB])
        EB3 = EB.rearrange("p (a q) -> p a q", q=64)
        s.op("dve", lambda h: h.tensor_tensor(out=EB3, in0=EB3, in1=ap3(Mk, 64), op=ALU.mult), reads=[bEB, bM], writes=[bEB])
        cnt = 0
        for (dsrc, dst, tot, bdst) in ((navA_d, v1A, NCH * 66, bvA), (navB_d, v1B, 127 * 66, bvB)):
            c0 = 0
            while c0 < tot:
                n = min(2048, tot - c0)
                j = cnt % 2; cnt += 1
                s.op("sp", (lambda h, j=j, c0=c0, n=n, dsrc=dsrc: h.dma_start(out=stg[j][:, 0:n], in_=dsrc[:, c0:c0 + n])), writes=[bstg[j]], dma_sem=dsem["a%d" % j])
                s.op("pool", (lambda h, j=j, c0=c0, n=n, dst=dst: h.tensor_copy(out=dst[:, c0:c0 + n], in_=stg[j][:, 0:n])), reads=[bstg[j]], writes=[bdst])
                c0 += n
        for (dsrc, dst, gi, bdst) in ((naq_d, qT, 0, bq), (nak_d, kT, 1, bk)):
            nb = (NS + 511) // 512
            for blk in range(nb):
                c0 = blk * 512
                n = min(512, NS - c0)
                j = cnt % 2; cnt += 1
                st = stg[j][0:64, 0:n]
                s.op("sp", (lambda h, st=st, c0=c0, n=n, dsrc=dsrc: h.dma_start(out=st, in_=dsrc[:, c0:c0 + n])), writes=[bstg[j]], dma_sem=dsem["a%d" % j])
                s.op("act", (lambda h, st=st, j=j, n=n: h.activation(out=sqs[j][0:64, 0:n], in_=st, func=AF.Square)), reads=[bstg[j]], writes=[bsq[j]])
                bk_ = banks[j]
                s.op("pe", (lambda h, j=j, n=n, bk_=bk_: h.matmul(bk_[0:64, 0:n], lhsT=obd[0:64, :], rhs=sqs[j][0:64, 0:n], start=True, stop=True)), reads=[bsq[j], bobd], writes=[bbank[j]])
                s.op("dve", (lambda h, j=j, n=n, bk_=bk_: h.tensor_scalar(out=rr[j][0:64, 0:n], in0=bk_[0:64, 0:n], scalar1=1.0 / 32, scalar2=EPS, op0=ALU.mult, op1=ALU.add)), reads=[bbank[j]], writes=[brr[j]])
                s.op("act", (lambda h, j=j, n=n: h.activation(out=rr[j][0:64, 0:n], in_=rr[j][0:64, 0:n], func=AF.Sqrt)), reads=[brr[j]], writes=[brr[j]])
                s.op("dve", (lambda h, j=j, n=n: h.reciprocal(out=rr[j][0:64, 0:n], in_=rr[j][0:64, 0:n])), reads=[brr[j]], writes=[brr[j]])
                s.op("dve", (lambda h, j=j, n=n, st=st, dst=dst, c0=c0, gi=gi: h.scalar_tensor_tensor(out=dst[0:64, c0:c0 + n], in0=st, scalar=gq[0:64, gi:gi + 1], in1=rr[j][0:64, 0:n], op0=ALU.mult, op1=ALU.mult)),
                     reads=[bstg[j], brr[j], bg], writes=[bdst])
        SCALE = 32 ** -0.5
        vA3 = v1A.rearrange("p (c f) -> p c f", f=66)
        vB3 = v1B.rearrange("p (c f) -> p c f", f=66)
        EB5 = EB.rearrange("p (d h x) -> p d h x", d=8, h=2)
        GR = 5
        for g in range(NROW // GR):
            pv = banks[2 + g % 2]; bpv = bbank[2 + g % 2]
            for ri in range(GR):
                rq = g * GR + ri
                for hl in range(2):
                    u = ri * 2 + hl
                    ui = (rq * 2 + hl) % 4
                    sc = banks[4 + ui]; bsc = bbank[4 + ui]
                    pT = pTs[ui]
                    if rq < 4:
                        chunks = [(4, 0, "A", 0), (5, 128, "A", 1)]
                        qc0 = rq * 64
                        dI = None
                    else:
                        r = rq - 4
                        rs = min(max(r - 4, 0), 248)
                        dI = r - rs
                        chunks = []
                        for c in range(4):
                            R0 = rs + 2 * c
                            if rs % 2 == 0:
                                chunks.append((c, 256 + 64 * R0, "A", 2 + R0 // 2))
                            else:
                                chunks.append((c, 256 + 64 * R0, "B", (R0 - 1) // 2))
                        chunks += [(4, 0, "A", 0), (5, 128, "A", 1)]
                        qc0 = 256 + r * 64
                    for (slot, kc0, til, ci) in chunks:
                        s.op("pe", (lambda h, sc=sc, slot=slot, kc0=kc0, hl=hl, qc0=qc0: h.matmul(
                            sc[:, slot * 64:(slot + 1) * 64], lhsT=kT[32 * hl:32 * hl + 32, kc0:kc0 + 128], rhs=qT[32 * hl:32 * hl + 32, qc0:qc0 + 64], start=True, stop=True)),
                            reads=[bq, bk], writes=[bsc])
                    lo = chunks[0][0] * 64
                    s.op("act", (lambda h, sc=sc, pT=pT, lo=lo: h.activation(out=pT[:, lo:384], in_=sc[:, lo:384], func=AF.Exp, scale=SCALE)), reads=[bsc], writes=[bpT[ui]])
                    if dI is not None:
                        s.op("dve", (lambda h, pT=pT, dI=dI, hl=hl: h.tensor_tensor(out=pT[:, 0:256], in0=pT[:, 0:256], in1=EB5[:, dI, hl, :], op=ALU.mult)), reads=[bEB, bpT[ui]], writes=[bpT[ui]])
                    for k_, (slot, kc0, til, ci) in enumerate(chunks):
                        vv = vA3 if til == "A" else vB3
                        s.op("pe", (lambda h, pv=pv, u=u, pT=pT, slot=slot, vv=vv, ci=ci, hl=hl, k_=k_, nck=len(chunks): h.matmul(
                            pv[0:64, u * 33:(u + 1) * 33], lhsT=pT[:, slot * 64:(slot + 1) * 64], rhs=vv[:, ci, hl * 33:(hl + 1) * 33], start=(k_ == 0), stop=(k_ == nck - 1))),
                            reads=[bpT[ui], bvA, bvB], writes=[bpv])
            gi = g % 2
            pv3 = pv[0:64, 0:GR * 2 * 33].rearrange("p (u f) -> p u f", f=33)
            rc = rec[gi][0:64, 0:GR * 2]
            s.op("dve", (lambda h, rc=rc, pv3=pv3: h.reciprocal(out=rc, in_=pv3[:, :, 32])), reads=[bpv], writes=[brec[gi]])
            y3 = yst[gi][0:64, 0:GR * 64].rearrange("p (u f) -> p u f", f=32)
            rc3 = bass.AP(rc.tensor, rc.offset, [list(rc.ap[0]), list(rc.ap[1]), [0, 32]])
            s.op("dve", (lambda h, y3=y3, pv3=pv3, rc3=rc3: h.tensor_tensor(out=y3, in0=pv3[:, :, 0:32], in1=rc3, op=ALU.mult)), reads=[bpv, brec[gi]], writes=[byst[gi]])
            s.op("sp", (lambda h, gi=gi, g=g: h.dma_start(out=yA_d[:, g * GR:(g + 1) * GR, :], in_=yst[gi][0:64, 0:GR * 64].rearrange("p (r f) -> p r f", f=64))),
                 reads=[byst[gi]], dma_sem=dsem["o%d" % gi])
        s.barrier()
    if do_ret:
        build_mixer_ret(P, banks, dsem)
    if do_s5:
        build_mixer_s5(P, banks, dsem)
    return P.finish()


def c_(a):
    return np.ascontiguousarray(a, dtype=np.float32)


_CONST = {}


def na_consts():
    if "na" not in _CONST:
        w = np.arange(64)
        cs = np.clip(w - 8, 0, 48)
        kc = np.arange(64)
        valid = (kc[:, None] >= cs[None, :]) & (kc[:, None] < cs[None, :] + 16)
        M = np.concatenate([valid, valid], 0).astype(np.float32)
        coff = np.clip(kc[:, None] - w[None, :] + 15, 0, 30)
        kr = np.arange(2)[:, None, None]; dI = np.arange(8)[None, :, None]; c = np.arange(4)[None, None, :]
        roff = (2 * c + kr) - dI + 7
        obd = np.kron(np.eye(2, dtype=np.float32), np.ones((32, 32), np.float32))
        _CONST["na"] = (M, coff, roff, obd)
    return _CONST["na"]


def prep_na(seq_b, l_rpb, qg, kg, j):
    M, coff, roff, obd = na_consts()
    q = seq_b[:, 64 * j:64 * j + 64]; k = seq_b[:, 256 + 64 * j:256 + 64 * j + 64]; v = seq_b[:, 512 + 64 * j:512 + 64 * j + 64]
    v1 = np.ones((NS, 2, 33), np.float32)
    v1[:, :, 0:32] = v.reshape(NS, 2, 32)
    v1 = v1.reshape(NS, 66)
    navA = v1.reshape(NCH, 128, 66).transpose(1, 0, 2).reshape(128, NCH * 66)
    navB = v1[320:320 + 127 * 128].reshape(127, 128, 66).transpose(1, 0, 2).reshape(128, 127 * 66)
    rp = l_rpb[2 * j:2 * j + 2]
    G = rp[:, roff][:, :, :, :, coff]
    G = G.transpose(1, 4, 2, 0, 3, 5).reshape(128, 4096)
    nag = np.stack([np.tile(qg, 2), np.tile(kg, 2)], 1)
    return {"naq": c_(q.T), "nak": c_(k.T), "navA": c_(navA), "navB": c_(navB), "nag": c_(nag), "naG": c_(G), "naM": c_(M), "onesbd": c_(obd)}


def unprep_na(yA):
    t = yA.transpose(1, 0, 2).reshape(NROW * 64, 64)
    return t[:256], t[256:]


def build_mixer_ret(P, banks, dsem):
    s = P.s
    P.off = 0
    G = 5
    NG = NCH // G
    rq_d = P.din("rq", [128, NCH, 64]); rk_d = P.din("rk", [128, NCH, 64]); rv_d = P.din("rv", [128, NCH, 128]); rg_d = P.din("rg", [128, NCH, 128])
    rcos_d = P.din("rcos", [128, NCH, 32]); rsin_d = P.din("rsin", [128, NCH, 32])
    rdec_d = P.din("rdec", [128, 2]); rgn_d = P.din("rgn", [128, 128])
    rcst_d = P.din("rcst", [128, 2 * 128 + 2 + 2 * 128])
    ident_d = P.din("ident", [128, 128])
    yB_d = P.dout("yB", [128, NCH, 128])
    Kt = P.sb(NCH * 64, BF16); Vb = P.sb(NCH * 128, BF16); QT = P.sb(NS, BF16); KT = P.sb(NS, BF16)
    Sf = P.sb(NCH * 128, BF16)
    cst = P.sb(514, F32); lg = P.sb(8, F32); DT = P.sb(128, BF16); DTt = P.sb(256, F32)
    wend = P.sb(2, F32); gch = P.sb(2, F32); WIN = P.sb(256, F32); gn = P.sb(128, F32)
    ident = P.sb(128, BF16); identf = P.sb(128, F32)
    qs = [P.sb(G * 64, F32) for _ in range(2)]; ks = [P.sb(G * 64, F32) for _ in range(2)]
    vs = [P.sb(G * 128, F32) for _ in range(2)]
    cs_ = [P.sb(G * 32, F32) for _ in range(2)]; sn_ = [P.sb(G * 32, F32) for _ in range(2)]
    t1 = P.sb(G * 32, F32); t2 = P.sb(G * 32, F32)
    qr = P.sb(G * 64, BF16); krf = P.sb(G * 64, F32)
    Sst = P.sb(128, F32); Sb = P.sb(128, F32); Sbb = P.sb(128, BF16)
    Vw = [P.sb(128, BF16) for _ in range(2)]
    PT = [P.sb(128, BF16) for _ in range(2)]
    Qw = [P.sb(256, BF16) for _ in range(2)]
    gs = [P.sb(128, F32) for _ in range(2)]
    on = [P.sb(128, F32) for _ in range(2)]
    sm = P.sb(16, F32)
    junk = P.sb(128, F32)
    bbank = [Buf() for _ in range(8)]
    (bKt, bVb, bQT, bKT, bSf, bcst, blg, bDT, bwend, bWIN, bgn, bid, bt, bqr, bkrf, bSst, bSb, bSbb, bsm, bjunk) = [Buf() for _ in range(20)]
    bqs = [Buf(), Buf()]; bks = [Buf(), Buf()]; bvs = [Buf(), Buf()]; bcs = [Buf(), Buf()]
    bVw = [Buf(), Buf()]; bPT = [Buf(), Buf()]; bQw = [Buf(), Buf()]; bgs = [Buf(), Buf()]; bon = [Buf(), Buf()]
    s.op("sp", lambda h: h.dma_start(out=identf, in_=ident_d), writes=[bid], dma_sem=dsem["m"])
    s.op("dve", lambda h: h.tensor_copy(out=ident, in_=identf), reads=[bid], writes=[bid])
    s.op("sp", lambda h: h.dma_start(out=cst, in_=rcst_d), writes=[bcst], dma_sem=dsem["m"])
    s.op("sp", lambda h: h.dma_start(out=lg[:, 0:2], in_=rdec_d), writes=[blg], dma_sem=dsem["m"])
    s.op("sp", lambda h: h.dma_start(out=gn, in_=rgn_d), writes=[bgn], dma_sem=dsem["m"])
    s.op("act", lambda h: h.activation(out=lg[:, 2:4], in_=lg[:, 0:2], func=AF.Exp, scale=-1.0), reads=[blg], writes=[blg])
    s.op("act", lambda h: h.activation(out=lg[:, 2:4], in_=lg[:, 2:4], func=AF.Ln, bias=1.0), reads=[blg], writes=[blg])
    s.op("dve", lambda h: h.tensor_scalar(out=lg[:, 4:6], in0=lg[:, 2:4], scalar1=-1.0, scalar2=None, op0=ALU.mult), reads=[blg], writes=[blg])
    lgf, lgb = lg[:, 4:5], lg[:, 5:6]
    s.op("act", lambda h: h.activation(out=DTt[:, 0:128], in_=cst[:, 0:128], func=AF.Exp, scale=lgf), reads=[blg, bcst], writes=[bDT])
    s.op("act", lambda h: h.activation(out=DTt[:, 128:256], in_=cst[:, 128:256], func=AF.Exp, scale=lgb), reads=[blg, bcst], writes=[bDT])
    s.op("dve", lambda h: h.tensor_tensor(out=DT, in0=DTt[:, 0:128], in1=DTt[:, 128:256], op=ALU.add), reads=[bDT], writes=[bDT])
    s.op("act", lambda h: h.activation(out=wend[:, 0:1], in_=cst[:, 256:257], func=AF.Exp, scale=lgf), reads=[blg, bcst], writes=[bwend])
    s.op("act", lambda h: h.activation(out=wend[:, 1:2], in_=cst[:, 257:258], func=AF.Exp, scale=lgb), reads=[blg, bcst], writes=[bwend])
    s.op("act", lambda h: h.activation(out=gch[:, 0:1], in_=lgf, func=AF.Exp, scale=128.0), reads=[blg], writes=[bwend])
    s.op("act", lambda h: h.activation(out=gch[:, 1:2], in_=lgb, func=AF.Exp, scale=128.0), reads=[blg], writes=[bwend])
    s.op("act", lambda h: h.activation(out=WIN[:, 0:128], in_=cst[:, 258:386], func=AF.Exp, scale=lgf), reads=[blg, bcst], writes=[bWIN])
    s.op("act", lambda h: h.activation(out=WIN[:, 128:256], in_=cst[:, 386:514], func=AF.Exp, scale=lgb), reads=[blg, bcst], writes=[bWIN])
    Kt3 = Kt.rearrange("p (c d) -> p c d", d=64)
    Vb3 = Vb.rearrange("p (c d) -> p c d", d=128)
    Sf3 = Sf.rearrange("p (c d) -> p c d", d=128)
    for g in range(NG):
        j = g % 2
        c0 = g * G
        q3 = qs[j].rearrange("p (c d) -> p c d", d=64); k3 = ks[j].rearrange("p (c d) -> p c d", d=64)
        v3 = vs[j].rearrange("p (c d) -> p c d", d=128)
        co3 = cs_[j].rearrange("p (c d) -> p c d", d=32); si3 = sn_[j].rearrange("p (c d) -> p c d", d=32)
        s.op("sp", (lambda h, q3=q3, c0=c0: h.dma_start(out=q3, in_=rq_d[:, c0:c0 + G, :])), writes=[bqs[j]], dma_sem=dsem["a%d" % j])
        s.op("sp", (lambda h, k3=k3, c0=c0: h.dma_start(out=k3, in_=rk_d[:, c0:c0 + G, :])), writes=[bks[j]], dma_sem=dsem["b%d" % j])
        s.op("sp", (lambda h, v3=v3, c0=c0: h.dma_start(out=v3, in_=rv_d[:, c0:c0 + G, :])), writes=[bvs[j]], dma_sem=dsem["c%d" % j])
        s.op("sp", (lambda h, co3=co3, c0=c0: h.dma_start(out=co3, in_=rcos_d[:, c0:c0 + G, :])), writes=[bcs[j]], dma_sem=dsem["o%d" % j])
        s.op("sp", (lambda h, si3=si3, c0=c0: h.dma_start(out=si3, in_=rsin_d[:, c0:c0 + G, :])), writes=[bcs[j]], dma_sem=dsem["o%d" % j])
        s.op("pool", (lambda h, v3=v3, c0=c0: h.tensor_copy(out=Vb3[:, c0:c0 + G, :], in_=v3)), reads=[bvs[j]], writes=[bVb])
        t13 = t1.rearrange("p (c d) -> p c d", d=32); t23 = t2.rearrange("p (c d) -> p c d", d=32)
        qr3 = qr.rearrange("p (c d) -> p c d", d=64); kr3 = krf.rearrange("p (c d) -> p c d", d=64)
        for (z3, o3, bz, bo) in ((q3, qr3, bqs[j], bqr), (k3, kr3, bks[j], bkrf)):
            s.op("dve", (lambda h, z3=z3, co3=co3: h.tensor_tensor(out=t13, in0=z3[:, :, 0:32], in1=co3, op=ALU.mult)), reads=[bz, bcs[j]], writes=[bt])
            s.op("dve", (lambda h, z3=z3, si3=si3: h.tensor_tensor(out=t23, in0=z3[:, :, 32:64], in1=si3, op=ALU.mult)), reads=[bz, bcs[j]], writes=[bt])
            s.op("dve", (lambda h, o3=o3: h.tensor_tensor(out=o3[:, :, 0:32], in0=t13, in1=t23, op=ALU.subtract)), reads=[bt], writes=[bo])
            s.op("dve", (lambda h, z3=z3, si3=si3: h.tensor_tensor(out=t13, in0=z3[:, :, 0:32], in1=si3, op=ALU.mult)), reads=[bz, bcs[j], bo], writes=[bt])
            s.op("dve", (lambda h, z3=z3, co3=co3: h.tensor_tensor(out=t23, in0=z3[:, :, 32:64], in1=co3, op=ALU.mult)), reads=[bz, bcs[j]], writes=[bt])
            s.op("dve", (lambda h, o3=o3: h.tensor_tensor(out=o3[:, :, 32:64], in0=t13, in1=t23, op=ALU.add)), reads=[bt], writes=[bo])
        s.op("act", (lambda h, c0=c0: h.activation(out=Kt3[:, c0:c0 + G, :], in_=kr3, func=AF.Copy, scale=0.125)), reads=[bkrf], writes=[bKt])
        for (src3, dstT, bsrc, bdst, bi) in ((qr3, QT, bqr, bQT, 0), (Kt3[:, c0:c0 + G, :], KT, bKt, bKT, 1)):
            tp = banks[bi][:, :].bitcast(BF16)
            for ci in range(G):
                s.op("pe", (lambda h, tp=tp, src3=src3, ci=ci: h.transpose(out=tp[0:64, ci * 128:(ci + 1) * 128], in_=src3[:, ci, :], identity=ident)),
                     reads=[bsrc, bid], writes=[bbank[bi]])
            col0 = c0 * 128
            if bi == 0:
                s.op("act", (lambda h, tp=tp, dstT=dstT, col0=col0: h.copy(out=dstT[0:64, col0:col0 + G * 128], in_=tp[0:64, 0:G * 128])), reads=[bbank[bi]], writes=[bdst])
            else:
                s.op("dve", (lambda h, tp=tp, dstT=dstT, col0=col0: h.tensor_copy(out=dstT[0:64, col0:col0 + G * 128], in_=tp[0:64, 0:G * 128])), reads=[bbank[bi]], writes=[bdst])
    s.op("dve", lambda h: h.memset(Sst[0:64, :], 0.0), writes=[bSst])
    s.op("dve", lambda h: h.memset(Sb[0:64, :], 0.0), writes=[bSb])
    for n in range(NCH):
        j = n % 2
        s.op("act", (lambda h, n=n: h.copy(out=Sf3[0:64, n, :], in_=Sst[0:64, :])), reads=[bSst], writes=[bSf])
        s.op("pool", (lambda h, n=n, j=j: h.tensor_scalar(out=Vw[j], in0=Vb3[:, n, :], scalar1=wend[:, 0:1], scalar2=None, op0=ALU.mult)), reads=[bVb, bwend], writes=[bVw[j]])
        bk_ = banks[2 + j]
        s.op("pe", (lambda h, n=n, j=j, bk_=bk_: h.matmul(bk_[0:64, 0:128], lhsT=Kt3[:, n, :], rhs=Vw[j], start=True, stop=True)), reads=[bKt, bVw[j]], writes=[bbank[2 + j]])
        s.op("dve", (lambda h, bk_=bk_: h.scalar_tensor_tensor(out=Sst[0:64, :], in0=Sst[0:64, :], scalar=gch[0:64, 0:1], in1=bk_[0:64, 0:128], op0=ALU.mult, op1=ALU.add)),
             reads=[bbank[2 + j], bSst, bwend], writes=[bSst])
    order = [1, 0] + list(range(NCH - 1, 1, -1))
    G2 = 4
    AX = mybir.AxisListType.X
    ysts = [P.sb(G2 * 128, F32) for _ in range(2)]
    gs4 = [P.sb(G2 * 128, F32) for _ in range(2)]
    sqb = P.sb(G2 * 128, F32)
    sm4 = P.sb(32, F32)
    bysts = [Buf(), Buf()]; bgs4 = [Buf(), Buf()]; bsqb = Buf(); bsm4 = Buf()
    ngroups = (NCH + G2 - 1) // G2
    oi = 0
    for gi_ in range(ngroups):
        grp = order[gi_ * G2:(gi_ + 1) * G2]
        ng = len(grp)
        gj = gi_ % 2
        ob = banks[6 + gj]; bob = bbank[6 + gj]
        for k, n in enumerate(grp):
            j = oi % 2
            oi += 1
            cols = slice(n * 128, (n + 1) * 128)
            osl = slice(k * 128, (k + 1) * 128)
            s.op("sp", (lambda h, n=n, gj=gj, osl=osl: h.dma_start(out=gs4[gj][:, osl], in_=rg_d[:, n, :])), writes=[bgs4[gj]], dma_sem=dsem["a%d" % j])
            sc = banks[4 + j]
            s.op("pe", (lambda h, sc=sc, cols=cols: h.matmul(sc[:, 0:128], lhsT=KT[0:64, cols], rhs=QT[0:64, cols], start=True, stop=True)), reads=[bKT, bQT], writes=[bbank[4 + j]])
            s.op("dve", (lambda h, sc=sc, j=j: h.tensor_tensor(out=PT[j], in0=sc[:, 0:128], in1=DT, op=ALU.mult)), reads=[bbank[4 + j], bDT], writes=[bPT[j]])
            s.op("pool", (lambda h, j=j, cols=cols: h.tensor_tensor(out=Qw[j][0:64, 0:128], in0=QT[0:64, cols], in1=WIN[0:64, 0:128], op=ALU.mult)), reads=[bQT, bWIN], writes=[bQw[j]])
            s.op("pool", (lambda h, j=j, cols=cols: h.tensor_tensor(out=Qw[j][0:64, 128:256], in0=QT[0:64, cols], in1=WIN[0:64, 128:256], op=ALU.mult)), reads=[bQT, bWIN], writes=[bQw[j]])
            s.op("act", lambda h: h.copy(out=Sbb[0:64, :], in_=Sb[0:64, :]), reads=[bSb], writes=[bSbb])
            s.op("pe", (lambda h, ob=ob, j=j, n=n, osl=osl: h.matmul(ob[:, osl], lhsT=PT[j], rhs=Vb3[:, n, :], start=True, stop=False)), reads=[bPT[j], bVb], writes=[bob])
            s.op("pe", (lambda h, ob=ob, j=j, n=n, osl=osl: h.matmul(ob[:, osl], lhsT=Qw[j][0:64, 0:128], rhs=Sf3[0:64, n, :], start=False, stop=False)), reads=[bQw[j], bSf], writes=[bob])
            s.op("pe", (lambda h, ob=ob, j=j, osl=osl: h.matmul(ob[:, osl], lhsT=Qw[j][0:64, 128:256], rhs=Sbb[0:64, :], start=False, stop=True)), reads=[bQw[j], bSbb], writes=[bob])
            s.op("pool", (lambda h, n=n, j=j: h.tensor_scalar(out=Vw[j], in0=Vb3[:, n, :], scalar1=wend[:, 1:2], scalar2=None, op0=ALU.mult)), reads=[bVb, bwend], writes=[bVw[j]])
            bk_ = banks[2 + j]
            s.op("pe", (lambda h, n=n, j=j, bk_=bk_: h.matmul(bk_[0:64, 0:128], lhsT=Kt3[:, n, :], rhs=Vw[j], start=True, stop=True)), reads=[bKt, bVw[j]], writes=[bbank[2 + j]])
            s.op("dve", (lambda h, bk_=bk_: h.scalar_tensor_tensor(out=Sb[0:64, :], in0=Sb[0:64, :], scalar=gch[0:64, 1:2], in1=bk_[0:64, 0:128], op0=ALU.mult, op1=ALU.add)),
                 reads=[bbank[2 + j], bSb, bSbb, bwend], writes=[bSb])
        W_ = ng * 128
        ob3 = ob[:, 0:W_].rearrange("p (c v) -> p c v", v=128)
        s.op("act", (lambda h, gj=gj, W_=W_: h.activation(out=gs4[gj][:, 0:W_], in_=gs4[gj][:, 0:W_], func=AF.Silu)), reads=[bgs4[gj]], writes=[bgs4[gj]])
        for k in range(ng):
            osl = slice(k * 128, (k + 1) * 128)
            s.op("act", (lambda h, ob=ob, osl=osl, k=k: h.activation(out=sqb[:, osl], in_=ob[:, osl], func=AF.Identity, accum_out=sm4[:, k:k + 1])), reads=[bob], writes=[bsqb, bsm4])
            s.op("act", (lambda h, ob=ob, osl=osl, k=k: h.activation(out=sqb[:, osl], in_=ob[:, osl], func=AF.Square, accum_out=sm4[:, 4 + k:5 + k])), reads=[bob], writes=[bsqb, bsm4])
        s.op("dve", lambda h: h.tensor_scalar(out=sm4[:, 8:16], in0=sm4[:, 0:8], scalar1=1.0 / 128, scalar2=None, op0=ALU.mult), reads=[bsm4], writes=[bsm4])
        s.op("dve", lambda h: h.tensor_tensor(out=sm4[:, 16:20], in0=sm4[:, 8:12], in1=sm4[:, 8:12], op=ALU.mult), reads=[bsm4], writes=[bsm4])
        s.op("dve", lambda h: h.tensor_tensor(out=sm4[:, 20:24], in0=sm4[:, 12:16], in1=sm4[:, 16:20], op=ALU.subtract), reads=[bsm4], writes=[bsm4])
        s.op("dve", lambda h: h.tensor_scalar(out=sm4[:, 20:24], in0=sm4[:, 20:24], scalar1=0.0, scalar2=EPS, op0=ALU.max, op1=ALU.add), reads=[bsm4], writes=[bsm4])
        s.op("act", lambda h: h.activation(out=sm4[:, 20:24], in_=sm4[:, 20:24], func=AF.Sqrt), reads=[bsm4], writes=[bsm4])
        s.op("dve", lambda h: h.reciprocal(out=sm4[:, 24:28], in_=sm4[:, 20:24]), reads=[bsm4], writes=[bsm4])

        def bcl(col, ng=ng):
            return bass.AP(col.tensor, col.offset, [list(col.ap[0]), [col.ap[1][0], ng], [0, 128]])
        y3 = ysts[gj][:, 0:W_].rearrange("p (c v) -> p c v", v=128)
        g3 = gs4[gj][:, 0:W_].rearrange("p (c v) -> p c v", v=128)
        mean_bc = bcl(sm4[:, 8:8 + ng]); rstd_bc = bcl(sm4[:, 24:24 + ng])
        gn_bc = bass.AP(gn.tensor, gn.offset, [list(gn.ap[0]), [0, ng], list(gn.ap[1])])
        s.op("dve", (lambda h, y3=y3, ob3=ob3, mean_bc=mean_bc: h.tensor_tensor(out=y3, in0=ob3, in1=mean_bc, op=ALU.subtract)), reads=[bob, bsm4], writes=[bysts[gj]])
        s.op("pool", (lambda h, y3=y3, rstd_bc=rstd_bc: h.tensor_tensor(out=y3, in0=y3, in1=rstd_bc, op=ALU.mult)), reads=[bsm4, bysts[gj]], writes=[bysts[gj]])
        s.op("pool", (lambda h, y3=y3, gn_bc=gn_bc: h.tensor_tensor(out=y3, in0=y3, in1=gn_bc, op=ALU.mult)), reads=[bgn, bysts[gj]], writes=[bysts[gj]])
        s.op("dve", (lambda h, y3=y3, g3=g3: h.tensor_tensor(out=y3, in0=y3, in1=g3, op=ALU.mult)), reads=[bgs4[gj], bysts[gj]], writes=[bysts[gj]])
        for k, n in enumerate(grp):
            s.op("sp", (lambda h, gj=gj, k=k, n=n: h.dma_start(out=yB_d[:, n, :], in_=ysts[gj][:, k * 128:(k + 1) * 128])), reads=[bysts[gj]], dma_sem=dsem["b%d" % (k % 2)])
    s.barrier()


def ret_consts():
    if "ret" not in _CONST:
        j = np.arange(128)[:, None].astype(np.float64); i = np.arange(128)[None, :].astype(np.float64)
        dmf = np.where(i >= j, i - j, 1e6); dmb = np.where(j > i, j - i, 1e6)
        posf = 127 - j; posb = j + 0 * j
        winf = np.broadcast_to(i + 1, (128, 128)); winb = np.broadcast_to(128 - i, (128, 128))
        cst = np.concatenate([dmf, dmb, posf, posb, winf, winb], 1)
        nf = 16
        inv = 10000.0 ** (-np.arange(nf, dtype=np.float32) / nf)
        t = np.arange(16384)
        row = (t // 64).astype(np.float32); col = (t % 64).astype(np.float32)
        ang = np.concatenate([row[:, None] * inv, col[:, None] * inv], -1).astype(np.float32)
        cos = np.concatenate([np.ones((256, 32), np.float32), np.cos(ang)], 0)
        sin = np.concatenate([np.zeros((256, 32), np.float32), np.sin(ang)], 0)
        cos = cos.reshape(NCH, 128, 32).transpose(1, 0, 2); sin = sin.reshape(NCH, 128, 32).transpose(1, 0, 2)
        _CONST["ret"] = (c_(cst), c_(cos), c_(sin), c_(np.eye(128)))
    return _CONST["ret"]


def tokmajor(a):
    return c_(a.reshape(NCH, 128, -1).transpose(1, 0, 2))


def prep_ret(seq_b, decay_l, gn_l, j):
    cst, cos, sin, ident = ret_consts()
    q = seq_b[:, 768 + 64 * j:768 + 64 * j + 64]; k = seq_b[:, 1024 + 64 * j:1024 + 64 * j + 64]
    v = seq_b[:, 1280 + 128 * j:1280 + 128 * j + 128]; g = seq_b[:, 1792 + 128 * j:1792 + 128 * j + 128]
    rdec = np.broadcast_to(decay_l[:, j][None, :], (128, 2))
    rgn = np.broadcast_to(gn_l[128 * j:128 * j + 128][None, :], (128, 128))
    return {"rq": tokmajor(q), "rk": tokmajor(k), "rv": tokmajor(v), "rg": tokmajor(g), "rcos": cos, "rsin": sin,
            "rdec": c_(rdec), "rgn": c_(rgn), "rcst": cst, "ident": ident}


def unprep_ret(yB):
    t = yB.transpose(1, 0, 2).reshape(NS, 128)
    return t[:256], t[256:]


def build_mixer_s5(P, banks, dsem):
    s = P.s
    P.off = 0
    TB = 1280
    SUB = 320
    NB = NS // TB
    su_d = [P.din("suF", [64, NS]), P.din("suB", [64, NS])]
    sP_d = P.din("sP", [128, 4, 3])
    sB_d = P.din("sB", [64, 2, 128])
    sC_d = P.din("sC", [128, 4, 2, 64])
    sD_d = P.din("sD", [64, 1])
    tpos_d = P.din("tpos", [128, TB])
    ys_d = [P.dout("ysF", [64, NS]), P.dout("ysB", [64, NS])]
    uT = [P.sb(NS, BF16) for _ in range(2)]
    par = P.sb(12, F32); Bf = P.sb(256, F32); Bb = P.sb(256, BF16); Cf = P.sb(512, F32); Cb = P.sb(512, BF16); Dv = P.sb(1, F32); halfpi = P.sb(1, F32)
    tpos = P.sb(TB, F32)
    w = P.sb(64, F32)
    stg = [P.sb(2048, F32) for _ in range(2)]
    ang = P.sb(TB, F32); kk = P.sb(TB, F32); sn = P.sb(TB, F32); cs = P.sb(TB, F32)
    wre = P.sb(TB, F32); wim = P.sb(TB, F32); sre = P.sb(TB, F32); sim = P.sb(TB, F32)
    ta = P.sb(TB, F32); tb = P.sb(TB, F32)
    xre = P.sb(TB, BF16); xim = P.sb(TB, BF16)
    yo = [P.sb(TB, F32) for _ in range(2)]
    last = P.sb(8, F32)
    (bu, bpar, bB, bC, bD, btp, bw, bang, bsn, bcs, bwre, bwim, bsre, bsim, bta, btb, bxre, bxim, blast) = [Buf() for _ in range(19)]
    bstg = [Buf(), Buf()]; byo = [Buf(), Buf()]
    bbank = [Buf() for _ in range(8)]
    s.op("sp", lambda h: h.dma_start(out=par.rearrange("p (u k) -> p u k", k=3), in_=sP_d), writes=[bpar], dma_sem=dsem["m"])
    s.op("sp", lambda h: h.dma_start(out=Bf[0:64, :].rearrange("p (r n) -> p r n", r=2), in_=sB_d), writes=[bB], dma_sem=dsem["m"])
    s.op("sp", lambda h: h.dma_start(out=Cf.rearrange("p (u r n) -> p u r n", u=4, r=2), in_=sC_d), writes=[bC], dma_sem=dsem["m"])
    s.op("sp", lambda h: h.dma_start(out=Dv[0:64, :], in_=sD_d), writes=[bD], dma_sem=dsem["m"])
    s.op("sp", lambda h: h.dma_start(out=tpos, in_=tpos_d), writes=[btp], dma_sem=dsem["m"])
    s.op("dve", lambda h: h.tensor_copy(out=Bb[0:64, :], in_=Bf[0:64, :]), reads=[bB], writes=[bB])
    cnt = 0
    for d in range(2):
        for c0 in range(0, NS, 2048):
            n = min(2048, NS - c0)
            j = cnt % 2; cnt += 1
            s.op("sp", (lambda h, j=j, c0=c0, n=n, d=d: h.dma_start(out=stg[j][0:64, 0:n], in_=su_d[d][:, c0:c0 + n])), writes=[bstg[j]], dma_sem=dsem["a%d" % j])
            s.op("pool", (lambda h, j=j, c0=c0, n=n, d=d: h.tensor_copy(out=uT[d][0:64, c0:c0 + n], in_=stg[j][0:64, 0:n])), reads=[bstg[j]], writes=[bu])
    p3 = par.rearrange("p (u k) -> p u k", k=3)
    are_, aim_, ldt = p3[:, :, 0], p3[:, :, 1], p3[:, :, 2]
    W = lambda i: w[:, 4 * i:4 * i + 4]
    dt, are, lam, mag, th, sn0, cs0, den, nr, fre, fim, t0_, t1_, rden = [W(i) for i in range(14)]
    ops = s.op

    def dv(fn, r=(bpar, bw), wr=(bw,)):
        ops("dve", fn, reads=list(r), writes=list(wr))

    def ac(fn, r=(bpar, bw), wr=(bw,)):
        ops("act", fn, reads=list(r), writes=list(wr))

    def sincos(src, dst_sin, dst_cos, n):
        pass

    dv(lambda h: h.memset(halfpi, 1.5707963267948966))
    ac(lambda h: h.activation(out=dt, in_=ldt, func=AF.Exp))
    dv(lambda h: h.tensor_scalar(out=are, in0=are_, scalar1=-1e-4, scalar2=None, op0=ALU.min))
    dv(lambda h: h.tensor_tensor(out=lam, in0=dt, in1=are, op=ALU.mult))
    ac(lambda h: h.activation(out=mag, in_=lam, func=AF.Exp))
    dv(lambda h: h.tensor_tensor(out=th, in0=dt, in1=aim_, op=ALU.mult))
    dv(lambda h: h.tensor_scalar(out=t0_, in0=th, scalar1=1.0 / TWO_PI, scalar2=MAGIC, op0=ALU.mult, op1=ALU.add))
    dv(lambda h: h.tensor_scalar(out=t0_, in0=t0_, scalar1=MAGIC, scalar2=-TWO_PI, op0=ALU.subtract, op1=ALU.mult))
    dv(lambda h: h.tensor_tensor(out=t1_, in0=t0_, in1=th, op=ALU.add))
    dv(lambda h: h.tensor_scalar(out=t1_, in0=t1_, scalar1=-3.14159, scalar2=3.14159, op0=ALU.max, op1=ALU.min))
    ac(lambda h: h.activation(out=sn0, in_=t1_, func=AF.Sin))
    ac(lambda h: h.activation(out=t0_, in_=t1_, func=AF.Abs))
    ac(lambda h: h.activation(out=cs0, in_=t0_, func=AF.Sin, scale=-1.0, bias=halfpi[:, 0:1]))
    abre, abim = W(14), W(15)
    dv(lambda h: h.tensor_tensor(out=abre, in0=mag, in1=cs0, op=ALU.mult))
    dv(lambda h: h.tensor_tensor(out=abim, in0=mag, in1=sn0, op=ALU.mult))
    dv(lambda h: h.tensor_tensor(out=den, in0=are, in1=are, op=ALU.mult))
    dv(lambda h: h.tensor_tensor(out=t0_, in0=aim_, in1=aim_, op=ALU.mult))
    dv(lambda h: h.tensor_tensor(out=den, in0=den, in1=t0_, op=ALU.add))
    dv(lambda h: h.reciprocal(out=rden, in_=den))
    dv(lambda h: h.tensor_scalar(out=nr, in0=abre, scalar1=-1.0, scalar2=None, op0=ALU.add))
    dv(lambda h: h.tensor_tensor(out=t0_, in0=nr, in1=are, op=ALU.mult))
    dv(lambda h: h.tensor_tensor(out=t1_, in0=abim, in1=aim_, op=ALU.mult))
    dv(lambda h: h.tensor_tensor(out=fre, in0=t0_, in1=t1_, op=ALU.add))
    dv(lambda h: h.tensor_tensor(out=fre, in0=fre, in1=rden, op=ALU.mult))
    dv(lambda h: h.tensor_tensor(out=t0_, in0=abim, in1=are, op=ALU.mult))
    dv(lambda h: h.tensor_tensor(out=t1_, in0=nr, in1=aim_, op=ALU.mult))
    dv(lambda h: h.tensor_tensor(out=fim, in0=t0_, in1=t1_, op=ALU.subtract))
    dv(lambda h: h.tensor_tensor(out=fim, in0=fim, in1=rden, op=ALU.mult))
    C4 = Cf.rearrange("p (u r n) -> p u r n", u=4, r=2)
    Cb4 = Cb.rearrange("p (u r n) -> p u r n", u=4, r=2)
    Ct = ta[:, 0:512].rearrange("p (u r n) -> p u r n", u=4, r=2)

    def bc(col):
        return bass.AP(col.tensor, col.offset, [list(col.ap[0]), list(col.ap[1]), [0, 64]])

    rC = (bC, bw, bta)
    dv(lambda h: h.tensor_tensor(out=Ct[:, :, 0, :], in0=C4[:, :, 0, :], in1=bc(fre), op=ALU.mult), r=rC, wr=(bta,))
    dv(lambda h: h.tensor_tensor(out=Ct[:, :, 1, :], in0=C4[:, :, 1, :], in1=bc(fim), op=ALU.mult), r=rC, wr=(bta,))
    dv(lambda h: h.tensor_tensor(out=Cb4[:, :, 0, :], in0=Ct[:, :, 0, :], in1=Ct[:, :, 1, :], op=ALU.subtract), r=rC, wr=(bC,))
    dv(lambda h: h.tensor_tensor(out=Ct[:, :, 0, :], in0=C4[:, :, 0, :], in1=bc(fim), op=ALU.mult), r=rC, wr=(bta,))
    dv(lambda h: h.tensor_tensor(out=Ct[:, :, 1, :], in0=C4[:, :, 1, :], in1=bc(fre), op=ALU.mult), r=rC, wr=(bta,))
    dv(lambda h: h.tensor_tensor(out=Ct[:, :, 0, :], in0=Ct[:, :, 0, :], in1=Ct[:, :, 1, :], op=ALU.add), r=rC, wr=(bta,))
    dv(lambda h: h.tensor_scalar(out=Cb4[:, :, 1, :], in0=Ct[:, :, 0, :], scalar1=-1.0, scalar2=None, op0=ALU.mult), r=rC, wr=(bC,))
    B3 = Bb[0:64, :].rearrange("p (r n) -> p r n", r=2)
    for u in range(4):
        q, d = u // 2, u % 2
        thu = th[:, u:u + 1]; magu = mag[:, u:u + 1]
        for blk in range(NB):
            t0 = blk * TB
            dv(lambda h, t0=t0, thu=thu: h.tensor_scalar(out=ang, in0=tpos, scalar1=float(t0), scalar2=thu, op0=ALU.add, op1=ALU.mult), r=(btp, bw, bang), wr=(bang,))
            dv(lambda h: h.tensor_scalar(out=kk, in0=ang, scalar1=1.0 / TWO_PI, scalar2=MAGIC, op0=ALU.mult, op1=ALU.add), r=(bang,), wr=(bang,))
            dv(lambda h: h.tensor_scalar(out=kk, in0=kk, scalar1=MAGIC, scalar2=-TWO_PI, op0=ALU.subtract, op1=ALU.mult), r=(bang,), wr=(bang,))
            dv(lambda h: h.tensor_tensor(out=ang, in0=ang, in1=kk, op=ALU.add), r=(bang,), wr=(bang,))
            dv(lambda h: h.tensor_scalar(out=ang, in0=ang, scalar1=-3.14159, scalar2=3.14159, op0=ALU.max, op1=ALU.min), r=(bang,), wr=(bang,))
            ac(lambda h: h.activation(out=sn, in_=ang, func=AF.Sin), r=(bang,), wr=(bsn,))
            ac(lambda h: h.activation(out=kk, in_=ang, func=AF.Abs), r=(bang,), wr=(bang,))
            ac(lambda h: h.activation(out=cs, in_=kk, func=AF.Sin, scale=-1.0, bias=halfpi[:, 0:1]), r=(bang,), wr=(bcs,))
            for sb_ in range(TB // SUB):
                c0 = t0 + sb_ * SUB
                lo = sb_ * SUB
                jb = sb_ % 2
                pre, pim = banks[jb * 2], banks[jb * 2 + 1]
                ops("pe", (lambda h, pre=pre, q=q, d=d, c0=c0: h.matmul(pre[:, 0:SUB], lhsT=B3[32 * q:32 * q + 32, 0, :], rhs=uT[d][32 * q:32 * q + 32, c0:c0 + SUB], start=True, stop=True)),
                    reads=[bB, bu], writes=[bbank[jb * 2]])
                ops("pe", (lambda h, pim=pim, q=q, d=d, c0=c0: h.matmul(pim[:, 0:SUB], lhsT=B3[32 * q:32 * q + 32, 1, :], rhs=uT[d][32 * q:32 * q + 32, c0:c0 + SUB], start=True, stop=True)),
                    reads=[bB, bu], writes=[bbank[jb * 2 + 1]])
                sl = slice(lo, lo + SUB)
                dv(lambda h, pre=pre, sl=sl: h.tensor_tensor(out=ta[:, sl], in0=pre[:, 0:SUB], in1=cs[:, sl], op=ALU.mult), r=(bbank[jb * 2], bcs), wr=(bta,))
                dv(lambda h, pim=pim, sl=sl: h.tensor_tensor(out=tb[:, sl], in0=pim[:, 0:SUB], in1=sn[:, sl], op=ALU.mult), r=(bbank[jb * 2 + 1], bsn), wr=(btb,))
                ops("pool", (lambda h, sl=sl: h.tensor_tensor(out=wre[:, sl], in0=ta[:, sl], in1=tb[:, sl], op=ALU.add)), reads=[bta, btb], writes=[bwre])
                dv(lambda h, pim=pim, sl=sl: h.tensor_tensor(out=sre[:, sl], in0=pim[:, 0:SUB], in1=cs[:, sl], op=ALU.mult), r=(bbank[jb * 2 + 1], bcs), wr=(bsre,))
                dv(lambda h, pre=pre, sl=sl: h.tensor_tensor(out=sim[:, sl], in0=pre[:, 0:SUB], in1=sn[:, sl], op=ALU.mult), r=(bbank[jb * 2], bsn), wr=(bsim,))
                ops("pool", (lambda h, sl=sl: h.tensor_tensor(out=wim[:, sl], in0=sre[:, sl], in1=sim[:, sl], op=ALU.subtract)), reads=[bsre, bsim], writes=[bwim])
            ini_re = 0.0 if blk == 0 else last[:, 0:1]
            ini_im = 0.0 if blk == 0 else last[:, 1:2]
            dv(lambda h, ini_re=ini_re, magu=magu: h.tensor_tensor_scan(out=sre, data0=bcast_free(magu, TB), data1=wre, initial=ini_re, op0=ALU.mult, op1=ALU.add), r=(bwre, bw, blast, bsre), wr=(bsre,))
            dv(lambda h, ini_im=ini_im, magu=magu: h.tensor_tensor_scan(out=sim, data0=bcast_free(magu, TB), data1=wim, initial=ini_im, op0=ALU.mult, op1=ALU.add), r=(bwim, bw, blast, bsim), wr=(bsim,))
            ops("pool", lambda h: h.tensor_copy(out=last[:, 0:1], in_=sre[:, TB - 1:TB]), reads=[bsre], writes=[blast])
            ops("pool", lambda h: h.tensor_copy(out=last[:, 1:2], in_=sim[:, TB - 1:TB]), reads=[bsim], writes=[blast])
            ops("pool", lambda h: h.tensor_tensor(out=ta, in0=sre, in1=cs, op=ALU.mult), reads=[bsre, bcs], writes=[bta])
            dv(lambda h: h.tensor_tensor(out=tb, in0=sim, in1=sn, op=ALU.mult), r=(bsim, bsn), wr=(btb,))
            dv(lambda h: h.tensor_tensor(out=xre, in0=ta, in1=tb, op=ALU.subtract), r=(bta, btb), wr=(bxre,))
            ops("pool", lambda h: h.tensor_tensor(out=ta, in0=sre, in1=sn, op=ALU.mult), reads=[bsre, bsn, bxre], writes=[bta])
            dv(lambda h: h.tensor_tensor(out=tb, in0=sim, in1=cs, op=ALU.mult), r=(bsim, bcs, bxre), wr=(btb,))
            dv(lambda h: h.tensor_tensor(out=xim, in0=ta, in1=tb, op=ALU.add), r=(bta, btb), wr=(bxim,))
            yj = blk % 2
            for sb_ in range(TB // SUB):
                lo = sb_ * SUB
                c0 = t0 + lo
                pb = banks[4 + sb_ % 2]; bpb = bbank[4 + sb_ % 2]
                ops("pe", (lambda h, pb=pb, u=u, lo=lo: h.matmul(pb[0:64, 0:SUB], lhsT=Cb4[:, u, 0, :], rhs=xre[:, lo:lo + SUB], start=True, stop=False)), reads=[bC, bxre], writes=[bpb])
                ops("pe", (lambda h, pb=pb, u=u, lo=lo: h.matmul(pb[0:64, 0:SUB], lhsT=Cb4[:, u, 1, :], rhs=xim[:, lo:lo + SUB], start=False, stop=True)), reads=[bC, bxim], writes=[bpb])
                if d == 0:
                    dv(lambda h, pb=pb, lo=lo, yj=yj, q=q, c0=c0: h.scalar_tensor_tensor(out=yo[yj][32 * q:32 * q + 32, lo:lo + SUB], in0=uT[0][32 * q:32 * q + 32, c0:c0 + SUB],
                                                                                        scalar=Dv[32 * q:32 * q + 32, 0:1], in1=pb[32 * q:32 * q + 32, 0:SUB], op0=ALU.mult, op1=ALU.add),
                       r=(bpb, bu, bD, byo[yj]), wr=(byo[yj],))
                else:
                    dv(lambda h, pb=pb, lo=lo, yj=yj, q=q: h.tensor_copy(out=yo[yj][32 * q:32 * q + 32, lo:lo + SUB], in_=pb[32 * q:32 * q + 32, 0:SUB]), r=(bpb, byo[yj]), wr=(byo[yj],))
            ops("sp", (lambda h, yj=yj, q=q, d=d, t0=t0: h.dma_start(out=ys_d[d][32 * q:32 * q + 32, t0:t0 + TB], in_=yo[yj][32 * q:32 * q + 32, :])), reads=[byo[yj]], dma_sem=dsem["o%d" % yj])
    s.barrier()


def s5_consts():
    if "s5" not in _CONST:
        _CONST["s5"] = c_(np.broadcast_to(np.arange(1280, dtype=np.float32)[None, :], (128, 1280)))
    return _CONST["s5"]


def prep_s5(seq_b, a_re, a_im, log_dt, b_re, b_im, c_re, c_im, dvec, j):
    u = seq_b[:, 2304 + 64 * j:2304 + 64 * j + 64]
    uF = u.T
    uB = np.concatenate([u[:256][::-1], u[256:][::-1]], 0).T
    sP = np.zeros((128, 4, 3), np.float32)
    sB = np.zeros((64, 2, 128), np.float32)
    sC = np.zeros((128, 4, 2, 64), np.float32)
    for q in range(2):
        for g2 in range(2):
            g = 4 * j + 2 * q + g2
            rows = slice(g2 * 64, g2 * 64 + 64)
            for d in range(2):
                un = q * 2 + d
                sP[rows, un, 0] = a_re[d, g]; sP[rows, un, 1] = a_im[d, g]; sP[rows, un, 2] = log_dt[d, g]
                sC[rows, un, 0, 32 * q + 16 * g2:32 * q + 16 * g2 + 16] = c_re[d, g].T
                sC[rows, un, 1, 32 * q + 16 * g2:32 * q + 16 * g2 + 16] = c_im[d, g].T
            sB[32 * q + 16 * g2:32 * q + 16 * g2 + 16, 0, rows] = b_re[g].T
            sB[32 * q + 16 * g2:32 * q + 16 * g2 + 16, 1, rows] = b_im[g].T
    sD = dvec[64 * j:64 * j + 64][:, None]
    return {"suF": c_(uF), "suB": c_(uB), "sP": sP, "sB": sB, "sC": sC, "sD": c_(sD), "tpos": s5_consts()}


def unprep_s5(ysF, ysB):
    f = ysF.T
    bproc = ysB.T
    b = np.concatenate([bproc[:256][::-1], bproc[256:][::-1]], 0)
    return f, b


def _modT(mm, b, i_scale, i_shift):
    v = np.stack([mm[b, i_scale], mm[b, i_shift], mm[2, i_scale], mm[2, i_shift]], -1)
    return c_(v.reshape(8, 128, 4).transpose(1, 0, 2))


def _gbc(mm, b, i):
    return c_(np.stack([np.broadcast_to(mm[b, i], (128, D)), np.broadcast_to(mm[2, i], (128, D))], 0))


def mod_all(c, c_ctx, w_mod, b_mod):
    rows = np.stack([c[0], c[1], c_ctx], 0)
    cT = c_(rows.T.reshape(8, 128, 3).transpose(1, 0, 2))
    maps = []
    for i in range(NCORE):
        l, part = i // 2, i % 2
        sl = slice(part * 4608, (part + 1) * 4608)
        maps.append({"cT": cT, "w": c_(w_mod[l][:, sl]), "b": c_(np.broadcast_to(b_mod[l][sl], (3, 4608)))})
    res = run(get_prog("mod", build_mod_prog), maps)
    m = np.stack([np.concatenate([res[2 * l]["m"], res[2 * l + 1]["m"]], axis=1) for l in range(4)], 1)
    return m.reshape(3, 4, 9, D)


def kernel(x, c, ctx, c_ctx, w_mod, b_mod, ffn1_w_in, ffn1_w_out, w_in, w_out,
           na_q_gain, na_k_gain, na_rpb, ret_decay, ret_gn,
           s5_a_re, s5_a_im, s5_log_dt, s5_b_re, s5_b_im, s5_c_re, s5_c_im, s5_d, s5_w_glu,
           ffn2_w_in, ffn2_w_out):
    f = lambda a: np.asarray(a, dtype=np.float32)
    x, c, ctx, c_ctx = f(x), f(c), f(ctx), f(c_ctx)
    mall = mod_all(c, c_ctx, f(w_mod), f(b_mod))
    ident = c_(np.eye(128))
    X = []
    for core in range(NCORE):
        b, sg = core // 4, core % 4
        xt = np.zeros((TOK, D), np.float32)
        xt[:4096] = x[b, sg * 4096:(sg + 1) * 4096]
        xt[4096:4160] = ctx[b, sg * 64:(sg + 1) * 64]
        X.append(xt)
    mix = None
    for l in range(5):
        passes = []
        maps = [{"x": X[core], "ident": ident} for core in range(NCORE)]
        if l > 0:
            mm = mall[:, l - 1]
            passes += [{"kind": "outproj"}, {"kind": "ffn"}]
            for core in range(NCORE):
                b, sg = core // 4, core % 4
                mp = maps[core]
                yab, ysf, ysb = mix[b]
                def cut(a):
                    o = np.zeros((a.shape[1], TOK), np.float32)
                    o[:, :4096] = a[256 + sg * 4096:256 + (sg + 1) * 4096].T
                    o[:, 4096:4160] = a[sg * 64:(sg + 1) * 64].T
                    return o
                mp["yab"] = cut(yab); mp["ysf"] = cut(ysf); mp["ysb"] = cut(ysb)
                mp["wg"] = f(s5_w_glu[l - 1]); mp["wo"] = f(w_out[l - 1]); mp["gbc_0"] = _gbc(mm, b, 5)
                mp["w1_1"] = f(ffn2_w_in[l - 1]); mp["w2_1"] = f(ffn2_w_out[l - 1]); mp["mod_1"] = _modT(mm, b, 7, 6); mp["gbc_1"] = _gbc(mm, b, 8)
        if l < 4:
            mm = mall[:, l]
            p0 = len(passes)
            passes += [{"kind": "ffn"}, {"kind": "inproj"}]
            for core in range(NCORE):
                b, sg = core // 4, core % 4
                mp = maps[core]
                mp["w1_%d" % p0] = f(ffn1_w_in[l]); mp["w2_%d" % p0] = f(ffn1_w_out[l]); mp["mod_%d" % p0] = _modT(mm, b, 1, 0); mp["gbc_%d" % p0] = _gbc(mm, b, 2)
                mp["wi_%d" % (p0 + 1)] = f(w_in[l]); mp["mod_%d" % (p0 + 1)] = _modT(mm, b, 4, 3)
        key = "tok_" + "_".join(p["kind"] for p in passes)
        res = run(get_prog(key, lambda: build_token_prog(passes)), maps)
        X = [res[core]["xo"] for core in range(NCORE)]
        if l == 4:
            break
        seqs = []
        for b in range(2):
            lat = np.concatenate([res[b * 4 + sg]["proj"][:4096] for sg in range(4)], 0)
            cx = np.concatenate([res[b * 4 + sg]["proj"][4096:4160] for sg in range(4)], 0)
            seqs.append(np.concatenate([cx, lat], 0))
        maps = []
        for core in range(NCORE):
            b, j = core // 4, core % 4
            mp = {}
            mp.update(prep_na(seqs[b], f(na_rpb[l]), f(na_q_gain[l]), f(na_k_gain[l]), j))
            mp.update(prep_ret(seqs[b], f(ret_decay[l]), f(ret_gn[l]), j))
            mp.update(prep_s5(seqs[b], f(s5_a_re[l]), f(s5_a_im[l]), f(s5_log_dt[l]), f(s5_b_re[l]), f(s5_b_im[l]), f(s5_c_re[l]), f(s5_c_im[l]), f(s5_d[l]), j))
            maps.append(mp)
        res = run(get_prog("mix", build_mixer_prog), maps)
        mix = []
        for b in range(2):
            yab = np.zeros((NS, 768), np.float32); ysf = np.zeros((NS, 256), np.float32); ysb = np.zeros((NS, 256), np.float32)
            for j in range(4):
                r = res[b * 4 + j]
                ca, la = unprep_na(r["yA"])
                yab[:256, 64 * j:64 * j + 64] = ca; yab[256:, 64 * j:64 * j + 64] = la
                cb, lb = unprep_ret(r["yB"])
                yab[:256, 256 + 128 * j:256 + 128 * j + 128] = cb; yab[256:, 256 + 128 * j:256 + 128 * j + 128] = lb
                sf, sb_ = unprep_s5(r["ysF"], r["ysB"])
                ysf[:, 64 * j:64 * j + 64] = sf; ysb[:, 64 * j:64 * j + 64] = sb_
            mix.append((yab, ysf, ysb))
    out = np.zeros((2, 16384, D), np.float32)
    for core in range(NCORE):
        b, sg = core // 4, core % 4
        out[b, sg * 4096:(sg + 1) * 4096] = X[core][:4096]
    return out
```

```python
import numpy as np
import concourse.bass as bass
import concourse.mybir as mybir

F32 = mybir.dt.float32
BF16 = mybir.dt.bfloat16
ALU = mybir.AluOpType
AF = mybir.ActivationFunctionType

ENGS = ("pe", "act", "dve", "pool", "sp")


class Buf:
    __slots__ = ("name", "writer", "readers")

    def __init__(self, name=""):
        self.name = name
        self.writer = None
        self.readers = []


class Sched:
    def __init__(self, nc, same_engine_sync=True):
        self.nc = nc
        self.streams = {e: [] for e in ENGS}
        self.count = {}
        self.waited = {e: {} for e in ENGS}
        self.semkeys = list(ENGS)
        self.same = same_engine_sync
        self.ndma = 0

    def new_dma_sem(self, name):
        key = "dma_" + name
        assert key not in self.semkeys
        self.semkeys.append(key)
        return key

    def _deps(self, eng, reads, writes):
        deps = []
        for b in reads:
            if b.writer is not None:
                deps.append(b.writer)
        for b in writes:
            if b.writer is not None:
                deps.append(b.writer)
            deps.extend(b.readers)
        need = {}
        for (k, v) in deps:
            if (not self.same) and k == eng:
                continue
            if k == "pe" and eng == "pe":
                continue
            if need.get(k, 0) < v:
                need[k] = v
        out = []
        for k, v in need.items():
            if self.waited[eng].get(k, 0) < v:
                self.waited[eng][k] = v
                out.append((k, v))
        return out

    def op(self, eng, fn, reads=(), writes=(), dma_sem=None):
        waits = self._deps(eng, reads, writes)
        if dma_sem is not None:
            key = dma_sem
            inc = 16
            prev = self.count.get(key, 0)
            if prev > 0 and self.waited[eng].get(key, 0) < prev:
                self.waited[eng][key] = prev
                waits = [w for w in waits if w[0] != key] + [(key, prev)]
        else:
            key = eng
            inc = 1
        self.count[key] = self.count.get(key, 0) + inc
        tok = (key, self.count[key])
        self.streams[eng].append((waits, fn, key, inc))
        for b in writes:
            b.writer = tok
            b.readers = []
        for b in reads:
            if b not in writes:
                b.readers.append(tok)
        return tok

    def barrier(self):
        snap = dict(self.count)
        for e in ENGS:
            waits = []
            for k, v in snap.items():
                if v > 0 and self.waited[e].get(k, 0) < v:
                    self.waited[e][k] = v
                    waits.append((k, v))
            if waits:
                self.streams[e].append((waits, None, None, 0))

    def emit(self, final_waits=True):
        nc = self.nc
        sems = {}
        ctxs = []
        for k in self.semkeys:
            if self.count.get(k, 0) > 0 or k in ENGS:
                c = nc.semaphore("s_" + k)
                sems[k] = c.__enter__()
                ctxs.append(c)
        blk_ctx = nc.Block()
        block = blk_ctx.__enter__()
        streams = self.streams
        counts = self.count

        def run(engname, h):
            for (waits, fn, key, inc) in streams[engname]:
                for (k, v) in waits:
                    h.wait_ge(sems[k], v)
                if fn is None:
                    continue
                inst = fn(h)
                inst.then_inc(sems[key], inc)
            if engname == "sp" and final_waits:
                for k, v in counts.items():
                    if v > 0:
                        h.wait_ge(sems[k], v)

        @block.tensor
        def _(h):
            run("pe", h)

        @block.scalar
        def _(h):
            run("act", h)

        @block.vector
        def _(h):
            run("dve", h)

        @block.gpsimd
        def _(h):
            run("pool", h)

        @block.sync
        def _(h):
            run("sp", h)

        blk_ctx.__exit__(None, None, None)
        for c in reversed(ctxs):
            c.__exit__(None, None, None)

from concourse.bass_utils import run_bass_kernel_spmd

D = 1024
DFF = 2816
EPS = 1e-6
NCORE = 8


class Prog:
    def __init__(self):
        self.nc = bass.Bass("TRN2", target_bir_lowering=False)
        self.s = Sched(self.nc)
        self._big_c = self.nc.sbuf_tensor("big", [128, 96000], BF16)
        self.big = self._big_c.__enter__()
        self.off = 0
        self._ps = []
        self.nps = 0

    def sb(self, n, dt, shape=None):
        sz = 4 if dt == F32 else 2
        self.off = (self.off + 31) // 32 * 32
        o = self.off
        self.off += n * sz
        assert self.off <= 96000 * 2, ("sbuf overflow", self.off)
        return self.big[:, o // 2: o // 2 + n * sz // 2].bitcast(dt)

    def ps(self, dt=F32):
        n = 512 if dt == F32 else 1024
        c = self.nc.psum_tensor("ps%d" % self.nps, [128, n], dt)
        self.nps += 1
        t = c.__enter__()
        self._ps.append(c)
        return t[:, :]

    def din(self, name, shape, dt=F32):
        return self.nc.dram_tensor(name, list(shape), dt, kind="ExternalInput").ap()

    def dout(self, name, shape, dt=F32):
        return self.nc.dram_tensor(name, list(shape), dt, kind="ExternalOutput").ap()

    def finish(self):
        self.s.emit()
        for c in reversed(self._ps):
            c.__exit__(None, None, None)
        self._big_c.__exit__(None, None, None)
        return self.nc


def bcast_free(ap1, n):
    return bass.AP(ap1.tensor, ap1.offset, [list(ap1.ap[0]), [0, n]])


TOK = 4224
TILES = [(i * 256, 2, 0) for i in range(16)] + [(4096, 1, 1)]
C_GELU = 0.044715
K_GELU = 1.5957691216057308


def build_token_prog(passes):
    P = Prog()
    s = P.s
    x_in = P.din("x", [TOK, D])
    x_out = P.dout("xo", [TOK, D])
    x_scr = P.nc.dram_tensor("xscr", [TOK, D], F32, kind="Internal").ap()
    upd = [i for i, p in enumerate(passes) if p["kind"] in ("ffn", "outproj")]
    last_upd = upd[-1] if upd else -1
    ident_d = P.din("ident", [128, 128])
    w1s = P.sb(8 * 5632, BF16)
    w2s = P.sb(22 * 1024, BF16)
    gT = P.sb(22 * 256, BF16)
    xts = [P.sb(1024, F32) for _ in range(4)]
    hTs = [P.sb(8 * 256, BF16) for _ in range(2)]
    xn = P.sb(1024, BF16)
    sgs = [P.sb(256, F32) for _ in range(2)]
    tmp = P.sb(1024, F32)
    gbc = P.sb(2048, F32)
    modt = P.sb(64, F32)
    small = P.sb(16, F32)
    ident = P.sb(128, BF16)
    identf = P.sb(128, F32)
    tps = [P.ps(BF16) for _ in range(2)]
    abs_ = [P.ps(F32) for _ in range(2)]
    obs = [(P.ps(F32), P.ps(F32)) for _ in range(2)]

    dsem = {n: s.new_dma_sem(n) for n in ["w0", "w1", "w2", "w3", "w4", "w5", "x0", "x1", "x2", "x3", "misc", "st", "st2", "y0", "y1"]}
    b_ident = Buf()
    s.op("sp", lambda h: h.dma_start(out=identf, in_=ident_d), writes=[b_ident], dma_sem=dsem["misc"])
    s.op("dve", lambda h: h.tensor_copy(out=ident, in_=identf), reads=[b_ident], writes=[b_ident])

    stage = [gT[:, k * 1408:(k + 1) * 1408].bitcast(F32) for k in range(4)] + [xts[2][:, 0:704], xts[3][:, 0:704]]
    NSTG = 6
    xbufs = [Buf() for _ in range(4)]
    xtile_bufs = {}

    def xdram_buf(t):
        if t not in xtile_bufs:
            xtile_bufs[t] = Buf()
        return xtile_bufs[t]

    cast_rr = [0]

    def load_w(dst3, dram, K, N, bstage, bdst):
        nch = (N + 703) // 704
        cw = N // nch
        assert cw * nch == N
        for kt in range(K):
            for c in range(nch):
                j = cast_rr[0] % NSTG
                cast_rr[0] += 1
                st = stage[j][:, 0:cw]
                src = dram[kt * 128:(kt + 1) * 128, c * cw:(c + 1) * cw]
                dst = dst3[:, kt, c * cw:(c + 1) * cw]
                s.op("sp", (lambda h, st=st, src=src: h.dma_start(out=st, in_=src)), writes=[bstage[j]], dma_sem=dsem["w%d" % j])
                eng = ("pool", "act", "dve")[cast_rr[0] % 3]
                if eng == "pool":
                    s.op("pool", (lambda h, st=st, dst=dst: h.tensor_copy(out=dst, in_=st)), reads=[bstage[j]], writes=[bdst])
                elif eng == "act":
                    s.op("act", (lambda h, st=st, dst=dst: h.copy(out=dst, in_=st)), reads=[bstage[j]], writes=[bdst])
                else:
                    s.op("dve", (lambda h, st=st, dst=dst: h.tensor_copy(out=dst, in_=st)), reads=[bstage[j]], writes=[bdst])

    def front_end(xt, bxt, hT, bhT, sub, cls, sc1, sh, bmod, tp, btp):
        ss = small[:, 0:1]
        rstd = small[:, 1:2]
        bss = b_ss[0]
        s.op("act", lambda h: h.activation(out=tmp[:, 0:512].bitcast(BF16), in_=xt, func=AF.Square, accum_out=ss), reads=[bxt], writes=[bss, b_tmp[0]])
        s.op("dve", lambda h: h.tensor_scalar(out=rstd, in0=ss, scalar1=1.0 / D, scalar2=EPS, op0=ALU.mult, op1=ALU.add), reads=[bss], writes=[bss])
        s.op("act", lambda h: h.activation(out=rstd, in_=rstd, func=AF.Sqrt), reads=[bss], writes=[bss])
        s.op("dve", lambda h: h.reciprocal(out=rstd, in_=rstd), reads=[bss], writes=[bss])
        s.op("dve", lambda h: h.tensor_scalar(out=xn, in0=xt, scalar1=rstd, scalar2=None, op0=ALU.mult), reads=[bss, bxt], writes=[b_xn[0]])
        for kt in range(8):
            s.op("pe", (lambda h, kt=kt: h.transpose(out=tp[:, kt * 128:(kt + 1) * 128], in_=xn[:, kt * 128:(kt + 1) * 128], identity=ident)),
                 reads=[b_xn[0], b_ident], writes=[btp])
        hv = hT.rearrange("p (k t) -> p k t", k=8)
        for kt in range(8):
            s.op("act", (lambda h, kt=kt: h.activation(out=hv[:, kt, sub * 128:(sub + 1) * 128], in_=tp[:, kt * 128:(kt + 1) * 128],
                                                      func=AF.Identity, scale=sc1[:, kt, cls:cls + 1], bias=sh[:, kt, cls:cls + 1])),
                 reads=[btp, bmod], writes=[bhT])

    b_tmp = [Buf()]
    b_xn = [Buf()]
    b_tp = [Buf(), Buf()]
    b_ab = [Buf(), Buf()]
    b_ob = [Buf(), Buf()]
    b_hT = [Buf(), Buf()]
    b_sg = [Buf(), Buf()]
    b_ss = [Buf()]
    b_ystg = [Buf(), Buf()]
    b_y = [Buf(), Buf(), Buf(), Buf()]

    def load_mod(mod_d, gbc_d):
        bmod = Buf()
        bg = Buf()
        mv = modt[:, 0:32].rearrange("p (k v) -> p k v", k=8)
        s.op("sp", lambda h: h.dma_start(out=mv, in_=mod_d), writes=[bmod], dma_sem=dsem["misc"])
        sc1 = modt[:, 32:48].rearrange("p (k c) -> p k c", k=8)
        sh = modt[:, 48:64].rearrange("p (k c) -> p k c", k=8)
        mv2 = modt[:, 0:32].rearrange("p (k c v) -> p k c v", k=8, c=2)
        s.op("dve", lambda h: h.tensor_scalar(out=sc1, in0=mv2[:, :, :, 0], scalar1=1.0, scalar2=None, op0=ALU.add), reads=[bmod], writes=[bmod])
        s.op("dve", lambda h: h.tensor_copy(out=sh, in_=mv2[:, :, :, 1]), reads=[bmod], writes=[bmod])
        if gbc_d is not None:
            gv = gbc.rearrange("p (c f) -> p c f", c=2)
            s.op("sp", lambda h: h.dma_start(out=gv, in_=gbc_d.rearrange("c p f -> p c f")), writes=[bg], dma_sem=dsem["misc"])
        return sc1, sh, bmod, bg

    first = [True]

    def xsrc():
        return x_in if first[0] else x_scr

    for pi, pk in enumerate(passes):
        kind = pk["kind"]
        s.barrier()
        bst = [Buf() for _ in range(6)]
        bw1, bw2 = Buf(), Buf()
        w1v = w1s.rearrange("p (k n) -> p k n", k=8)
        w2v = w2s.rearrange("p (k n) -> p k n", k=22)
        if kind == "ffn":
            w1_d = P.din("w1_%d" % pi, [D, 2 * DFF])
            w2_d = P.din("w2_%d" % pi, [DFF, D])
            mod_d = P.din("mod_%d" % pi, [128, 8, 4])
            gbc_d = P.din("gbc_%d" % pi, [2, 128, D])
            sc1, sh, bmod, bg = load_mod(mod_d, gbc_d)
            gscale = 0.5
            s.op("pool", lambda h: h.tensor_scalar(out=gbc, in0=gbc, scalar1=0.5, scalar2=None, op0=ALU.mult), reads=[bg], writes=[bg])
            load_w(w1v, w1_d, 8, 2 * DFF, bst, bw1)
            load_w(w2v, w2_d, 22, D, bst, bw2)
            s.barrier()
            bgT = Buf()
            gTv = gT.rearrange("p (f t) -> p f t", f=22)
            src = xsrc()
            for ti, (t0, nsub, cls) in enumerate(TILES):
                TT = nsub * 128
                hT = hTs[ti % 2]
                bhT = b_hT[ti % 2]
                subx = []
                for sub in range(nsub):
                    xi = (ti * 2 + sub) % 4
                    xt, bxt = xts[xi], xbufs[xi]
                    r0 = t0 + sub * 128
                    s.op("sp", (lambda h, xt=xt, r0=r0, src=src: h.dma_start(out=xt, in_=src[r0:r0 + 128, :])),
                         reads=[xdram_buf(r0)], writes=[bxt], dma_sem=dsem["x%d" % xi])
                    tp = tps[(ti * 2 + sub) % 2]
                    front_end(xt, bxt, hT, bhT, sub, cls, sc1, sh, bmod, tp, b_tp[(ti * 2 + sub) % 2])
                    subx.append((xt, bxt, r0))
                hv = hT.rearrange("p (k t) -> p k t", k=8)
                for f in range(22):
                    ab = abs_[f % 2]
                    bab = b_ab[f % 2]
                    for half in range(2):
                        for kt in range(8):
                            s.op("pe", (lambda h, ab=ab, half=half, kt=kt, f=f, hv=hv, TT=TT: h.matmul(
                                ab[:, half * 256: half * 256 + TT], lhsT=w1v[:, kt, half * DFF + f * 128: half * DFF + (f + 1) * 128],
                                rhs=hv[:, kt, 0:TT], start=(kt == 0), stop=(kt == 7))), reads=[bw1, bhT], writes=[bab])
                    sg = sgs[f % 2]
                    bsg = b_sg[f % 2]
                    s.op("act", (lambda h, sg=sg, ab=ab, TT=TT: h.activation(out=sg[:, 0:TT], in_=ab[:, 0:TT], func=AF.Silu)), reads=[bab], writes=[bsg])
                    s.op("dve", (lambda h, sg=sg, ab=ab, TT=TT, f=f: h.tensor_tensor(out=gTv[:, f, 0:TT], in0=sg[:, 0:TT], in1=ab[:, 256:256 + TT], op=ALU.mult)),
                         reads=[bsg, bab], writes=[bgT])
                for sub in range(nsub):
                    xt, bxt, r0 = subx[sub]
                    ob = obs[sub % 2]
                    bo = b_ob[sub % 2]
                    for half in range(2):
                        for f in range(22):
                            s.op("pe", (lambda h, ob=ob, half=half, f=f, sub=sub: h.matmul(
                                ob[half][:, 0:512], lhsT=gTv[:, f, sub * 128:(sub + 1) * 128], rhs=w2v[:, f, half * 512:(half + 1) * 512],
                                start=(f == 0), stop=(f == 21))), reads=[bgT, bw2], writes=[bo])
                    for half in range(2):
                        s.op("dve", (lambda h, ob=ob, half=half, cls=cls: h.tensor_tensor(
                            out=tmp[:, half * 512:(half + 1) * 512], in0=ob[half][:, 0:512], in1=gbc[:, cls * 1024 + half * 512: cls * 1024 + (half + 1) * 512], op=ALU.mult)),
                            reads=[bo, bg], writes=[b_tmp[0]])
                    s.op("dve", (lambda h, xt=xt: h.tensor_tensor(out=xt, in0=xt, in1=tmp, op=ALU.add)), reads=[b_tmp[0], bxt], writes=[bxt])
                    s.op("sp", (lambda h, xt=xt, r0=r0: h.dma_start(out=x_scr[r0:r0 + 128, :], in_=xt)), reads=[bxt], writes=[xdram_buf(r0)], dma_sem=dsem["st"])
                    if pi == last_upd:
                        s.op("sp", (lambda h, xt=xt, r0=r0: h.dma_start(out=x_out[r0:r0 + 128, :], in_=xt)), reads=[bxt], dma_sem=dsem["st2"])
            first[0] = False
        elif kind == "inproj":
            wi_d = P.din("wi_%d" % pi, [D, 2560])
            mod_d = P.din("mod_%d" % pi, [128, 8, 4])
            proj_d = P.dout("proj", [TOK, 2560])
            sc1, sh, bmod, bg = load_mod(mod_d, None)
            load_w(w1v, wi_d, 8, 2560, bst, bw1)
            s.barrier()
            prs = [w2s[:, 0:5120].bitcast(F32), w2s[:, 5120:10240].bitcast(F32)]
            bprs = [Buf(), Buf()]
            src = xsrc()
            n128 = TOK // 128
            for ti in range(n128):
                cls = 1 if ti == n128 - 1 else 0
                xi = ti % 4
                xt, bxt = xts[xi], xbufs[xi]
                r0 = ti * 128
                s.op("sp", (lambda h, xt=xt, r0=r0, src=src: h.dma_start(out=xt, in_=src[r0:r0 + 128, :])),
                     reads=[xdram_buf(r0)], writes=[bxt], dma_sem=dsem["x%d" % xi])
                hT = hTs[ti % 2]
                bhT = b_hT[ti % 2]
                front_end(xt, bxt, hT, bhT, 0, cls, sc1, sh, bmod, tps[ti % 2], b_tp[ti % 2])
                hv = hT.rearrange("p (k t) -> p k t", k=8)
                pr, bpr = prs[ti % 2], bprs[ti % 2]
                for cb in range(5):
                    ob = obs[cb % 2][0]
                    bo = b_ob[cb % 2]
                    for kt in range(8):
                        s.op("pe", (lambda h, ob=ob, kt=kt, cb=cb, hv=hv: h.matmul(ob[:, 0:512], lhsT=hv[:, kt, 0:128], rhs=w1v[:, kt, cb * 512:(cb + 1) * 512],
                                                                                 start=(kt == 0), stop=(kt == 7))), reads=[bhT, bw1], writes=[bo])
                    if cb % 2 == 0:
                        s.op("act", (lambda h, ob=ob, pr=pr, cb=cb: h.copy(out=pr[:, cb * 512:(cb + 1) * 512], in_=ob[:, 0:512])), reads=[bo], writes=[bpr])
                    else:
                        s.op("dve", (lambda h, ob=ob, pr=pr, cb=cb: h.tensor_copy(out=pr[:, cb * 512:(cb + 1) * 512], in_=ob[:, 0:512])), reads=[bo], writes=[bpr])
                s.op("sp", (lambda h, pr=pr, r0=r0: h.dma_start(out=proj_d[r0:r0 + 128, :], in_=pr)), reads=[bpr], dma_sem=dsem["y%d" % (ti % 2)])
        elif kind == "outproj":
            yab_d = P.din("yab", [768, TOK])
            ysf_d = P.din("ysf", [256, TOK])
            ysb_d = P.din("ysb", [256, TOK])
            wg_d = P.din("wg", [256, 512])
            wo_d = P.din("wo", [D, D])
            gbc_d = P.din("gbc_%d" % pi, [2, 128, D])
            bg = Buf()
            gv = gbc.rearrange("p (c f) -> p c f", c=2)
            s.op("sp", (lambda h, gv=gv, gbc_d=gbc_d: h.dma_start(out=gv, in_=gbc_d.rearrange("c p f -> p c f"))), writes=[bg], dma_sem=dsem["misc"])
            wgv = w1v[:, 0:2, 1024:1536]
            load_w(w1v, wo_d, 8, D, bst, bw1)
            wgview = w1s.rearrange("p (k n) -> p k n", k=8)[:, :, 1024:1536]
            load_w(wgview, wg_d, 2, 512, bst, bw1)
            s.barrier()
            ystg = [w2s[:, 0:3072].bitcast(F32), w2s[:, 3072:6144].bitcast(F32)]
            ys1 = w2s[:, 6144:7168].bitcast(F32)
            ys2 = w2s[:, 7168:8192].bitcast(F32)
            ys3 = w2s[:, 8192:9216].bitcast(F32)
            geT = w2s[:, 9216:9728]
            src = xsrc()
            for ti, (t0, nsub, cls) in enumerate(TILES):
                TT = nsub * 128
                yT = hTs[ti % 2]
                yv = yT.rearrange("p (k t) -> p k t", k=8)
                byT = b_hT[ti % 2]
                st = ystg[ti % 2]
                stv = st.rearrange("p (k t) -> p k t", k=6)
                bstg = b_ystg[ti % 2]
                s.op("sp", (lambda h, stv=stv, t0=t0, TT=TT: h.dma_start(out=stv[:, :, 0:TT], in_=yab_d[:, t0:t0 + TT].rearrange("(k p) t -> p k t", p=128))),
                     writes=[bstg], dma_sem=dsem["y%d" % (ti % 2)])
                s.op("pool", (lambda h, stv=stv, yv=yv, TT=TT: h.tensor_copy(out=yv[:, 0:6, 0:TT], in_=stv[:, :, 0:TT])), reads=[bstg], writes=[byT])
                y1v = ys1.rearrange("p (k t) -> p k t", k=2)
                y2v = ys2.rearrange("p (k t) -> p k t", k=2)
                y3v = ys3.rearrange("p (k t) -> p k t", k=2)
                gev = geT.rearrange("p (k t) -> p k t", k=2)
                by1, by2, by3, bge = b_y
                s.op("sp", (lambda h, t0=t0, TT=TT: h.dma_start(out=y1v[:, :, 0:TT], in_=ysf_d[:, t0:t0 + TT].rearrange("(k p) t -> p k t", p=128))), writes=[by1], dma_sem=dsem["w0"])
                s.op("sp", (lambda h, t0=t0, TT=TT: h.dma_start(out=y2v[:, :, 0:TT], in_=ysb_d[:, t0:t0 + TT].rearrange("(k p) t -> p k t", p=128))), writes=[by2], dma_sem=dsem["w1"])
                s.op("dve", lambda h: h.tensor_tensor(out=ys1, in0=ys1, in1=ys2, op=ALU.add), reads=[by1, by2], writes=[by1])
                s.op("dve", lambda h: h.tensor_tensor(out=ys2, in0=ys1, in1=ys1, op=ALU.mult), reads=[by1], writes=[by2])
                s.op("dve", lambda h: h.tensor_scalar(out=ys2, in0=ys2, scalar1=C_GELU, scalar2=1.0, op0=ALU.mult, op1=ALU.add), reads=[by2], writes=[by2])
                s.op("dve", lambda h: h.tensor_tensor(out=ys2, in0=ys2, in1=ys1, op=ALU.mult), reads=[by1, by2], writes=[by2])
                s.op("act", lambda h: h.activation(out=ys3, in_=ys2, func=AF.Sigmoid, scale=K_GELU), reads=[by2], writes=[by3])
                s.op("dve", lambda h: h.tensor_tensor(out=geT, in0=ys1, in1=ys3, op=ALU.mult), reads=[by1, by3], writes=[bge])
                gl = abs_[ti % 2]
                gl2 = abs_[(ti + 1) % 2]
                bgl = b_ab[ti % 2]
                bgl2 = b_ob[ti % 2]
                glb = [gl, obs[ti % 2][1]]
                for fb in range(4):
                    dst = glb[fb // 2][:, (fb % 2) * 256:(fb % 2) * 256 + TT]
                    for kt in range(2):
                        s.op("pe", (lambda h, dst=dst, fb=fb, kt=kt, TT=TT: h.matmul(dst, lhsT=w1v[:, kt, 1024 + fb * 128:1024 + (fb + 1) * 128], rhs=gev[:, kt, 0:TT],
                                                                                    start=(kt == 0), stop=(kt == 1))), reads=[bge, bw1], writes=[bgl, bgl2])
                for i in range(2):
                    s.op("act", (lambda h, i=i, TT=TT, glb=glb: h.activation(out=ys3[:, i * 256:i * 256 + TT], in_=glb[1][:, i * 256:i * 256 + TT], func=AF.Sigmoid)), reads=[bgl, bgl2], writes=[by3])
                    s.op("dve", (lambda h, i=i, TT=TT, yv=yv, glb=glb: h.tensor_tensor(out=yv[:, 6 + i, 0:TT], in0=glb[0][:, i * 256:i * 256 + TT], in1=ys3[:, i * 256:i * 256 + TT], op=ALU.mult)),
                         reads=[bgl, bgl2, by3], writes=[byT])
                for sub in range(nsub):
                    xi = (ti * 2 + sub) % 4
                    xt, bxt = xts[xi], xbufs[xi]
                    r0 = t0 + sub * 128
                    s.op("sp", (lambda h, xt=xt, r0=r0, src=src: h.dma_start(out=xt, in_=src[r0:r0 + 128, :])),
                         reads=[xdram_buf(r0)], writes=[bxt], dma_sem=dsem["x%d" % xi])
                    ob = obs[sub % 2]
                    bo = b_ob[sub % 2]
                    for half in range(2):
                        for kt in range(8):
                            s.op("pe", (lambda h, ob=ob, half=half, kt=kt, sub=sub, yv=yv: h.matmul(
                                ob[half][:, 0:512], lhsT=yv[:, kt, sub * 128:(sub + 1) * 128], rhs=w1v[:, kt, half * 512:(half + 1) * 512],
                                start=(kt == 0), stop=(kt == 7))), reads=[byT, bw1], writes=[bo])
                    for half in range(2):
                        s.op("dve", (lambda h, ob=ob, half=half, cls=cls: h.tensor_tensor(
                            out=tmp[:, half * 512:(half + 1) * 512], in0=ob[half][:, 0:512], in1=gbc[:, cls * 1024 + half * 512: cls * 1024 + (half + 1) * 512], op=ALU.mult)),
                            reads=[bo, bg], writes=[b_tmp[0]])
                    s.op("dve", (lambda h, xt=xt: h.tensor_tensor(out=xt, in0=xt, in1=tmp, op=ALU.add)), reads=[b_tmp[0], bxt], writes=[bxt])
                    s.op("sp", (lambda h, xt=xt, r0=r0: h.dma_start(out=x_scr[r0:r0 + 128, :], in_=xt)), reads=[bxt], writes=[xdram_buf(r0)], dma_sem=dsem["st"])
                    if pi == last_upd:
                        s.op("sp", (lambda h, xt=xt, r0=r0: h.dma_start(out=x_out[r0:r0 + 128, :], in_=xt)), reads=[bxt], dma_sem=dsem["st2"])
            first[0] = False
    return P.finish()


def build_mod_prog():
    P = Prog()
    s = P.s
    NCOL = 4608
    cT_d = P.din("cT", [128, 8, 3])
    w_d = P.din("w", [D, NCOL])
    b_d = P.din("b", [3, NCOL])
    m_d = P.dout("m", [3, NCOL])
    cT = P.sb(24, F32)
    ws = [P.sb(8 * 512, F32) for _ in range(2)]
    bt = P.sb(NCOL, F32)
    mt = P.sb(NCOL, F32)
    pss = [P.ps(F32) for _ in range(2)]
    d0, d1, dm = s.new_dma_sem("a0"), s.new_dma_sem("a1"), s.new_dma_sem("am")
    bc, bb, bm = Buf(), Buf(), Buf()
    bws = [Buf(), Buf()]
    bps = [Buf(), Buf()]
    cv = cT.rearrange("p (k r) -> p k r", k=8)
    s.op("sp", lambda h: h.dma_start(out=cv, in_=cT_d), writes=[bc], dma_sem=dm)
    s.op("sp", lambda h: h.dma_start(out=bt[0:3, :], in_=b_d), writes=[bb], dma_sem=dm)
    s.op("act", lambda h: h.activation(out=cT, in_=cT, func=AF.Silu), reads=[bc], writes=[bc])
    for cb in range(NCOL // 512):
        j = cb % 2
        wv = ws[j].rearrange("p (k n) -> p k n", k=8)
        s.op("sp", (lambda h, wv=wv, cb=cb: h.dma_start(out=wv, in_=w_d[:, cb * 512:(cb + 1) * 512].rearrange("(k p) n -> p k n", p=128))),
             writes=[bws[j]], dma_sem=[d0, d1][j])
        for kt in range(8):
            s.op("pe", (lambda h, wv=wv, kt=kt, j=j: h.matmul(pss[j][0:3, 0:512], lhsT=cv[:, kt, :], rhs=wv[:, kt, :], start=(kt == 0), stop=(kt == 7))),
                 reads=[bc, bws[j]], writes=[bps[j]])
        s.op("dve", (lambda h, j=j, cb=cb: h.tensor_tensor(out=mt[0:3, cb * 512:(cb + 1) * 512], in0=pss[j][0:3, 0:512], in1=bt[0:3, cb * 512:(cb + 1) * 512], op=ALU.add)),
             reads=[bps[j], bb], writes=[bm])
    s.op("sp", lambda h: h.dma_start(out=m_d, in_=mt[0:3, :]), reads=[bm], dma_sem=dm)
    return P.finish()


_PROG_CACHE = {}


def get_prog(key, fn):
    if key not in _PROG_CACHE:
        _PROG_CACHE[key] = fn()
    return _PROG_CACHE[key]


def run(nc, in_maps):
    res = run_bass_kernel_spmd(nc, in_maps, core_ids=list(range(NCORE)))
    return res.results


NS = 16640
NCH = 130
NROW = 260
MAGIC = 12582912.0
TWO_PI = 6.283185307179586


def ap3(ap, mid_n):
    return bass.AP(ap.tensor, ap.offset, [list(ap.ap[0]), [0, mid_n], list(ap.ap[1])])


def build_mixer_prog(do_na=True, do_ret=True, do_s5=True):
    P = Prog()
    s = P.s
    banks = [P.ps(F32) for _ in range(8)]
    dsem = {n: s.new_dma_sem(n) for n in ["a0", "a1", "b0", "b1", "c0", "c1", "m", "o0", "o1"]}

    if do_na:
        P.off = 0
        naq_d = P.din("naq", [64, NS]); nak_d = P.din("nak", [64, NS])
        navA_d = P.din("navA", [128, NCH * 66]); navB_d = P.din("navB", [128, 127 * 66])
        nag_d = P.din("nag", [64, 2]); naG_d = P.din("naG", [128, 4096]); naM_d = P.din("naM", [128, 64])
        obd_d = P.din("onesbd", [64, 64])
        yA_d = P.dout("yA", [64, NROW, 64])
        qT = P.sb(NS, BF16); kT = P.sb(NS, BF16)
        v1A = P.sb(NCH * 66, BF16); v1B = P.sb(127 * 66, BF16)
        EB = P.sb(4096, F32); Mk = P.sb(64, F32); gq = P.sb(2, F32); obd = P.sb(64, F32)
        stg = [P.sb(2048, F32) for _ in range(2)]
        sqs = [P.sb(512, F32) for _ in range(2)]
        rr = [P.sb(512, F32) for _ in range(2)]
        pTs = [P.sb(384, BF16) for _ in range(4)]
        rec = [P.sb(16, F32) for _ in range(2)]
        yst = [P.sb(320, F32) for _ in range(2)]
        bq, bk, bvA, bvB, bEB, bM, bg, bobd = [Buf() for _ in range(8)]
        bstg = [Buf(), Buf()]; bsq = [Buf(), Buf()]; brr = [Buf(), Buf()]; bpT = [Buf(), Buf(), Buf(), Buf()]; brec = [Buf(), Buf()]; byst = [Buf(), Buf()]
        bbank = [Buf() for _ in range(8)]
        s.op("sp", lambda h: h.dma_start(out=gq[0:64, :], in_=nag_d), writes=[bg], dma_sem=dsem["m"])
        s.op("sp", lambda h: h.dma_start(out=obd[0:64, :], in_=obd_d), writes=[bobd], dma_sem=dsem["m"])
        s.op("sp", lambda h: h.dma_start(out=Mk, in_=naM_d), writes=[bM], dma_sem=dsem["m"])
        s.op("sp", lambda h: h.dma_start(out=EB, in_=naG_d), writes=[bEB], dma_sem=dsem["m"])
        s.op("act", lambda h: h.activation(out=EB, in_=EB, func=AF.Exp), reads=[bEB], writes=[bEB])
        EB3 = EB.rearrange("p (a q) -> p a q", q=64)
        s.op("dve", lambda h: h.tensor_tensor(out=EB3, in0=EB3, in1=ap3(Mk, 64), op=ALU.mult), reads=[bEB, bM], writes=[bEB])
        cnt = 0
        for (dsrc, dst, tot, bdst) in ((navA_d, v1A, NCH * 66, bvA), (navB_d, v1B, 127 * 66, bvB)):
            c0 = 0
            while c0 < tot:
                n = min(2048, tot - c0)
                j = cnt % 2; cnt += 1
                s.op("sp", (lambda h, j=j, c0=c0, n=n, dsrc=dsrc: h.dma_start(out=stg[j][:, 0:n], in_=dsrc[:, c0:c0 + n])), writes=[bstg[j]], dma_sem=dsem["a%d" % j])
                s.op("pool", (lambda h, j=j, c0=c0, n=n, dst=dst: h.tensor_copy(out=dst[:, c0:c0 + n], in_=stg[j][:, 0:n])), reads=[bstg[j]], writes=[bdst])
                c0 += n
        for (dsrc, dst, gi, bdst) in ((naq_d, qT, 0, bq), (nak_d, kT, 1, bk)):
            nb = (NS + 511) // 512
            for blk in range(nb):
                c0 = blk * 512
                n = min(512, NS - c0)
                j = cnt % 2; cnt += 1
                st = stg[j][0:64, 0:n]
                s.op("sp", (lambda h, st=st, c0=c0, n=n, dsrc=dsrc: h.dma_start(out=st, in_=dsrc[:, c0:c0 + n])), writes=[bstg[j]], dma_sem=dsem["a%d" % j])
                s.op("act", (lambda h, st=st, j=j, n=n: h.activation(out=sqs[j][0:64, 0:n], in_=st, func=AF.Square)), reads=[bstg[j]], writes=[bsq[j]])
                bk_ = banks[j]
                s.op("pe", (lambda h, j=j, n=n, bk_=bk_: h.matmul(bk_[0:64, 0:n], lhsT=obd[0:64, :], rhs=sqs[j][0:64, 0:n], start=True, stop=True)), reads=[bsq[j], bobd], writes=[bbank[j]])
                s.op("dve", (lambda h, j=j, n=n, bk_=bk_: h.tensor_scalar(out=rr[j][0:64, 0:n], in0=bk_[0:64, 0:n], scalar1=1.0 / 32, scalar2=EPS, op0=ALU.mult, op1=ALU.add)), reads=[bbank[j]], writes=[brr[j]])
                s.op("act", (lambda h, j=j, n=n: h.activation(out=rr[j][0:64, 0:n], in_=rr[j][0:64, 0:n], func=AF.Sqrt)), reads=[brr[j]], writes=[brr[j]])
                s.op("dve", (lambda h, j=j, n=n: h.reciprocal(out=rr[j][0:64, 0:n], in_=rr[j][0:64, 0:n])), reads=[brr[j]], writes=[brr[j]])
                s.op("dve", (lambda h, j=j, n=n, st=st, dst=dst, c0=c0, gi=gi: h.scalar_tensor_tensor(out=dst[0:64, c0:c0 + n], in0=st, scalar=gq[0:64, gi:gi + 1], in1=rr[j][0:64, 0:n], op0=ALU.mult, op1=ALU.mult)),
                     reads=[bstg[j], brr[j], bg], writes=[bdst])
        SCALE = 32 ** -0.5
        vA3 = v1A.rearrange("p (c f) -> p c f", f=66)
        vB3 = v1B.rearrange("p (c f) -> p c f", f=66)
        EB5 = EB.rearrange("p (d h x) -> p d h x", d=8, h=2)
        GR = 5
        for g in range(NROW // GR):
            pv = banks[2 + g % 2]; bpv = bbank[2 + g % 2]
            for ri in range(GR):
                rq = g * GR + ri
                for hl in range(2):
                    u = ri * 2 + hl
                    ui = (rq * 2 + hl) % 4
                    sc = banks[4 + ui]; bsc = bbank[4 + ui]
                    pT = pTs[ui]
                    if rq < 4:
                        chunks = [(4, 0, "A", 0), (5, 128, "A", 1)]
                        qc0 = rq * 64
                        dI = None
                    else:
                        r = rq - 4
                        rs = min(max(r - 4, 0), 248)
                        dI = r - rs
                        chunks = []
                        for c in range(4):
                            R0 = rs + 2 * c
                            if rs % 2 == 0:
                                chunks.append((c, 256 + 64 * R0, "A", 2 + R0 // 2))
                            else:
                                chunks.append((c, 256 + 64 * R0, "B", (R0 - 1) // 2))
                        chunks += [(4, 0, "A", 0), (5, 128, "A", 1)]
                        qc0 = 256 + r * 64
                    for (slot, kc0, til, ci) in chunks:
                        s.op("pe", (lambda h, sc=sc, slot=slot, kc0=kc0, hl=hl, qc0=qc0: h.matmul(
                            sc[:, slot * 64:(slot + 1) * 64], lhsT=kT[32 * hl:32 * hl + 32, kc0:kc0 + 128], rhs=qT[32 * hl:32 * hl + 32, qc0:qc0 + 64], start=True, stop=True)),
                            reads=[bq, bk], writes=[bsc])
                    lo = chunks[0][0] * 64
                    s.op("act", (lambda h, sc=sc, pT=pT, lo=lo: h.activation(out=pT[:, lo:384], in_=sc[:, lo:384], func=AF.Exp, scale=SCALE)), reads=[bsc], writes=[bpT[ui]])
                    if dI is not None:
                        s.op("dve", (lambda h, pT=pT, dI=dI, hl=hl: h.tensor_tensor(out=pT[:, 0:256], in0=pT[:, 0:256], in1=EB5[:, dI, hl, :], op=ALU.mult)), reads=[bEB, bpT[ui]], writes=[bpT[ui]])
                    for k_, (slot, kc0, til, ci) in enumerate(chunks):
                        vv = vA3 if til == "A" else vB3
                        s.op("pe", (lambda h, pv=pv, u=u, pT=pT, slot=slot, vv=vv, ci=ci, hl=hl, k_=k_, nck=len(chunks): h.matmul(
                            pv[0:64, u * 33:(u + 1) * 33], lhsT=pT[:, slot * 64:(slot + 1) * 64], rhs=vv[:, ci, hl * 33:(hl + 1) * 33], start=(k_ == 0), stop=(k_ == nck - 1))),
                            reads=[bpT[ui], bvA, bvB], writes=[bpv])
            gi = g % 2
            pv3 = pv[0:64, 0:GR * 2 * 33].rearrange("p (u f) -> p u f", f=33)
            rc = rec[gi][0:64, 0:GR * 2]
            s.op("dve", (lambda h, rc=rc, pv3=pv3: h.reciprocal(out=rc, in_=pv3[:, :, 32])), reads=[bpv], writes=[brec[gi]])
            y3 = yst[gi][0:64, 0:GR * 64].rearrange("p (u f) -> p u f", f=32)
            rc3 = bass.AP(rc.tensor, rc.offset, [list(rc.ap[0]), list(rc.ap[1]), [0, 32]])
            s.op("dve", (lambda h, y3=y3, pv3=pv3, rc3=rc3: h.tensor_tensor(out=y3, in0=pv3[:, :, 0:32], in1=rc3, op=ALU.mult)), reads=[bpv, brec[gi]], writes=[byst[gi]])
            s.op("sp", (lambda h, gi=gi, g=g: h.dma_start(out=yA_d[:, g * GR:(g + 1) * GR, :], in_=yst[gi][0:64, 0:GR * 64].rearrange("p (r f) -> p r f", f=64))),
                 reads=[byst[gi]], dma_sem=dsem["o%d" % gi])
        s.barrier()
    if do_ret:
        build_mixer_ret(P, banks, dsem)
    if do_s5:
        build_mixer_s5(P, banks, dsem)
    return P.finish()


def c_(a):
    return np.ascontiguousarray(a, dtype=np.float32)


_CONST = {}


def na_consts():
    if "na" not in _CONST:
        w = np.arange(64)
        cs = np.clip(w - 8, 0, 48)
        kc = np.arange(64)
        valid = (kc[:, None] >= cs[None, :]) & (kc[:, None] < cs[None, :] + 16)
        M = np.concatenate([valid, valid], 0).astype(np.float32)
        coff = np.clip(kc[:, None] - w[None, :] + 15, 0, 30)
        kr = np.arange(2)[:, None, None]; dI = np.arange(8)[None, :, None]; c = np.arange(4)[None, None, :]
        roff = (2 * c + kr) - dI + 7
        obd = np.kron(np.eye(2, dtype=np.float32), np.ones((32, 32), np.float32))
        _CONST["na"] = (M, coff, roff, obd)
    return _CONST["na"]


def prep_na(seq_b, l_rpb, qg, kg, j):
    M, coff, roff, obd = na_consts()
    q = seq_b[:, 64 * j:64 * j + 64]; k = seq_b[:, 256 + 64 * j:256 + 64 * j + 64]; v = seq_b[:, 512 + 64 * j:512 + 64 * j + 64]
    v1 = np.ones((NS, 2, 33), np.float32)
    v1[:, :, 0:32] = v.reshape(NS, 2, 32)
    v1 = v1.reshape(NS, 66)
    navA = v1.reshape(NCH, 128, 66).transpose(1, 0, 2).reshape(128, NCH * 66)
    navB = v1[320:320 + 127 * 128].reshape(127, 128, 66).transpose(1, 0, 2).reshape(128, 127 * 66)
    rp = l_rpb[2 * j:2 * j + 2]
    G = rp[:, roff][:, :, :, :, coff]
    G = G.transpose(1, 4, 2, 0, 3, 5).reshape(128, 4096)
    nag = np.stack([np.tile(qg, 2), np.tile(kg, 2)], 1)
    return {"naq": c_(q.T), "nak": c_(k.T), "navA": c_(navA), "navB": c_(navB), "nag": c_(nag), "naG": c_(G), "naM": c_(M), "onesbd": c_(obd)}


def unprep_na(yA):
    t = yA.transpose(1, 0, 2).reshape(NROW * 64, 64)
    return t[:256], t[256:]


def build_mixer_ret(P, banks, dsem):
    s = P.s
    P.off = 0
    G = 5
    NG = NCH // G
    rq_d = P.din("rq", [128, NCH, 64]); rk_d = P.din("rk", [128, NCH, 64]); rv_d = P.din("rv", [128, NCH, 128]); rg_d = P.din("rg", [128, NCH, 128])
    rcos_d = P.din("rcos", [128, NCH, 32]); rsin_d = P.din("rsin", [128, NCH, 32])
    rdec_d = P.din("rdec", [128, 2]); rgn_d = P.din("rgn", [128, 128])
    rcst_d = P.din("rcst", [128, 2 * 128 + 2 + 2 * 128])
    ident_d = P.din("ident", [128, 128])
    yB_d = P.dout("yB", [128, NCH, 128])
    Kt = P.sb(NCH * 64, BF16); Vb = P.sb(NCH * 128, BF16); QT = P.sb(NS, BF16); KT = P.sb(NS, BF16)
    Sf = P.sb(NCH * 128, BF16)
    cst = P.sb(514, F32); lg = P.sb(8, F32); DT = P.sb(128, BF16); DTt = P.sb(256, F32)
    wend = P.sb(2, F32); gch = P.sb(2, F32); WIN = P.sb(256, F32); gn = P.sb(128, F32)
    ident = P.sb(128, BF16); identf = P.sb(128, F32)
    qs = [P.sb(G * 64, F32) for _ in range(2)]; ks = [P.sb(G * 64, F32) for _ in range(2)]
    vs = [P.sb(G * 128, F32) for _ in range(2)]
    cs_ = [P.sb(G * 32, F32) for _ in range(2)]; sn_ = [P.sb(G * 32, F32) for _ in range(2)]
    t1 = P.sb(G * 32, F32); t2 = P.sb(G * 32, F32)
    qr = P.sb(G * 64, BF16); krf = P.sb(G * 64, F32)
    Sst = P.sb(128, F32); Sb = P.sb(128, F32); Sbb = P.sb(128, BF16)
    Vw = [P.sb(128, BF16) for _ in range(2)]
    PT = [P.sb(128, BF16) for _ in range(2)]
    Qw = [P.sb(256, BF16) for _ in range(2)]
    gs = [P.sb(128, F32) for _ in range(2)]
    on = [P.sb(128, F32) for _ in range(2)]
    sm = P.sb(16, F32)
    junk = P.sb(128, F32)
    bbank = [Buf() for _ in range(8)]
    (bKt, bVb, bQT, bKT, bSf, bcst, blg, bDT, bwend, bWIN, bgn, bid, bt, bqr, bkrf, bSst, bSb, bSbb, bsm, bjunk) = [Buf() for _ in range(20)]
    bqs = [Buf(), Buf()]; bks = [Buf(), Buf()]; bvs = [Buf(), Buf()]; bcs = [Buf(), Buf()]
    bVw = [Buf(), Buf()]; bPT = [Buf(), Buf()]; bQw = [Buf(), Buf()]; bgs = [Buf(), Buf()]; bon = [Buf(), Buf()]
    s.op("sp", lambda h: h.dma_start(out=identf, in_=ident_d), writes=[bid], dma_sem=dsem["m"])
    s.op("dve", lambda h: h.tensor_copy(out=ident, in_=identf), reads=[bid], writes=[bid])
    s.op("sp", lambda h: h.dma_start(out=cst, in_=rcst_d), writes=[bcst], dma_sem=dsem["m"])
    s.op("sp", lambda h: h.dma_start(out=lg[:, 0:2], in_=rdec_d), writes=[blg], dma_sem=dsem["m"])
    s.op("sp", lambda h: h.dma_start(out=gn, in_=rgn_d), writes=[bgn], dma_sem=dsem["m"])
    s.op("act", lambda h: h.activation(out=lg[:, 2:4], in_=lg[:, 0:2], func=AF.Exp, scale=-1.0), reads=[blg], writes=[blg])
    s.op("act", lambda h: h.activation(out=lg[:, 2:4], in_=lg[:, 2:4], func=AF.Ln, bias=1.0), reads=[blg], writes=[blg])
    s.op("dve", lambda h: h.tensor_scalar(out=lg[:, 4:6], in0=lg[:, 2:4], scalar1=-1.0, scalar2=None, op0=ALU.mult), reads=[blg], writes=[blg])
    lgf, lgb = lg[:, 4:5], lg[:, 5:6]
    s.op("act", lambda h: h.activation(out=DTt[:, 0:128], in_=cst[:, 0:128], func=AF.Exp, scale=lgf), reads=[blg, bcst], writes=[bDT])
    s.op("act", lambda h: h.activation(out=DTt[:, 128:256], in_=cst[:, 128:256], func=AF.Exp, scale=lgb), reads=[blg, bcst], writes=[bDT])
    s.op("dve", lambda h: h.tensor_tensor(out=DT, in0=DTt[:, 0:128], in1=DTt[:, 128:256], op=ALU.add), reads=[bDT], writes=[bDT])
    s.op("act", lambda h: h.activation(out=wend[:, 0:1], in_=cst[:, 256:257], func=AF.Exp, scale=lgf), reads=[blg, bcst], writes=[bwend])
    s.op("act", lambda h: h.activation(out=wend[:, 1:2], in_=cst[:, 257:258], func=AF.Exp, scale=lgb), reads=[blg, bcst], writes=[bwend])
    s.op("act", lambda h: h.activation(out=gch[:, 0:1], in_=lgf, func=AF.Exp, scale=128.0), reads=[blg], writes=[bwend])
    s.op("act", lambda h: h.activation(out=gch[:, 1:2], in_=lgb, func=AF.Exp, scale=128.0), reads=[blg], writes=[bwend])
    s.op("act", lambda h: h.activation(out=WIN[:, 0:128], in_=cst[:, 258:386], func=AF.Exp, scale=lgf), reads=[blg, bcst], writes=[bWIN])
    s.op("act", lambda h: h.activation(out=WIN[:, 128:256], in_=cst[:, 386:514], func=AF.Exp, scale=lgb), reads=[blg, bcst], writes=[bWIN])
    Kt3 = Kt.rearrange("p (c d) -> p c d", d=64)
    Vb3 = Vb.rearrange("p (c d) -> p c d", d=128)
    Sf3 = Sf.rearrange("p (c d) -> p c d", d=128)
    for g in range(NG):
        j = g % 2
        c0 = g * G
        q3 = qs[j].rearrange("p (c d) -> p c d", d=64); k3 = ks[j].rearrange("p (c d) -> p c d", d=64)
        v3 = vs[j].rearrange("p (c d) -> p c d", d=128)
        co3 = cs_[j].rearrange("p (c d) -> p c d", d=32); si3 = sn_[j].rearrange("p (c d) -> p c d", d=32)
        s.op("sp", (lambda h, q3=q3, c0=c0: h.dma_start(out=q3, in_=rq_d[:, c0:c0 + G, :])), writes=[bqs[j]], dma_sem=dsem["a%d" % j])
        s.op("sp", (lambda h, k3=k3, c0=c0: h.dma_start(out=k3, in_=rk_d[:, c0:c0 + G, :])), writes=[bks[j]], dma_sem=dsem["b%d" % j])
        s.op("sp", (lambda h, v3=v3, c0=c0: h.dma_start(out=v3, in_=rv_d[:, c0:c0 + G, :])), writes=[bvs[j]], dma_sem=dsem["c%d" % j])
        s.op("sp", (lambda h, co3=co3, c0=c0: h.dma_start(out=co3, in_=rcos_d[:, c0:c0 + G, :])), writes=[bcs[j]], dma_sem=dsem["o%d" % j])
        s.op("sp", (lambda h, si3=si3, c0=c0: h.dma_start(out=si3, in_=rsin_d[:, c0:c0 + G, :])), writes=[bcs[j]], dma_sem=dsem["o%d" % j])
        s.op("pool", (lambda h, v3=v3, c0=c0: h.tensor_copy(out=Vb3[:, c0:c0 + G, :], in_=v3)), reads=[bvs[j]], writes=[bVb])
        t13 = t1.rearrange("p (c d) -> p c d", d=32); t23 = t2.rearrange("p (c d) -> p c d", d=32)
        qr3 = qr.rearrange("p (c d) -> p c d", d=64); kr3 = krf.rearrange("p (c d) -> p c d", d=64)
        for (z3, o3, bz, bo) in ((q3, qr3, bqs[j], bqr), (k3, kr3, bks[j], bkrf)):
            s.op("dve", (lambda h, z3=z3, co3=co3: h.tensor_tensor(out=t13, in0=z3[:, :, 0:32], in1=co3, op=ALU.mult)), reads=[bz, bcs[j]], writes=[bt])
            s.op("dve", (lambda h, z3=z3, si3=si3: h.tensor_tensor(out=t23, in0=z3[:, :, 32:64], in1=si3, op=ALU.mult)), reads=[bz, bcs[j]], writes=[bt])
            s.op("dve", (lambda h, o3=o3: h.tensor_tensor(out=o3[:, :, 0:32], in0=t13, in1=t23, op=ALU.subtract)), reads=[bt], writes=[bo])
            s.op("dve", (lambda h, z3=z3, si3=si3: h.tensor_tensor(out=t13, in0=z3[:, :, 0:32], in1=si3, op=ALU.mult)), reads=[bz, bcs[j], bo], writes=[bt])
            s.op("dve", (lambda h, z3=z3, co3=co3: h.tensor_tensor(out=t23, in0=z3[:, :, 32:64], in1=co3, op=ALU.mult)), reads=[bz, bcs[j]], writes=[bt])
            s.op("dve", (lambda h, o3=o3: h.tensor_tensor(out=o3[:, :, 32:64], in0=t13, in1=t23, op=ALU.add)), reads=[bt], writes=[bo])
        s.op("act", (lambda h, c0=c0: h.activation(out=Kt3[:, c0:c0 + G, :], in_=kr3, func=AF.Copy, scale=0.125)), reads=[bkrf], writes=[bKt])
        for (src3, dstT, bsrc, bdst, bi) in ((qr3, QT, bqr, bQT, 0), (Kt3[:, c0:c0 + G, :], KT, bKt, bKT, 1)):
            tp = banks[bi][:, :].bitcast(BF16)
            for ci in range(G):
                s.op("pe", (lambda h, tp=tp, src3=src3, ci=ci: h.transpose(out=tp[0:64, ci * 128:(ci + 1) * 128], in_=src3[:, ci, :], identity=ident)),
                     reads=[bsrc, bid], writes=[bbank[bi]])
            col0 = c0 * 128
            if bi == 0:
                s.op("act", (lambda h, tp=tp, dstT=dstT, col0=col0: h.copy(out=dstT[0:64, col0:col0 + G * 128], in_=tp[0:64, 0:G * 128])), reads=[bbank[bi]], writes=[bdst])
            else:
                s.op("dve", (lambda h, tp=tp, dstT=dstT, col0=col0: h.tensor_copy(out=dstT[0:64, col0:col0 + G * 128], in_=tp[0:64, 0:G * 128])), reads=[bbank[bi]], writes=[bdst])
    s.op("dve", lambda h: h.memset(Sst[0:64, :], 0.0), writes=[bSst])
    s.op("dve", lambda h: h.memset(Sb[0:64, :], 0.0), writes=[bSb])
    for n in range(NCH):
        j = n % 2
        s.op("act", (lambda h, n=n: h.copy(out=Sf3[0:64, n, :], in_=Sst[0:64, :])), reads=[bSst], writes=[bSf])
        s.op("pool", (lambda h, n=n, j=j: h.tensor_scalar(out=Vw[j], in0=Vb3[:, n, :], scalar1=wend[:, 0:1], scalar2=None, op0=ALU.mult)), reads=[bVb, bwend], writes=[bVw[j]])
        bk_ = banks[2 + j]
        s.op("pe", (lambda h, n=n, j=j, bk_=bk_: h.matmul(bk_[0:64, 0:128], lhsT=Kt3[:, n, :], rhs=Vw[j], start=True, stop=True)), reads=[bKt, bVw[j]], writes=[bbank[2 + j]])
        s.op("dve", (lambda h, bk_=bk_: h.scalar_tensor_tensor(out=Sst[0:64, :], in0=Sst[0:64, :], scalar=gch[0:64, 0:1], in1=bk_[0:64, 0:128], op0=ALU.mult, op1=ALU.add)),
             reads=[bbank[2 + j], bSst, bwend], writes=[bSst])
    order = [1, 0] + list(range(NCH - 1, 1, -1))
    G2 = 4
    AX = mybir.AxisListType.X
    ysts = [P.sb(G2 * 128, F32) for _ in range(2)]
    gs4 = [P.sb(G2 * 128, F32) for _ in range(2)]
    sqb = P.sb(G2 * 128, F32)
    sm4 = P.sb(32, F32)
    bysts = [Buf(), Buf()]; bgs4 = [Buf(), Buf()]; bsqb = Buf(); bsm4 = Buf()
    ngroups = (NCH + G2 - 1) // G2
    oi = 0
    for gi_ in range(ngroups):
        grp = order[gi_ * G2:(gi_ + 1) * G2]
        ng = len(grp)
        gj = gi_ % 2
        ob = banks[6 + gj]; bob = bbank[6 + gj]
        for k, n in enumerate(grp):
            j = oi % 2
            oi += 1
            cols = slice(n * 128, (n + 1) * 128)
            osl = slice(k * 128, (k + 1) * 128)
            s.op("sp", (lambda h, n=n, gj=gj, osl=osl: h.dma_start(out=gs4[gj][:, osl], in_=rg_d[:, n, :])), writes=[bgs4[gj]], dma_sem=dsem["a%d" % j])
            sc = banks[4 + j]
            s.op("pe", (lambda h, sc=sc, cols=cols: h.matmul(sc[:, 0:128], lhsT=KT[0:64, cols], rhs=QT[0:64, cols], start=True, stop=True)), reads=[bKT, bQT], writes=[bbank[4 + j]])
            s.op("dve", (lambda h, sc=sc, j=j: h.tensor_tensor(out=PT[j], in0=sc[:, 0:128], in1=DT, op=ALU.mult)), reads=[bbank[4 + j], bDT], writes=[bPT[j]])
            s.op("pool", (lambda h, j=j, cols=cols: h.tensor_tensor(out=Qw[j][0:64, 0:128], in0=QT[0:64, cols], in1=WIN[0:64, 0:128], op=ALU.mult)), reads=[bQT, bWIN], writes=[bQw[j]])
            s.op("pool", (lambda h, j=j, cols=cols: h.tensor_tensor(out=Qw[j][0:64, 128:256], in0=QT[0:64, cols], in1=WIN[0:64, 128:256], op=ALU.mult)), reads=[bQT, bWIN], writes=[bQw[j]])
            s.op("act", lambda h: h.copy(out=Sbb[0:64, :], in_=Sb[0:64, :]), reads=[bSb], writes=[bSbb])
            s.op("pe", (lambda h, ob=ob, j=j, n=n, osl=osl: h.matmul(ob[:, osl], lhsT=PT[j], rhs=Vb3[:, n, :], start=True, stop=False)), reads=[bPT[j], bVb], writes=[bob])
            s.op("pe", (lambda h, ob=ob, j=j, n=n, osl=osl: h.matmul(ob[:, osl], lhsT=Qw[j][0:64, 0:128], rhs=Sf3[0:64, n, :], start=False, stop=False)), reads=[bQw[j], bSf], writes=[bob])
            s.op("pe", (lambda h, ob=ob, j=j, osl=osl: h.matmul(ob[:, osl], lhsT=Qw[j][0:64, 128:256], rhs=Sbb[0:64, :], start=False, stop=True)), reads=[bQw[j], bSbb], writes=[bob])
            s.op("pool", (lambda h, n=n, j=j: h.tensor_scalar(out=Vw[j], in0=Vb3[:, n, :], scalar1=wend[:, 1:2], scalar2=None, op0=ALU.mult)), reads=[bVb, bwend], writes=[bVw[j]])
            bk_ = banks[2 + j]
            s.op("pe", (lambda h, n=n, j=j, bk_=bk_: h.matmul(bk_[0:64, 0:128], lhsT=Kt3[:, n, :], rhs=Vw[j], start=True, stop=True)), reads=[bKt, bVw[j]], writes=[bbank[2 + j]])
            s.op("dve", (lambda h, bk_=bk_: h.scalar_tensor_tensor(out=Sb[0:64, :], in0=Sb[0:64, :], scalar=gch[0:64, 1:2], in1=bk_[0:64, 0:128], op0=ALU.mult, op1=ALU.add)),
                 reads=[bbank[2 + j], bSb, bSbb, bwend], writes=[bSb])
        W_ = ng * 128
        ob3 = ob[:, 0:W_].rearrange("p (c v) -> p c v", v=128)
        s.op("act", (lambda h, gj=gj, W_=W_: h.activation(out=gs4[gj][:, 0:W_], in_=gs4[gj][:, 0:W_], func=AF.Silu)), reads=[bgs4[gj]], writes=[bgs4[gj]])
        for k in range(ng):
            osl = slice(k * 128, (k + 1) * 128)
            s.op("act", (lambda h, ob=ob, osl=osl, k=k: h.activation(out=sqb[:, osl], in_=ob[:, osl], func=AF.Identity, accum_out=sm4[:, k:k + 1])), reads=[bob], writes=[bsqb, bsm4])
            s.op("act", (lambda h, ob=ob, osl=osl, k=k: h.activation(out=sqb[:, osl], in_=ob[:, osl], func=AF.Square, accum_out=sm4[:, 4 + k:5 + k])), reads=[bob], writes=[bsqb, bsm4])
        s.op("dve", lambda h: h.tensor_scalar(out=sm4[:, 8:16], in0=sm4[:, 0:8], scalar1=1.0 / 128, scalar2=None, op0=ALU.mult), reads=[bsm4], writes=[bsm4])
        s.op("dve", lambda h: h.tensor_tensor(out=sm4[:, 16:20], in0=sm4[:, 8:12], in1=sm4[:, 8:12], op=ALU.mult), reads=[bsm4], writes=[bsm4])
        s.op("dve", lambda h: h.tensor_tensor(out=sm4[:, 20:24], in0=sm4[:, 12:16], in1=sm4[:, 16:20], op=ALU.subtract), reads=[bsm4], writes=[bsm4])
        s.op("dve", lambda h: h.tensor_scalar(out=sm4[:, 20:24], in0=sm4[:, 20:24], scalar1=0.0, scalar2=EPS, op0=ALU.max, op1=ALU.add), reads=[bsm4], writes=[bsm4])
        s.op("act", lambda h: h.activation(out=sm4[:, 20:24], in_=sm4[:, 20:24], func=AF.Sqrt), reads=[bsm4], writes=[bsm4])
        s.op("dve", lambda h: h.reciprocal(out=sm4[:, 24:28], in_=sm4[:, 20:24]), reads=[bsm4], writes=[bsm4])

        def bcl(col, ng=ng):
            return bass.AP(col.tensor, col.offset, [list(col.ap[0]), [col.ap[1][0], ng], [0, 128]])
        y3 = ysts[gj][:, 0:W_].rearrange("p (c v) -> p c v", v=128)
        g3 = gs4[gj][:, 0:W_].rearrange("p (c v) -> p c v", v=128)
        mean_bc = bcl(sm4[:, 8:8 + ng]); rstd_bc = bcl(sm4[:, 24:24 + ng])
        gn_bc = bass.AP(gn.tensor, gn.offset, [list(gn.ap[0]), [0, ng], list(gn.ap[1])])
        s.op("dve", (lambda h, y3=y3, ob3=ob3, mean_bc=mean_bc: h.tensor_tensor(out=y3, in0=ob3, in1=mean_bc, op=ALU.subtract)), reads=[bob, bsm4], writes=[bysts[gj]])
        s.op("pool", (lambda h, y3=y3, rstd_bc=rstd_bc: h.tensor_tensor(out=y3, in0=y3, in1=rstd_bc, op=ALU.mult)), reads=[bsm4, bysts[gj]], writes=[bysts[gj]])
        s.op("pool", (lambda h, y3=y3, gn_bc=gn_bc: h.tensor_tensor(out=y3, in0=y3, in1=gn_bc, op=ALU.mult)), reads=[bgn, bysts[gj]], writes=[bysts[gj]])
        s.op("dve", (lambda h, y3=y3, g3=g3: h.tensor_tensor(out=y3, in0=y3, in1=g3, op=ALU.mult)), reads=[bgs4[gj], bysts[gj]], writes=[bysts[gj]])
        for k, n in enumerate(grp):
            s.op("sp", (lambda h, gj=gj, k=k, n=n: h.dma_start(out=yB_d[:, n, :], in_=ysts[gj][:, k * 128:(k + 1) * 128])), reads=[bysts[gj]], dma_sem=dsem["b%d" % (k % 2)])
    s.barrier()


def ret_consts():
    if "ret" not in _CONST:
        j = np.arange(128)[:, None].astype(np.float64); i = np.arange(128)[None, :].astype(np.float64)
        dmf = np.where(i >= j, i - j, 1e6); dmb = np.where(j > i, j - i, 1e6)
        posf = 127 - j; posb = j + 0 * j
        winf = np.broadcast_to(i + 1, (128, 128)); winb = np.broadcast_to(128 - i, (128, 128))
        cst = np.concatenate([dmf, dmb, posf, posb, winf, winb], 1)
        nf = 16
        inv = 10000.0 ** (-np.arange(nf, dtype=np.float32) / nf)
        t = np.arange(16384)
        row = (t // 64).astype(np.float32); col = (t % 64).astype(np.float32)
        ang = np.concatenate([row[:, None] * inv, col[:, None] * inv], -1).astype(np.float32)
        cos = np.concatenate([np.ones((256, 32), np.float32), np.cos(ang)], 0)
        sin = np.concatenate([np.zeros((256, 32), np.float32), np.sin(ang)], 0)
        cos = cos.reshape(NCH, 128, 32).transpose(1, 0, 2); sin = sin.reshape(NCH, 128, 32).transpose(1, 0, 2)
        _CONST["ret"] = (c_(cst), c_(cos), c_(sin), c_(np.eye(128)))
    return _CONST["ret"]


def tokmajor(a):
    return c_(a.reshape(NCH, 128, -1).transpose(1, 0, 2))


def prep_ret(seq_b, decay_l, gn_l, j):
    cst, cos, sin, ident = ret_consts()
    q = seq_b[:, 768 + 64 * j:768 + 64 * j + 64]; k = seq_b[:, 1024 + 64 * j:1024 + 64 * j + 64]
    v = seq_b[:, 1280 + 128 * j:1280 + 128 * j + 128]; g = seq_b[:, 1792 + 128 * j:1792 + 128 * j + 128]
    rdec = np.broadcast_to(decay_l[:, j][None, :], (128, 2))
    rgn = np.broadcast_to(gn_l[128 * j:128 * j + 128][None, :], (128, 128))
    return {"rq": tokmajor(q), "rk": tokmajor(k), "rv": tokmajor(v), "rg": tokmajor(g), "rcos": cos, "rsin": sin,
            "rdec": c_(rdec), "rgn": c_(rgn), "rcst": cst, "ident": ident}


def unprep_ret(yB):
    t = yB.transpose(1, 0, 2).reshape(NS, 128)
    return t[:256], t[256:]


def build_mixer_s5(P, banks, dsem):
    s = P.s
    P.off = 0
    TB = 1280
    SUB = 320
    NB = NS // TB
    su_d = [P.din("suF", [64, NS]), P.din("suB", [64, NS])]
    sP_d = P.din("sP", [128, 4, 3])
    sB_d = P.din("sB", [64, 2, 128])
    sC_d = P.din("sC", [128, 4, 2, 64])
    sD_d = P.din("sD", [64, 1])
    tpos_d = P.din("tpos", [128, TB])
    ys_d = [P.dout("ysF", [64, NS]), P.dout("ysB", [64, NS])]
    uT = [P.sb(NS, BF16) for _ in range(2)]
    par = P.sb(12, F32); Bf = P.sb(256, F32); Bb = P.sb(256, BF16); Cf = P.sb(512, F32); Cb = P.sb(512, BF16); Dv = P.sb(1, F32); halfpi = P.sb(1, F32)
    tpos = P.sb(TB, F32)
    w = P.sb(64, F32)
    stg = [P.sb(2048, F32) for _ in range(2)]
    ang = P.sb(TB, F32); kk = P.sb(TB, F32); sn = P.sb(TB, F32); cs = P.sb(TB, F32)
    wre = P.sb(TB, F32); wim = P.sb(TB, F32); sre = P.sb(TB, F32); sim = P.sb(TB, F32)
    ta = P.sb(TB, F32); tb = P.sb(TB, F32)
    xre = P.sb(TB, BF16); xim = P.sb(TB, BF16)
    yo = [P.sb(TB, F32) for _ in range(2)]
    last = P.sb(8, F32)
    (bu, bpar, bB, bC, bD, btp, bw, bang, bsn, bcs, bwre, bwim, bsre, bsim, bta, btb, bxre, bxim, blast) = [Buf() for _ in range(19)]
    bstg = [Buf(), Buf()]; byo = [Buf(), Buf()]
    bbank = [Buf() for _ in range(8)]
    s.op("sp", lambda h: h.dma_start(out=par.rearrange("p (u k) -> p u k", k=3), in_=sP_d), writes=[bpar], dma_sem=dsem["m"])
    s.op("sp", lambda h: h.dma_start(out=Bf[0:64, :].rearrange("p (r n) -> p r n", r=2), in_=sB_d), writes=[bB], dma_sem=dsem["m"])
    s.op("sp", lambda h: h.dma_start(out=Cf.rearrange("p (u r n) -> p u r n", u=4, r=2), in_=sC_d), writes=[bC], dma_sem=dsem["m"])
    s.op("sp", lambda h: h.dma_start(out=Dv[0:64, :], in_=sD_d), writes=[bD], dma_sem=dsem["m"])
    s.op("sp", lambda h: h.dma_start(out=tpos, in_=tpos_d), writes=[btp], dma_sem=dsem["m"])
    s.op("dve", lambda h: h.tensor_copy(out=Bb[0:64, :], in_=Bf[0:64, :]), reads=[bB], writes=[bB])
    cnt = 0
    for d in range(2):
        for c0 in range(0, NS, 2048):
            n = min(2048, NS - c0)
            j = cnt % 2; cnt += 1
            s.op("sp", (lambda h, j=j, c0=c0, n=n, d=d: h.dma_start(out=stg[j][0:64, 0:n], in_=su_d[d][:, c0:c0 + n])), writes=[bstg[j]], dma_sem=dsem["a%d" % j])
            s.op("pool", (lambda h, j=j, c0=c0, n=n, d=d: h.tensor_copy(out=uT[d][0:64, c0:c0 + n], in_=stg[j][0:64, 0:n])), reads=[bstg[j]], writes=[bu])
    p3 = par.rearrange("p (u k) -> p u k", k=3)
    are_, aim_, ldt = p3[:, :, 0], p3[:, :, 1], p3[:, :, 2]
    W = lambda i: w[:, 4 * i:4 * i + 4]
    dt, are, lam, mag, th, sn0, cs0, den, nr, fre, fim, t0_, t1_, rden = [W(i) for i in range(14)]
    ops = s.op

    def dv(fn, r=(bpar, bw), wr=(bw,)):
        ops("dve", fn, reads=list(r), writes=list(wr))

    def ac(fn, r=(bpar, bw), wr=(bw,)):
        ops("act", fn, reads=list(r), writes=list(wr))

    def sincos(src, dst_sin, dst_cos, n):
        pass

    dv(lambda h: h.memset(halfpi, 1.5707963267948966))
    ac(lambda h: h.activation(out=dt, in_=ldt, func=AF.Exp))
    dv(lambda h: h.tensor_scalar(out=are, in0=are_, scalar1=-1e-4, scalar2=None, op0=ALU.min))
    dv(lambda h: h.tensor_tensor(out=lam, in0=dt, in1=are, op=ALU.mult))
    ac(lambda h: h.activation(out=mag, in_=lam, func=AF.Exp))
    dv(lambda h: h.tensor_tensor(out=th, in0=dt, in1=aim_, op=ALU.mult))
    dv(lambda h: h.tensor_scalar(out=t0_, in0=th, scalar1=1.0 / TWO_PI, scalar2=MAGIC, op0=ALU.mult, op1=ALU.add))
    dv(lambda h: h.tensor_scalar(out=t0_, in0=t0_, scalar1=MAGIC, scalar2=-TWO_PI, op0=ALU.subtract, op1=ALU.mult))
    dv(lambda h: h.tensor_tensor(out=t1_, in0=t0_, in1=th, op=ALU.add))
    dv(lambda h: h.tensor_scalar(out=t1_, in0=t1_, scalar1=-3.14159, scalar2=3.14159, op0=ALU.max, op1=ALU.min))
    ac(lambda h: h.activation(out=sn0, in_=t1_, func=AF.Sin))
    ac(lambda h: h.activation(out=t0_, in_=t1_, func=AF.Abs))
    ac(lambda h: h.activation(out=cs0, in_=t0_, func=AF.Sin, scale=-1.0, bias=halfpi[:, 0:1]))
    abre, abim = W(14), W(15)
    dv(lambda h: h.tensor_tensor(out=abre, in0=mag, in1=cs0, op=ALU.mult))
    dv(lambda h: h.tensor_tensor(out=abim, in0=mag, in1=sn0, op=ALU.mult))
    dv(lambda h: h.tensor_tensor(out=den, in0=are, in1=are, op=ALU.mult))
    dv(lambda h: h.tensor_tensor(out=t0_, in0=aim_, in1=aim_, op=ALU.mult))
    dv(lambda h: h.tensor_tensor(out=den, in0=den, in1=t0_, op=ALU.add))
    dv(lambda h: h.reciprocal(out=rden, in_=den))
    dv(lambda h: h.tensor_scalar(out=nr, in0=abre, scalar1=-1.0, scalar2=None, op0=ALU.add))
    dv(lambda h: h.tensor_tensor(out=t0_, in0=nr, in1=are, op=ALU.mult))
    dv(lambda h: h.tensor_tensor(out=t1_, in0=abim, in1=aim_, op=ALU.mult))
    dv(lambda h: h.tensor_tensor(out=fre, in0=t0_, in1=t1_, op=ALU.add))
    dv(lambda h: h.tensor_tensor(out=fre, in0=fre, in1=rden, op=ALU.mult))
    dv(lambda h: h.tensor_tensor(out=t0_, in0=abim, in1=are, op=ALU.mult))
    dv(lambda h: h.tensor_tensor(out=t1_, in0=nr, in1=aim_, op=ALU.mult))
    dv(lambda h: h.tensor_tensor(out=fim, in0=t0_, in1=t1_, op=ALU.subtract))
    dv(lambda h: h.tensor_tensor(out=fim, in0=fim, in1=rden, op=ALU.mult))
    C4 = Cf.rearrange("p (u r n) -> p u r n", u=4, r=2)
    Cb4 = Cb.rearrange("p (u r n) -> p u r n", u=4, r=2)
    Ct = ta[:, 0:512].rearrange("p (u r n) -> p u r n", u=4, r=2)

    def bc(col):
        return bass.AP(col.tensor, col.offset, [list(col.ap[0]), list(col.ap[1]), [0, 64]])

    rC = (bC, bw, bta)
    dv(lambda h: h.tensor_tensor(out=Ct[:, :, 0, :], in0=C4[:, :, 0, :], in1=bc(fre), op=ALU.mult), r=rC, wr=(bta,))
    dv(lambda h: h.tensor_tensor(out=Ct[:, :, 1, :], in0=C4[:, :, 1, :], in1=bc(fim), op=ALU.mult), r=rC, wr=(bta,))
    dv(lambda h: h.tensor_tensor(out=Cb4[:, :, 0, :], in0=Ct[:, :, 0, :], in1=Ct[:, :, 1, :], op=ALU.subtract), r=rC, wr=(bC,))
    dv(lambda h: h.tensor_tensor(out=Ct[:, :, 0, :], in0=C4[:, :, 0, :], in1=bc(fim), op=ALU.mult), r=rC, wr=(bta,))
    dv(lambda h: h.tensor_tensor(out=Ct[:, :, 1, :], in0=C4[:, :, 1, :], in1=bc(fre), op=ALU.mult), r=rC, wr=(bta,))
    dv(lambda h: h.tensor_tensor(out=Ct[:, :, 0, :], in0=Ct[:, :, 0, :], in1=Ct[:, :, 1, :], op=ALU.add), r=rC, wr=(bta,))
    dv(lambda h: h.tensor_scalar(out=Cb4[:, :, 1, :], in0=Ct[:, :, 0, :], scalar1=-1.0, scalar2=None, op0=ALU.mult), r=rC, wr=(bC,))
    B3 = Bb[0:64, :].rearrange("p (r n) -> p r n", r=2)
    for u in range(4):
        q, d = u // 2, u % 2
        thu = th[:, u:u + 1]; magu = mag[:, u:u + 1]
        for blk in range(NB):
            t0 = blk * TB
            dv(lambda h, t0=t0, thu=thu: h.tensor_scalar(out=ang, in0=tpos, scalar1=float(t0), scalar2=thu, op0=ALU.add, op1=ALU.mult), r=(btp, bw, bang), wr=(bang,))
            dv(lambda h: h.tensor_scalar(out=kk, in0=ang, scalar1=1.0 / TWO_PI, scalar2=MAGIC, op0=ALU.mult, op1=ALU.add), r=(bang,), wr=(bang,))
            dv(lambda h: h.tensor_scalar(out=kk, in0=kk, scalar1=MAGIC, scalar2=-TWO_PI, op0=ALU.subtract, op1=ALU.mult), r=(bang,), wr=(bang,))
            dv(lambda h: h.tensor_tensor(out=ang, in0=ang, in1=kk, op=ALU.add), r=(bang,), wr=(bang,))
            dv(lambda h: h.tensor_scalar(out=ang, in0=ang, scalar1=-3.14159, scalar2=3.14159, op0=ALU.max, op1=ALU.min), r=(bang,), wr=(bang,))
            ac(lambda h: h.activation(out=sn, in_=ang, func=AF.Sin), r=(bang,), wr=(bsn,))
            ac(lambda h: h.activation(out=kk, in_=ang, func=AF.Abs), r=(bang,), wr=(bang,))
            ac(lambda h: h.activation(out=cs, in_=kk, func=AF.Sin, scale=-1.0, bias=halfpi[:, 0:1]), r=(bang,), wr=(bcs,))
            for sb_ in range(TB // SUB):
                c0 = t0 + sb_ * SUB
                lo = sb_ * SUB
                jb = sb_ % 2
                pre, pim = banks[jb * 2], banks[jb * 2 + 1]
                ops("pe", (lambda h, pre=pre, q=q, d=d, c0=c0: h.matmul(pre[:, 0:SUB], lhsT=B3[32 * q:32 * q + 32, 0, :], rhs=uT[d][32 * q:32 * q + 32, c0:c0 + SUB], start=True, stop=True)),
                    reads=[bB, bu], writes=[bbank[jb * 2]])
                ops("pe", (lambda h, pim=pim, q=q, d=d, c0=c0: h.matmul(pim[:, 0:SUB], lhsT=B3[32 * q:32 * q + 32, 1, :], rhs=uT[d][32 * q:32 * q + 32, c0:c0 + SUB], start=True, stop=True)),
                    reads=[bB, bu], writes=[bbank[jb * 2 + 1]])
                sl = slice(lo, lo + SUB)
                dv(lambda h, pre=pre, sl=sl: h.tensor_tensor(out=ta[:, sl], in0=pre[:, 0:SUB], in1=cs[:, sl], op=ALU.mult), r=(bbank[jb * 2], bcs), wr=(bta,))
                dv(lambda h, pim=pim, sl=sl: h.tensor_tensor(out=tb[:, sl], in0=pim[:, 0:SUB], in1=sn[:, sl], op=ALU.mult), r=(bbank[jb * 2 + 1], bsn), wr=(btb,))
                ops("pool", (lambda h, sl=sl: h.tensor_tensor(out=wre[:, sl], in0=ta[:, sl], in1=tb[:, sl], op=ALU.add)), reads=[bta, btb], writes=[bwre])
                dv(lambda h, pim=pim, sl=sl: h.tensor_tensor(out=sre[:, sl], in0=pim[:, 0:SUB], in1=cs[:, sl], op=ALU.mult), r=(bbank[jb * 2 + 1], bcs), wr=(bsre,))
                dv(lambda h, pre=pre, sl=sl: h.tensor_tensor(out=sim[:, sl], in0=pre[:, 0:SUB], in1=sn[:, sl], op=ALU.mult), r=(bbank[jb * 2], bsn), wr=(bsim,))
                ops("pool", (lambda h, sl=sl: h.tensor_tensor(out=wim[:, sl], in0=sre[:, sl], in1=sim[:, sl], op=ALU.subtract)), reads=[bsre, bsim], writes=[bwim])
            ini_re = 0.0 if blk == 0 else last[:, 0:1]
            ini_im = 0.0 if blk == 0 else last[:, 1:2]
            dv(lambda h, ini_re=ini_re, magu=magu: h.tensor_tensor_scan(out=sre, data0=bcast_free(magu, TB), data1=wre, initial=ini_re, op0=ALU.mult, op1=ALU.add), r=(bwre, bw, blast, bsre), wr=(bsre,))
            dv(lambda h, ini_im=ini_im, magu=magu: h.tensor_tensor_scan(out=sim, data0=bcast_free(magu, TB), data1=wim, initial=ini_im, op0=ALU.mult, op1=ALU.add), r=(bwim, bw, blast, bsim), wr=(bsim,))
            ops("pool", lambda h: h.tensor_copy(out=last[:, 0:1], in_=sre[:, TB - 1:TB]), reads=[bsre], writes=[blast])
            ops("pool", lambda h: h.tensor_copy(out=last[:, 1:2], in_=sim[:, TB - 1:TB]), reads=[bsim], writes=[blast])
            ops("pool", lambda h: h.tensor_tensor(out=ta, in0=sre, in1=cs, op=ALU.mult), reads=[bsre, bcs], writes=[bta])
            dv(lambda h: h.tensor_tensor(out=tb, in0=sim, in1=sn, op=ALU.mult), r=(bsim, bsn), wr=(btb,))
            dv(lambda h: h.tensor_tensor(out=xre, in0=ta, in1=tb, op=ALU.subtract), r=(bta, btb), wr=(bxre,))
            ops("pool", lambda h: h.tensor_tensor(out=ta, in0=sre, in1=sn, op=ALU.mult), reads=[bsre, bsn, bxre], writes=[bta])
            dv(lambda h: h.tensor_tensor(out=tb, in0=sim, in1=cs, op=ALU.mult), r=(bsim, bcs, bxre), wr=(btb,))
            dv(lambda h: h.tensor_tensor(out=xim, in0=ta, in1=tb, op=ALU.add), r=(bta, btb), wr=(bxim,))
            yj = blk % 2
            for sb_ in range(TB // SUB):
                lo = sb_ * SUB
                c0 = t0 + lo
                pb = banks[4 + sb_ % 2]; bpb = bbank[4 + sb_ % 2]
                ops("pe", (lambda h, pb=pb, u=u, lo=lo: h.matmul(pb[0:64, 0:SUB], lhsT=Cb4[:, u, 0, :], rhs=xre[:, lo:lo + SUB], start=True, stop=False)), reads=[bC, bxre], writes=[bpb])
                ops("pe", (lambda h, pb=pb, u=u, lo=lo: h.matmul(pb[0:64, 0:SUB], lhsT=Cb4[:, u, 1, :], rhs=xim[:, lo:lo + SUB], start=False, stop=True)), reads=[bC, bxim], writes=[bpb])
                if d == 0:
                    dv(lambda h, pb=pb, lo=lo, yj=yj, q=q, c0=c0: h.scalar_tensor_tensor(out=yo[yj][32 * q:32 * q + 32, lo:lo + SUB], in0=uT[0][32 * q:32 * q + 32, c0:c0 + SUB],
                                                                                        scalar=Dv[32 * q:32 * q + 32, 0:1], in1=pb[32 * q:32 * q + 32, 0:SUB], op0=ALU.mult, op1=ALU.add),
                       r=(bpb, bu, bD, byo[yj]), wr=(byo[yj],))
                else:
                    dv(lambda h, pb=pb, lo=lo, yj=yj, q=q: h.tensor_copy(out=yo[yj][32 * q:32 * q + 32, lo:lo + SUB], in_=pb[32 * q:32 * q + 32, 0:SUB]), r=(bpb, byo[yj]), wr=(byo[yj],))
            ops("sp", (lambda h, yj=yj, q=q, d=d, t0=t0: h.dma_start(out=ys_d[d][32 * q:32 * q + 32, t0:t0 + TB], in_=yo[yj][32 * q:32 * q + 32, :])), reads=[byo[yj]], dma_sem=dsem["o%d" % yj])
    s.barrier()


def s5_consts():
    if "s5" not in _CONST:
        _CONST["s5"] = c_(np.broadcast_to(np.arange(1280, dtype=np.float32)[None, :], (128, 1280)))
    return _CONST["s5"]


def prep_s5(seq_b, a_re, a_im, log_dt, b_re, b_im, c_re, c_im, dvec, j):
    u = seq_b[:, 2304 + 64 * j:2304 + 64 * j + 64]
    uF = u.T
    uB = np.concatenate([u[:256][::-1], u[256:][::-1]], 0).T
    sP = np.zeros((128, 4, 3), np.float32)
    sB = np.zeros((64, 2, 128), np.float32)
    sC = np.zeros((128, 4, 2, 64), np.float32)
    for q in range(2):
        for g2 in range(2):
            g = 4 * j + 2 * q + g2
            rows = slice(g2 * 64, g2 * 64 + 64)
            for d in range(2):
                un = q * 2 + d
                sP[rows, un, 0] = a_re[d, g]; sP[rows, un, 1] = a_im[d, g]; sP[rows, un, 2] = log_dt[d, g]
                sC[rows, un, 0, 32 * q + 16 * g2:32 * q + 16 * g2 + 16] = c_re[d, g].T
                sC[rows, un, 1, 32 * q + 16 * g2:32 * q + 16 * g2 + 16] = c_im[d, g].T
            sB[32 * q + 16 * g2:32 * q + 16 * g2 + 16, 0, rows] = b_re[g].T
            sB[32 * q + 16 * g2:32 * q + 16 * g2 + 16, 1, rows] = b_im[g].T
    sD = dvec[64 * j:64 * j + 64][:, None]
    return {"suF": c_(uF), "suB": c_(uB), "sP": sP, "sB": sB, "sC": sC, "sD": c_(sD), "tpos": s5_consts()}


def unprep_s5(ysF, ysB):
    f = ysF.T
    bproc = ysB.T
    b = np.concatenate([bproc[:256][::-1], bproc[256:][::-1]], 0)
    return f, b


def _modT(mm, b, i_scale, i_shift):
    v = np.stack([mm[b, i_scale], mm[b, i_shift], mm[2, i_scale], mm[2, i_shift]], -1)
    return c_(v.reshape(8, 128, 4).transpose(1, 0, 2))


def _gbc(mm, b, i):
    return c_(np.stack([np.broadcast_to(mm[b, i], (128, D)), np.broadcast_to(mm[2, i], (128, D))], 0))


def mod_all(c, c_ctx, w_mod, b_mod):
    rows = np.stack([c[0], c[1], c_ctx], 0)
    cT = c_(rows.T.reshape(8, 128, 3).transpose(1, 0, 2))
    maps = []
    for i in range(NCORE):
        l, part = i // 2, i % 2
        sl = slice(part * 4608, (part + 1) * 4608)
        maps.append({"cT": cT, "w": c_(w_mod[l][:, sl]), "b": c_(np.broadcast_to(b_mod[l][sl], (3, 4608)))})
    res = run(get_prog("mod", build_mod_prog), maps)
    m = np.stack([np.concatenate([res[2 * l]["m"], res[2 * l + 1]["m"]], axis=1) for l in range(4)], 1)
    return m.reshape(3, 4, 9, D)


def kernel(x, c, ctx, c_ctx, w_mod, b_mod, ffn1_w_in, ffn1_w_out, w_in, w_out,
           na_q_gain, na_k_gain, na_rpb, ret_decay, ret_gn,
           s5_a_re, s5_a_im, s5_log_dt, s5_b_re, s5_b_im, s5_c_re, s5_c_im, s5_d, s5_w_glu,
           ffn2_w_in, ffn2_w_out):
    f = lambda a: np.asarray(a, dtype=np.float32)
    x, c, ctx, c_ctx = f(x), f(c), f(ctx), f(c_ctx)
    mall = mod_all(c, c_ctx, f(w_mod), f(b_mod))
    ident = c_(np.eye(128))
    X = []
    for core in range(NCORE):
        b, sg = core // 4, core % 4
        xt = np.zeros((TOK, D), np.float32)
        xt[:4096] = x[b, sg * 4096:(sg + 1) * 4096]
        xt[4096:4160] = ctx[b, sg * 64:(sg + 1) * 64]
        X.append(xt)
    mix = None
    for l in range(5):
        passes = []
        maps = [{"x": X[core], "ident": ident} for core in range(NCORE)]
        if l > 0:
            mm = mall[:, l - 1]
            passes += [{"kind": "outproj"}, {"kind": "ffn"}]
            for core in range(NCORE):
                b, sg = core // 4, core % 4
                mp = maps[core]
                yab, ysf, ysb = mix[b]
                def cut(a):
                    o = np.zeros((a.shape[1], TOK), np.float32)
                    o[:, :4096] = a[256 + sg * 4096:256 + (sg + 1) * 4096].T
                    o[:, 4096:4160] = a[sg * 64:(sg + 1) * 64].T
                    return o
                mp["yab"] = cut(yab); mp["ysf"] = cut(ysf); mp["ysb"] = cut(ysb)
                mp["wg"] = f(s5_w_glu[l - 1]); mp["wo"] = f(w_out[l - 1]); mp["gbc_0"] = _gbc(mm, b, 5)
                mp["w1_1"] = f(ffn2_w_in[l - 1]); mp["w2_1"] = f(ffn2_w_out[l - 1]); mp["mod_1"] = _modT(mm, b, 7, 6); mp["gbc_1"] = _gbc(mm, b, 8)
        if l < 4:
            mm = mall[:, l]
            p0 = len(passes)
            passes += [{"kind": "ffn"}, {"kind": "inproj"}]
            for core in range(NCORE):
                b, sg = core // 4, core % 4
                mp = maps[core]
                mp["w1_%d" % p0] = f(ffn1_w_in[l]); mp["w2_%d" % p0] = f(ffn1_w_out[l]); mp["mod_%d" % p0] = _modT(mm, b, 1, 0); mp["gbc_%d" % p0] = _gbc(mm, b, 2)
                mp["wi_%d" % (p0 + 1)] = f(w_in[l]); mp["mod_%d" % (p0 + 1)] = _modT(mm, b, 4, 3)
        key = "tok_" + "_".join(p["kind"] for p in passes)
        res = run(get_prog(key, lambda: build_token_prog(passes)), maps)
        X = [res[core]["xo"] for core in range(NCORE)]
        if l == 4:
            break
        seqs = []
        for b in range(2):
            lat = np.concatenate([res[b * 4 + sg]["proj"][:4096] for sg in range(4)], 0)
            cx = np.concatenate([res[b * 4 + sg]["proj"][4096:4160] for sg in range(4)], 0)
            seqs.append(np.concatenate([cx, lat], 0))
        maps = []
        for core in range(NCORE):
            b, j = core // 4, core % 4
            mp = {}
            mp.update(prep_na(seqs[b], f(na_rpb[l]), f(na_q_gain[l]), f(na_k_gain[l]), j))
            mp.update(prep_ret(seqs[b], f(ret_decay[l]), f(ret_gn[l]), j))
            mp.update(prep_s5(seqs[b], f(s5_a_re[l]), f(s5_a_im[l]), f(s5_log_dt[l]), f(s5_b_re[l]), f(s5_b_im[l]), f(s5_c_re[l]), f(s5_c_im[l]), f(s5_d[l]), j))
            maps.append(mp)
        res = run(get_prog("mix", build_mixer_prog), maps)
        mix = []
        for b in range(2):
            yab = np.zeros((NS, 768), np.float32); ysf = np.zeros((NS, 256), np.float32); ysb = np.zeros((NS, 256), np.float32)
            for j in range(4):
                r = res[b * 4 + j]
                ca, la = unprep_na(r["yA"])
                yab[:256, 64 * j:64 * j + 64] = ca; yab[256:, 64 * j:64 * j + 64] = la
                cb, lb = unprep_ret(r["yB"])
                yab[:256, 256 + 128 * j:256 + 128 * j + 128] = cb; yab[256:, 256 + 128 * j:256 + 128 * j + 128] = lb
                sf, sb_ = unprep_s5(r["ysF"], r["ysB"])
                ysf[:, 64 * j:64 * j + 64] = sf; ysb[:, 64 * j:64 * j + 64] = sb_
            mix.append((yab, ysf, ysb))
    out = np.zeros((2, 16384, D), np.float32)
    for core in range(NCORE):
        b, sg = core // 4, core % 4
        out[b, sg * 4096:(sg + 1) * 4096] = X[core][:4096]
    return out
```
